# Optimizing a Trainium2 kernel written in Bass

```python
import math
import jax, jax.numpy as jnp
from jax import lax
import numpy as np

D_MODEL = 1024
BATCH = 8
SEQ = 4096
DEPTH = 4
DEC_BATCH = 2
DEC_SEQ = 16384
PAST_LEN = 128

N_META = 16
GRID_W = 64
EPS = 1e-6
NEG = -1e30
S5_WIDTH = 256
S5_GROUP = 16
S5_GROUPS = S5_WIDTH // S5_GROUP
S5_STATE = 64
NA_HEADS = 8
NA_HEAD_DIM = 64
NA_WIDTH = NA_HEADS * NA_HEAD_DIM
NA_ROWS = 8
NA_COLS = 16
NA_QBLOCK = 16
NA_KBLOCK = NA_QBLOCK + NA_COLS
HG_HEADS = 4
HG_DK = 64
HG_DV = 64
HG_WIDTH = HG_HEADS * HG_DK
HG_CHUNK = 64
FFN_HIDDEN = -(-8 * D_MODEL // (3 * 256)) * 256
IN_SIZES = [S5_WIDTH, NA_WIDTH, NA_WIDTH, NA_WIDTH, HG_WIDTH, HG_WIDTH, HG_WIDTH, HG_WIDTH, HG_WIDTH, D_MODEL, D_MODEL, D_MODEL]
IN_WIDTH = sum(IN_SIZES)

kernel_name = "hybrid_s5_natten_hgrn2_encoder"


def rms_norm(x, g):
    xf = x.astype(jnp.float32)
    y = xf * lax.rsqrt(jnp.mean(xf * xf, axis=-1, keepdims=True) + EPS)
    return (y * g.astype(jnp.float32)).astype(x.dtype)


def s5_scan_dir(u, a_re, a_im, log_dt, b_re, b_im, c_re, c_im, reverse):
    dt = jnp.exp(log_dt)[:, None]
    mag = jnp.exp(a_re * dt)
    lbar_re = mag * jnp.cos(a_im * dt)
    lbar_im = mag * jnp.sin(a_im * dt)
    den = a_re * a_re + a_im * a_im
    nr = lbar_re - 1.0
    ni = lbar_im
    z_re = (nr * a_re + ni * a_im) / den
    z_im = (ni * a_re - nr * a_im) / den
    bbar_re = z_re[..., None] * b_re - z_im[..., None] * b_im
    bbar_im = z_re[..., None] * b_im + z_im[..., None] * b_re
    bu_re = jnp.einsum('blgc,gpc->blgp', u, bbar_re)
    bu_im = jnp.einsum('blgc,gpc->blgp', u, bbar_im)
    lr = jnp.broadcast_to(lbar_re, bu_re.shape)
    li = jnp.broadcast_to(lbar_im, bu_re.shape)

    def combine(e1, e2):
        a1r, a1i, b1r, b1i = e1
        a2r, a2i, b2r, b2i = e2
        return (a1r * a2r - a1i * a2i, a1r * a2i + a1i * a2r,
                a2r * b1r - a2i * b1i + b2r, a2r * b1i + a2i * b1r + b2i)

    _, _, xr, xi = lax.associative_scan(combine, (lr, li, bu_re, bu_im), axis=1, reverse=reverse)
    return jnp.einsum('blgp,gcp->blgc', xr, c_re) - jnp.einsum('blgp,gcp->blgc', xi, c_im)


def s5_mixer(u, a_re, a_im, log_dt, b_re, b_im, c_re, c_im, d_skip, w_glu):
    f32 = jnp.float32
    bsz, L, _ = u.shape
    uf = u.astype(f32)
    ug = uf.reshape(bsz, L, S5_GROUPS, S5_GROUP)
    y = (s5_scan_dir(ug, a_re[0].astype(f32), a_im[0].astype(f32), log_dt[0].astype(f32), b_re[0].astype(f32),
                     b_im[0].astype(f32), c_re[0].astype(f32), c_im[0].astype(f32), False)
         + s5_scan_dir(ug, a_re[1].astype(f32), a_im[1].astype(f32), log_dt[1].astype(f32), b_re[1].astype(f32),
                       b_im[1].astype(f32), c_re[1].astype(f32), c_im[1].astype(f32), True))
    y = y.reshape(bsz, L, S5_WIDTH) + d_skip.astype(f32) * uf
    y = jax.nn.gelu(y)
    y = y * jax.nn.sigmoid(y @ w_glu.astype(f32))
    return y.astype(u.dtype)


def neighborhood_attention(q, k, v, rpb):
    bsz, L, _ = q.shape
    n_tok = L - N_META
    rows = n_tok // GRID_W
    kr = min(NA_ROWS, rows)
    scale = NA_HEAD_DIM ** -0.5
    q = q.reshape(bsz, L, NA_HEADS, NA_HEAD_DIM) * scale
    k = k.reshape(bsz, L, NA_HEADS, NA_HEAD_DIM)
    v = v.reshape(bsz, L, NA_HEADS, NA_HEAD_DIM)
    qm, km, vm = q[:, :N_META], k[:, :N_META], v[:, :N_META]
    sm = jnp.einsum('bqhd,bkhd->bhqk', qm, km).astype(jnp.float32)
    om = jnp.einsum('bhqk,bkhd->bqhd', jax.nn.softmax(sm, axis=-1).astype(v.dtype), vm)
    grid = lambda t: t[:, N_META:].reshape(bsz, rows, GRID_W, NA_HEADS, NA_HEAD_DIM)
    qg, kg, vg = grid(q), grid(k), grid(v)
    ncb = GRID_W // NA_QBLOCK
    qcols = np.arange(GRID_W).reshape(ncb, NA_QBLOCK)
    ks = np.clip(np.arange(ncb) * NA_QBLOCK - NA_COLS // 2, 0, GRID_W - NA_KBLOCK)
    kcols = ks[:, None] + np.arange(NA_KBLOCK)
    wstart = np.clip(qcols - NA_COLS // 2, 0, GRID_W - NA_COLS)
    colmask = (kcols[:, None, :] >= wstart[..., None]) & (kcols[:, None, :] < wstart[..., None] + NA_COLS)
    dcol = np.clip(kcols[:, None, :] - qcols[..., None], -(NA_COLS - 1), NA_COLS - 1) + NA_COLS - 1
    mask = np.broadcast_to(colmask[:, :, None, :], (ncb, NA_QBLOCK, kr, NA_KBLOCK)).reshape(ncb, NA_QBLOCK, kr * NA_KBLOCK)
    rpb = rpb.astype(jnp.float32)

    def row_fn(args):
        r, q_row = args
        rs = jnp.clip(r - NA_ROWS // 2, 0, rows - kr)
        k_strip = lax.dynamic_slice_in_dim(kg, rs, kr, axis=1)
        v_strip = lax.dynamic_slice_in_dim(vg, rs, kr, axis=1)
        k_blk = k_strip[:, :, kcols].transpose(0, 2, 1, 3, 4, 5).reshape(bsz, ncb, kr * NA_KBLOCK, NA_HEADS, NA_HEAD_DIM)
        v_blk = v_strip[:, :, kcols].transpose(0, 2, 1, 3, 4, 5).reshape(bsz, ncb, kr * NA_KBLOCK, NA_HEADS, NA_HEAD_DIM)
        q_blk = q_row.reshape(bsz, ncb, NA_QBLOCK, NA_HEADS, NA_HEAD_DIM)
        drow = rs + jnp.arange(kr) - r + NA_ROWS - 1
        bias = rpb[:, drow][:, :, dcol]
        bias = bias.transpose(0, 2, 3, 1, 4).reshape(NA_HEADS, ncb, NA_QBLOCK, kr * NA_KBLOCK)
        s_loc = jnp.einsum('bnqhd,bnkhd->bhnqk', q_blk, k_blk).astype(jnp.float32) + bias
        s_loc = jnp.where(mask, s_loc, NEG)
        s_meta = jnp.einsum('bnqhd,bmhd->bhnqm', q_blk, km).astype(jnp.float32)
        p = jax.nn.softmax(jnp.concatenate([s_meta, s_loc], axis=-1), axis=-1).astype(v.dtype)
        o = (jnp.einsum('bhnqm,bmhd->bnqhd', p[..., :N_META], vm)
             + jnp.einsum('bhnqk,bnkhd->bnqhd', p[..., N_META:], v_blk))
        return o.reshape(bsz, GRID_W, NA_HEADS, NA_HEAD_DIM)

    og = lax.map(row_fn, (jnp.arange(rows), qg.transpose(1, 0, 2, 3, 4)))
    og = og.transpose(1, 0, 2, 3, 4).reshape(bsz, n_tok, NA_WIDTH)
    return jnp.concatenate([om.reshape(bsz, N_META, NA_WIDTH), og], axis=1)


def chunk_recurrence(q, k, v, g):
    bsz, Lp, nh, dk = q.shape
    dv = v.shape[-1]
    nc = Lp // HG_CHUNK
    to_chunks = lambda t: t.reshape(bsz, nc, HG_CHUNK, nh, t.shape[-1]).transpose(1, 0, 3, 2, 4)
    tri = np.tril(np.ones((HG_CHUNK, HG_CHUNK), dtype=bool))[:, :, None]

    def step(S, xs):
        qc, kc, vc, gc = xs
        b = jnp.cumsum(gc, axis=2)
        o_inter = jnp.einsum('bhcd,bhde->bhce', qc * jnp.exp(b), S)
        diff = b[:, :, :, None, :] - b[:, :, None, :, :]
        decay = jnp.where(tri, jnp.exp(jnp.where(tri, diff, 0.0)), 0.0)
        att = jnp.einsum('bhid,bhjd,bhijd->bhij', qc, kc, decay)
        o_intra = jnp.einsum('bhij,bhje->bhie', att, vc)
        b_last = b[:, :, -1:, :]
        S = jnp.exp(b_last[:, :, 0, :, None]) * S + jnp.einsum('bhjd,bhje->bhde', kc * jnp.exp(b_last - b), vc)
        return S, o_inter + o_intra

    S0 = jnp.zeros((bsz, nh, dk, dv), jnp.float32)
    _, o = lax.scan(step, S0, (to_chunks(q), to_chunks(k), to_chunks(v), to_chunks(g)))
    return o.transpose(1, 0, 3, 2, 4).reshape(bsz, Lp, nh, dv)


def hgrn2_mixer(q, f_fwd, f_bwd, i, g_out, lb, onorm_g):
    f32 = jnp.float32
    bsz, L, _ = q.shape
    pad = HG_CHUNK - N_META
    heads = lambda t: jnp.pad(t.astype(f32), ((0, 0), (pad, 0), (0, 0))).reshape(bsz, L + pad, HG_HEADS, -1)
    qh = heads(jax.nn.silu(q.astype(f32)))
    vh = heads(i)
    lb = lb.astype(f32)

    def gates(f):
        ff = f.astype(f32)
        fg = lb + (1.0 - lb) * jax.nn.sigmoid(ff)
        log_f = jnp.log(fg)
        kk = (1.0 - lb) * jax.nn.sigmoid(-ff)
        return heads(kk), heads(log_f)

    kf, gf = gates(f_fwd)
    kb, gb = gates(f_bwd)
    flip = lambda t: jnp.flip(t, axis=1)
    o = chunk_recurrence(qh, kf, vh, gf) + flip(chunk_recurrence(flip(qh), flip(kb), flip(vh), flip(gb)))
    o = o[:, pad:]
    o = o * lax.rsqrt(jnp.mean(o * o, axis=-1, keepdims=True) + EPS) * onorm_g.astype(f32)
    o = o.reshape(bsz, L, HG_WIDTH) * jax.nn.silu(g_out.astype(f32))
    return o.astype(q.dtype)


def hybrid_layer(h, lb, norm1_g, w_in, s5_a_re, s5_a_im, s5_log_dt, s5_b_re, s5_b_im, s5_c_re, s5_c_im,
                 s5_d, s5_w_glu, na_rpb, hg_onorm_g, w_up_a, w_up_b, w_up_c, w_o, norm2_g,
                 w_ffn_gate, w_ffn_up, w_ffn_down):
    xn = rms_norm(h, norm1_g)
    z = xn @ w_in
    (u_a, q_b, k_b, v_b, q_c, f_cf, f_cb, i_c, g_c, gt_a, gt_b, gt_c) = jnp.split(
        z, list(np.cumsum(IN_SIZES)[:-1]), axis=-1)
    y_a = s5_mixer(u_a, s5_a_re, s5_a_im, s5_log_dt, s5_b_re, s5_b_im, s5_c_re, s5_c_im, s5_d, s5_w_glu) @ w_up_a
    y_b = neighborhood_attention(q_b, k_b, v_b, na_rpb) @ w_up_b
    y_c = hgrn2_mixer(q_c, f_cf, f_cb, i_c, g_c, lb, hg_onorm_g) @ w_up_c
    mix = jax.nn.sigmoid(gt_a) * y_a + jax.nn.sigmoid(gt_b) * y_b + jax.nn.sigmoid(gt_c) * y_c
    h = h + mix @ w_o
    hn = rms_norm(h, norm2_g)
    h = h + (jax.nn.silu(hn @ w_ffn_gate) * (hn @ w_ffn_up)) @ w_ffn_down
    return h


def trunk(x, lbs, meta_tokens, norm1_g, w_in, s5_a_re, s5_a_im, s5_log_dt, s5_b_re, s5_b_im, s5_c_re, s5_c_im,
          s5_d, s5_w_glu, na_rpb, hg_onorm_g, w_up_a, w_up_b, w_up_c, w_o, norm2_g,
          w_ffn_gate, w_ffn_up, w_ffn_down, final_norm_g):
    bsz = x.shape[0]
    meta = jnp.broadcast_to(meta_tokens[None].astype(x.dtype), (bsz, N_META, D_MODEL))
    h = jnp.concatenate([meta, x], axis=1)
    for l in range(DEPTH):
        h = hybrid_layer(h, lbs[l], norm1_g[l], w_in[l], s5_a_re[l], s5_a_im[l], s5_log_dt[l], s5_b_re[l],
                         s5_b_im[l], s5_c_re[l], s5_c_im[l], s5_d[l], s5_w_glu[l], na_rpb[l], hg_onorm_g[l],
                         w_up_a[l], w_up_b[l], w_up_c[l], w_o[l], norm2_g[l],
                         w_ffn_gate[l], w_ffn_up[l], w_ffn_down[l])
    h = rms_norm(h, final_norm_g)
    return h[:, N_META:]


def setup_inputs(seed: int = 0) -> dict:
    key = jax.random.key(seed)
    ks = jax.random.split(key, 32)
    nrm = lambda k, shape, s: jax.random.normal(k, shape, jnp.float32) * s
    G, P, C = S5_GROUPS, S5_STATE, S5_GROUP
    a_im_init = jnp.pi * jnp.arange(P, dtype=jnp.float32)
    return {
        "x_prompt": nrm(ks[0], (BATCH, SEQ, D_MODEL), 1.0),
        "x_sample": nrm(ks[1], (DEC_BATCH, DEC_SEQ, D_MODEL), 1.0),
        "meta_tokens": nrm(ks[2], (N_META, D_MODEL), 1.0),
        "norm1_g": 1.0 + nrm(ks[3], (DEPTH, D_MODEL), 0.1),
        "w_in": nrm(ks[4], (DEPTH, D_MODEL, IN_WIDTH), D_MODEL ** -0.5),
        "s5_a_re": -0.5 + nrm(ks[5], (DEPTH, 2, G, P), 0.01),
        "s5_a_im": a_im_init + nrm(ks[6], (DEPTH, 2, G, P), 0.01),
        "s5_log_dt": jax.random.uniform(ks[7], (DEPTH, 2, G), jnp.float32, math.log(1e-3), math.log(1e-1)),
        "s5_b_re": nrm(ks[8], (DEPTH, 2, G, P, C), (2 * C) ** -0.5),
        "s5_b_im": nrm(ks[9], (DEPTH, 2, G, P, C), (2 * C) ** -0.5),
        "s5_c_re": nrm(ks[10], (DEPTH, 2, G, C, P), P ** -0.5),
        "s5_c_im": nrm(ks[11], (DEPTH, 2, G, C, P), P ** -0.5),
        "s5_d": nrm(ks[12], (DEPTH, S5_WIDTH), 1.0),
        "s5_w_glu": nrm(ks[13], (DEPTH, S5_WIDTH, S5_WIDTH), S5_WIDTH ** -0.5),
        "na_rpb": nrm(ks[14], (DEPTH, NA_HEADS, 2 * NA_ROWS - 1, 2 * NA_COLS - 1), 0.1),
        "hg_lb_logits": nrm(ks[15], (DEPTH, HG_WIDTH), 1.0),
        "hg_onorm_g": 1.0 + nrm(ks[16], (DEPTH, HG_DV), 0.1),
        "w_up_a": nrm(ks[17], (DEPTH, S5_WIDTH, D_MODEL), S5_WIDTH ** -0.5),
        "w_up_b": nrm(ks[18], (DEPTH, NA_WIDTH, D_MODEL), NA_WIDTH ** -0.5),
        "w_up_c": nrm(ks[19], (DEPTH, HG_WIDTH, D_MODEL), HG_WIDTH ** -0.5),
        "w_o": nrm(ks[20], (DEPTH, D_MODEL, D_MODEL), D_MODEL ** -0.5),
        "norm2_g": 1.0 + nrm(ks[21], (DEPTH, D_MODEL), 0.1),
        "w_ffn_gate": nrm(ks[22], (DEPTH, D_MODEL, FFN_HIDDEN), D_MODEL ** -0.5),
        "w_ffn_up": nrm(ks[23], (DEPTH, D_MODEL, FFN_HIDDEN), D_MODEL ** -0.5),
        "w_ffn_down": nrm(ks[24], (DEPTH, FFN_HIDDEN, D_MODEL), FFN_HIDDEN ** -0.5),
        "final_norm_g": 1.0 + nrm(ks[25], (D_MODEL,), 0.1),
    }


def reference(x_prompt, x_sample, meta_tokens, norm1_g, w_in, s5_a_re, s5_a_im, s5_log_dt, s5_b_re, s5_b_im,
              s5_c_re, s5_c_im, s5_d, s5_w_glu, na_rpb, hg_lb_logits, hg_onorm_g, w_up_a, w_up_b, w_up_c,
              w_o, norm2_g, w_ffn_gate, w_ffn_up, w_ffn_down, final_norm_g):
    sm = jax.nn.softmax(hg_lb_logits.astype(jnp.float32), axis=0)
    lbs = jnp.cumsum(sm, axis=0) - sm[0:1]
    y_prompt = trunk(x_prompt, lbs, meta_tokens, norm1_g, w_in, s5_a_re, s5_a_im, s5_log_dt, s5_b_re, s5_b_im,
                     s5_c_re, s5_c_im, s5_d, s5_w_glu, na_rpb, hg_onorm_g, w_up_a, w_up_b, w_up_c, w_o,
                     norm2_g, w_ffn_gate, w_ffn_up, w_ffn_down, final_norm_g)
    y_sample = trunk(x_sample, lbs, meta_tokens, norm1_g, w_in, s5_a_re, s5_a_im, s5_log_dt, s5_b_re, s5_b_im,
                     s5_c_re, s5_c_im, s5_d, s5_w_glu, na_rpb, hg_onorm_g, w_up_a, w_up_b, w_up_c, w_o,
                     norm2_g, w_ffn_gate, w_ffn_up, w_ffn_down, final_norm_g)
    return (y_prompt, y_sample)
```

```python
import numpy as np
from contextlib import ExitStack
import concourse.bass as bass
import concourse.mybir as mybir
from concourse.bass_utils import run_bass_kernel_spmd

F32 = mybir.dt.float32
BF16 = mybir.dt.bfloat16
AF = mybir.ActivationFunctionType
ALU = mybir.AluOpType

D = 1024
NM = 16
EPS = 1e-6
FF = 2816
TT = 256
NEGV = -30000.0


class _Rec:
    def __getattr__(self, name):
        def f(*a, **k):
            self.call = (name, a, k)
            return self
        return f


NDS = 8


class Prog:
    def __init__(self, nc):
        self.nc = nc
        self.streams = {k: [] for k in ['sp', 'act', 'dve', 'pool', 'pe']}
        self.cnt = {k: 0 for k in ['act', 'dve', 'pool', 'pe']}
        for q in ('sp', 'pooldma'):
            for i in range(NDS):
                self.cnt[f"{q}{i}"] = 0
        self.dma_n = {'sp': 0, 'pooldma': 0}
        self.known = {k: {} for k in self.streams}
        self.lastw = {}
        self.readers = {}

    def op(self, stream, fn, reads=(), writes=(), dma=False):
        deps = {}

        def need(w):
            if w[1] > deps.get(w[0], 0):
                deps[w[0]] = w[1]
        if dma:
            q = 'sp' if stream == 'sp' else 'pooldma'
            i = self.dma_n[q]
            self.dma_n[q] += 1
            semname = f"{q}{i % NDS}"
            if self.cnt[semname] > 0:
                need((semname, self.cnt[semname]))
        else:
            assert stream != 'sp'
            semname = stream
        for k in reads:
            w = self.lastw.get(k)
            if w:
                need(w)
        for k in writes:
            w = self.lastw.get(k)
            if w:
                need(w)
            for sem, c in self.readers.get(k, {}).items():
                need((sem, c))
        waits = []
        for sem, c in deps.items():
            if sem == 'pe' and stream == 'pe':
                continue
            if self.known[stream].get(sem, 0) >= c:
                continue
            self.known[stream][sem] = c
            waits.append((sem, c))
        self.cnt[semname] += 1
        my = self.cnt[semname]
        rec = _Rec()
        fn(rec)
        self.streams[stream].append((waits, rec.call, semname))
        for k in writes:
            self.lastw[k] = (semname, my)
            self.readers[k] = {}
        for k in reads:
            self.readers.setdefault(k, {})[semname] = my

    def barrier(self):
        for st in self.streams:
            waits = []
            for sem, c in self.cnt.items():
                if c > self.known[st].get(sem, 0):
                    self.known[st][sem] = c
                    waits.append((sem, c))
            if waits:
                self.streams[st].append((waits, None, None))

    def emit(self, sems):
        mult = {k: (1 if k in ('act', 'dve', 'pool', 'pe') else 16) for k in self.cnt}
        nc = self.nc
        with nc.Block() as block:
            def run(e, items):
                for waits, fn, semname in items:
                    for sem, c in waits:
                        e.wait_ge(sems[sem], c * mult[sem])
                    if fn is not None:
                        name, a, k = fn
                        getattr(e, name)(*a, **k).then_inc(sems[semname], mult[semname])

            @block.sync
            def _(e):
                run(e, self.streams['sp'])

            @block.scalar
            def _(e):
                run(e, self.streams['act'])

            @block.vector
            def _(e):
                run(e, self.streams['dve'])

            @block.gpsimd
            def _(e):
                run(e, self.streams['pool'])

            @block.tensor
            def _(e):
                run(e, self.streams['pe'])


def tiles_of(n_tok):
    t = [(0, NM)]
    for i in range(n_tok // TT):
        t.append((NM + i * TT, TT))
    return t


def build(nP, nS, depth, mixers=('a', 'b', 'c'), hg_stop=9):
    nc = bass.Bass("TRN2", target_bir_lowering=False)
    P = Prog(nc)
    seqs = [('p', nP), ('s', nS)]

    def din(name, shape, dt=F32):
        return nc.dram_tensor(name, list(shape), dt, kind="ExternalInput").ap()

    x_in = {'p': din("x_p", [nP, D]), 's': din("x_s", [nS, D])}
    meta_in = din("meta", [NM, D])
    y_out = {'p': nc.dram_tensor("y_p", [nP, D], F32, kind="ExternalOutput").ap(),
             's': nc.dram_tensor("y_s", [nS, D], F32, kind="ExternalOutput").ap()}
    w_in = din("w_in", [depth, D, 6144])
    w_up_a = din("w_up_a", [depth, 256, D])
    w_up_b = din("w_up_b", [depth, 512, D])
    w_up_c = din("w_up_c", [depth, 256, D])
    w_o = din("w_o", [depth, D, D])
    w_fg = din("w_ffn_gate", [depth, D, FF])
    w_fu = din("w_ffn_up", [depth, D, FF])
    w_fd = din("w_ffn_down", [depth, FF, D])
    g1 = din("g1", [128, depth, 8])
    g2 = din("g2", [128, depth, 8])
    gf = din("gf", [128, 8])
    onorm = din("onorm", [128, depth])
    c_ident = din("c_ident", [128, 128])
    c_ones = din("c_ones", [128, 128])
    c_blk = din("c_blk", [128, 128])
    c_maskf = din("c_maskf", [128, 128])
    c_maskb = din("c_maskb", [128, 128])
    lbl = din("lbl", [64, 4, 4])
    c_cmask = din("c_cmask", [128, 4])
    c_iota = din("c_iota", [128, 128])
    c_negm = din("c_negm", [128, 128])
    rpbpad = din("rpbpad", [depth, 8, 16, 127])
    s5sp = din("s5sp", [128, depth, 2, 3, 8])
    s5rep = din("s5rep", [128, depth, 2, 3, 1024])
    s5B = din("s5B", [depth, 2, 2, 128, 8, 128])
    s5C = din("s5C", [depth, 2, 2, 128, 8, 128])
    s5d = din("s5d", [128, depth, 2])
    w_glu = din("w_glu", [depth, 256, 256])

    scr = {}
    for s, n in seqs:
        L = NM + n
        scr[s] = dict(
            hT=nc.dram_tensor(f"hT_{s}", [D, L], F32, kind="Internal").ap(),
            yaT=nc.dram_tensor(f"yaT_{s}", [256, L], BF16, kind="Internal").ap(),
            ybT=nc.dram_tensor(f"ybT_{s}", [512, L], BF16, kind="Internal").ap(),
            ycT=nc.dram_tensor(f"ycT_{s}", [256, L], F32, kind="Internal").ap(),
            uT=nc.dram_tensor(f"uT_{s}", [256, L], F32, kind="Internal").ap(),
            ysT=nc.dram_tensor(f"ysT_{s}", [256, L], F32, kind="Internal").ap(),
            qkT=nc.dram_tensor(f"qkT_{s}", [1024, L], BF16, kind="Internal").ap(),
            hgT=nc.dram_tensor(f"hgT_{s}", [768, L], F32, kind="Internal").ap(),
            vtok=nc.dram_tensor(f"vtok_{s}", [L, 512], BF16, kind="Internal").ap(),
            ictok=nc.dram_tensor(f"ictok_{s}", [L, 256], BF16, kind="Internal").ap(),
        )

    es = ExitStack()
    with es:
        def sb(name, shape, dt=F32):
            return es.enter_context(nc.sbuf_tensor(name, list(shape), dt))
        sems = {k: es.enter_context(nc.semaphore(k)) for k in P.cnt}
        ps_t = [es.enter_context(nc.psum_tensor(f"ps{i}", [128, 1024], F32)) for i in range(4)]
        ps_i = [0]

        def next_ps():
            i = ps_i[0]
            ps_i[0] = (i + 1) % 8
            return ps_t[i // 2][:, (i % 2) * 512:(i % 2) * 512 + 512], ('ps', i)

        def next_ps_big():
            i = (ps_i[0] + 1) // 2 * 2 % 8
            ps_i[0] = (i + 2) % 8
            return ps_t[i // 2], [('ps', i), ('ps', i + 1)]

        ident = sb("ident", [128, 128])
        ones = sb("ones", [128, 128])
        blk = sb("blk", [128, 128])
        g1s = sb("g1s", [128, depth, 8])
        g2s = sb("g2s", [128, depth, 8])
        gfs = sb("gfs", [128, 8])
        onorms = sb("onorms", [128, depth])
        epsT = sb("epsT", [128, 1])
        for dst, src, k in [(ident, c_ident, 'ident'), (ones, c_ones, 'ones'), (blk, c_blk, 'blk'),
                            (g1s, g1, 'g1s'), (g2s, g2, 'g2s'), (gfs, gf, 'gfs'), (onorms, onorm, 'onorms')]:
            P.op('sp', lambda e, d=dst, s_=src: e.dma_start(out=d[:], in_=s_), writes=[k], dma=True)
        P.op('dve', lambda e: e.memset(epsT[:], EPS), writes=['epsT'])
        maskf = sb("maskf", [128, 128])
        maskb = sb("maskb", [128, 128])
        lbe = sb("lbe", [64, 4, 4])
        lbs = sb("lbs", [64, 4, 4])
        oml = sb("oml", [64, 4, 4])
        noml = sb("noml", [64, 4, 4])
        lsum = sb("lsum", [64, 4, 1])
        onesT = sb("onesT", [128, 128])
        cmask = sb("cmask", [128, 4])
        iota1 = sb("iota1", [128, 128])
        s5ds = sb("s5ds", [128, depth, 2])
        P.op('sp', lambda e: e.dma_start(out=iota1[:], in_=c_iota), writes=['iota1'], dma=True)
        P.op('sp', lambda e: e.dma_start(out=s5ds[:], in_=s5d), writes=['s5ds'], dma=True)
        P.op('sp', lambda e: e.dma_start(out=cmask[:], in_=c_cmask), writes=['cmask'], dma=True)
        P.op('sp', lambda e: e.dma_start(out=maskf[:], in_=c_maskf), writes=['maskf'], dma=True)
        P.op('sp', lambda e: e.dma_start(out=maskb[:], in_=c_maskb), writes=['maskb'], dma=True)
        P.op('sp', lambda e: e.dma_start(out=lbe[:], in_=lbl), writes=['lbe'], dma=True)
        P.op('dve', lambda e: e.memset(onesT[:], 1.0), writes=['onesT'])
        P.op('act', lambda e: e.activation(out=lbe[:], in_=lbe[:], func=AF.Exp), reads=['lbe'], writes=['lbe'])
        P.op('dve', lambda e: e.tensor_tensor(out=lsum[:], in0=lbe[:, :, 0:1], in1=lbe[:, :, 1:2], op=ALU.add), reads=['lbe'], writes=['lsum'])
        P.op('dve', lambda e: e.tensor_tensor(out=lsum[:], in0=lsum[:], in1=lbe[:, :, 2:3], op=ALU.add), reads=['lbe', 'lsum'], writes=['lsum'])
        P.op('dve', lambda e: e.tensor_tensor(out=lsum[:], in0=lsum[:], in1=lbe[:, :, 3:4], op=ALU.add), reads=['lbe', 'lsum'], writes=['lsum'])
        P.op('dve', lambda e: e.reciprocal(out=lsum[:], in_=lsum[:]), reads=['lsum'], writes=['lsum'])
        P.op('dve', lambda e: e.memset(lbs[:, :, 0:1], 0.0), writes=['lbs'])
        P.op('dve', lambda e: e.tensor_copy(out=lbs[:, :, 1:2], in_=lbe[:, :, 1:2]), reads=['lbe'], writes=['lbs'])
        P.op('dve', lambda e: e.tensor_tensor(out=lbs[:, :, 2:3], in0=lbs[:, :, 1:2], in1=lbe[:, :, 2:3], op=ALU.add), reads=['lbe', 'lbs'], writes=['lbs'])
        P.op('dve', lambda e: e.tensor_tensor(out=lbs[:, :, 3:4], in0=lbs[:, :, 2:3], in1=lbe[:, :, 3:4], op=ALU.add), reads=['lbe', 'lbs'], writes=['lbs'])
        for t_ in range(4):
            P.op('dve', lambda e, t_=t_: e.tensor_scalar(out=lbs[:, t_, :], in0=lbs[:, t_, :], scalar1=lsum[:, t_, 0:1], scalar2=None, op0=ALU.mult),
                 reads=['lbs', 'lsum'], writes=['lbs'])
        P.op('dve', lambda e: e.tensor_scalar(out=oml[:], in0=lbs[:], scalar1=-1.0, scalar2=1.0, op0=ALU.mult, op1=ALU.add), reads=['lbs'], writes=['oml'])
        P.op('dve', lambda e: e.tensor_scalar(out=noml[:], in0=oml[:], scalar1=-1.0, scalar2=None, op0=ALU.mult), reads=['oml'], writes=['noml'])

        hbuf = sb("hbuf", [128, 8, TT])
        sqb = sb("sqb", [128, 8, TT])
        rstd = sb("rstd", [128, TT])
        xn = sb("xn", [128, 8, TT], BF16)
        stage = sb("stage", [128, 3328])

        hTv = {s: scr[s]['hT'].rearrange("(k p) l -> p k l", p=128) for s, _ in seqs}

        def load_h(s, pos, n):
            P.op('sp', lambda e: e.dma_start(out=hbuf[:, :, :n], in_=hTv[s][:, :, pos:pos + n]),
                 reads=[('hT', s, pos)], writes=['hbuf'], dma=True)

        def store_h(s, pos, n):
            P.op('sp', lambda e: e.dma_start(out=hTv[s][:, :, pos:pos + n], in_=hbuf[:, :, :n]),
                 reads=['hbuf'], writes=[('hT', s, pos)], dma=True)

        sqt = sb("sqt", [128, TT])

        def rsqrt_ps(dst, dkey, ps, pk, n):
            P.op('act', lambda e: e.activation(out=sqt[:, :n], in_=ps[:, :n], func=AF.Sqrt, bias=epsT[:, 0:1], scale=1.0),
                 reads=[pk, 'epsT'], writes=['sqt'])
            P.op('dve', lambda e: e.reciprocal(out=dst[:, :n], in_=sqt[:, :n]), reads=['sqt'], writes=[dkey])

        def rmsnorm_to_xn(n):
            P.op('act', lambda e: e.activation(out=sqb[:, :, :n], in_=hbuf[:, :, :n], func=AF.Square),
                 reads=['hbuf'], writes=['sqb'])
            ps, pk = next_ps()
            for k in range(8):
                P.op('pe', lambda e, k=k: e.matmul(ps[:, :n], lhsT=ones[:], rhs=sqb[:, k, :n],
                                                  start=(k == 0), stop=(k == 7)),
                     reads=['sqb', 'ones'], writes=[pk])
            rsqrt_ps(rstd, 'rstd', ps, pk, n)
            for k in range(8):
                P.op('dve', lambda e, k=k: e.tensor_tensor(out=xn[:, k, :n], in0=hbuf[:, k, :n],
                                                           in1=rstd[:, :n], op=ALU.mult),
                     reads=['hbuf', 'rstd'], writes=['xn'])

        cvt_i = [0]

        def load_w(dst, dkey, src, ncols, scale=None):
            c0 = 0
            while c0 < ncols:
                c1 = min(ncols, c0 + 1664)
                half = cvt_i[0] % 2
                cvt_i[0] += 1
                st = stage[:, half * 1664: half * 1664 + (c1 - c0)]
                sk = ('stage', half)
                P.op('sp', lambda e, st=st, c0=c0, c1=c1: e.dma_start(out=st, in_=src[:, c0:c1]),
                     writes=[sk], dma=True)
                if scale is None:
                    P.op('pool', lambda e, st=st, c0=c0, c1=c1: e.tensor_copy(out=dst[:, c0:c1], in_=st),
                         reads=[sk], writes=[dkey])
                else:
                    P.op('act', lambda e, st=st, c0=c0, c1=c1: e.activation(out=dst[:, c0:c1], in_=st,
                                                                           func=AF.Copy, scale=scale),
                         reads=[sk], writes=[dkey])
                c0 = c1

        with ExitStack() as ph:
            xin = ph.enter_context(nc.sbuf_tensor("xin", [128, D], F32))
            for s, n_tok in seqs:
                blocks = [(meta_in, 0, NM, 0)] + [(x_in[s], b * 128, 128, NM + b * 128) for b in range(n_tok // 128)]
                for src, r0, nr, pos in blocks:
                    P.op('sp', lambda e, src=src, r0=r0, nr=nr: e.dma_start(out=xin[:nr, :], in_=src[r0:r0 + nr, :]),
                         writes=['xin'], dma=True)
                    pb, pks = next_ps_big()
                    for k in range(8):
                        P.op('pe', lambda e, k=k, nr=nr, pb=pb: e.transpose(
                            out=pb[:, k * 128:k * 128 + nr], in_=xin[:nr, k * 128:(k + 1) * 128],
                            identity=ident[:nr, :nr]), reads=['xin', 'ident'], writes=pks)
                    P.op('dve', lambda e, nr=nr, pb=pb: e.tensor_copy(
                        out=hbuf[:, :, :nr], in_=pb.rearrange("p (k t) -> p k t", k=8)[:, :, :nr]),
                        reads=pks, writes=['hbuf'])
                    store_h(s, pos, nr)
        P.barrier()

        for l in range(depth):

            if mixers:
                with ExitStack() as ph:
                    def pb_(name, shape, dt=F32):
                        return ph.enter_context(nc.sbuf_tensor(f"{name}_{l}", list(shape), dt))
                    WA = pb_("WA", [128, 8, 2816], BF16)
                    stu = pb_("stu", [128, 2, TT])
                    stqk = pb_("stqk", [128, 8, TT], BF16)
                    sthg = pb_("sthg", [128, 6, TT])
                    stv = pb_("stv", [128, 768], BF16)
                    for k in range(8):
                        load_w(WA[:, k, :], 'WA', w_in[l, k * 128:(k + 1) * 128, 0:2816], 2816, scale=g1s[:, l, k:k + 1])
                    for s, n_tok in seqs:
                        uV = scr[s]['uT'].rearrange("(k p) l -> p k l", p=128)
                        qkV = scr[s]['qkT'].rearrange("(k p) l -> p k l", p=128)
                        hgV = scr[s]['hgT'].rearrange("(k p) l -> p k l", p=128)
                        for pos, n in tiles_of(n_tok):
                            load_h(s, pos, n)
                            rmsnorm_to_xn(n)
                            fm = [(c, 'u', c) for c in (0, 1)] + [(2 + i, 'qk', i) for i in range(8)] + [(14 + i, 'hg', i) for i in range(6)]
                            for cc, kind, idx in fm:
                                ps, pk = next_ps()
                                for k in range(8):
                                    P.op('pe', lambda e, k=k: e.matmul(ps[:, :n], lhsT=WA[:, k, cc * 128:(cc + 1) * 128], rhs=xn[:, k, :n],
                                                                      start=(k == 0), stop=(k == 7)), reads=['WA', 'xn'], writes=[pk])
                                if kind == 'u':
                                    P.op('act', lambda e: e.activation(out=stu[:, idx, :n], in_=ps[:, :n], func=AF.Copy), reads=[pk], writes=['stu'])
                                elif kind == 'qk':
                                    P.op('act', lambda e: e.activation(out=stqk[:, idx, :n], in_=ps[:, :n], func=AF.Copy,
                                                                       scale=(0.125 if idx < 4 else 1.0)), reads=[pk], writes=['stqk'])
                                else:
                                    P.op('dve', lambda e: e.tensor_copy(out=sthg[:, idx, :n], in_=ps[:, :n]), reads=[pk], writes=['sthg'])
                            P.op('sp', lambda e: e.dma_start(out=uV[:, :, pos:pos + n], in_=stu[:, :, :n]), reads=['stu'], writes=[('uT', s)], dma=True)
                            P.op('sp', lambda e: e.dma_start(out=qkV[:, :, pos:pos + n], in_=stqk[:, :, :n]), reads=['stqk'], writes=[('qkT', s)], dma=True)
                            P.op('sp', lambda e: e.dma_start(out=hgV[:, :, pos:pos + n], in_=sthg[:, :, :n]), reads=['sthg'], writes=[('hgT', s)], dma=True)
                            for tb in range((n + 127) // 128):
                                nt = min(128, n - tb * 128)
                                ps, pk = next_ps()
                                ps2, pk2 = next_ps()
                                for k in range(8):
                                    P.op('pe', lambda e, k=k: e.matmul(ps[:nt, 0:512], lhsT=xn[:, k, tb * 128:tb * 128 + nt], rhs=WA[:, k, 1280:1792],
                                                                      start=(k == 0), stop=(k == 7)), reads=['WA', 'xn'], writes=[pk])
                                for k in range(8):
                                    P.op('pe', lambda e, k=k: e.matmul(ps2[:nt, 0:256], lhsT=xn[:, k, tb * 128:tb * 128 + nt], rhs=WA[:, k, 2560:2816],
                                                                      start=(k == 0), stop=(k == 7)), reads=['WA', 'xn'], writes=[pk2])
                                P.op('act', lambda e: e.activation(out=stv[:nt, 0:512], in_=ps[:nt, 0:512], func=AF.Copy), reads=[pk], writes=['stv'])
                                P.op('dve', lambda e: e.tensor_copy(out=stv[:nt, 512:768], in_=ps2[:nt, 0:256]), reads=[pk2], writes=['stv'])
                                r0 = pos + tb * 128
                                P.op('sp', lambda e: e.dma_start(out=scr[s]['vtok'][r0:r0 + nt, :], in_=stv[:nt, 0:512]), reads=['stv'], writes=[('vtok', s)], dma=True)
                                P.op('sp', lambda e: e.dma_start(out=scr[s]['ictok'][r0:r0 + nt, :], in_=stv[:nt, 512:768]), reads=['stv'], writes=[('ictok', s)], dma=True)
                    P.barrier()
                P.barrier()
                if 'a' in mixers:
                  with ExitStack() as ph:
                    def pb_(name, shape, dt=F32):
                        return ph.enter_context(nc.sbuf_tensor(f"{name}_{l}", list(shape), dt))
                    PI = 3.14159265358979
                    MAGIC = 12582912.0

                    def sin_of(dst, src, t1, t2, keys_r, key_w):
                        P.op('dve', lambda e: e.tensor_scalar(out=t1, in0=src, scalar1=1.0 / (2 * PI), scalar2=MAGIC, op0=ALU.mult, op1=ALU.add),
                             reads=keys_r, writes=['s5t1'])
                        P.op('dve', lambda e: e.tensor_scalar(out=t1, in0=t1, scalar1=MAGIC, scalar2=2 * PI, op0=ALU.subtract, op1=ALU.mult),
                             reads=['s5t1'], writes=['s5t1'])
                        P.op('dve', lambda e: e.tensor_tensor(out=t2, in0=src, in1=t1, op=ALU.subtract), reads=keys_r + ['s5t1'], writes=['s5t2'])
                        P.op('dve', lambda e: e.tensor_scalar(out=t2, in0=t2, scalar1=-3.141592, scalar2=3.141592, op0=ALU.max, op1=ALU.min),
                             reads=['s5t2'], writes=['s5t2'])
                        P.op('act', lambda e: e.activation(out=dst, in_=t2, func=AF.Sin), reads=['s5t2'], writes=[key_w])

                    rT = pb_("s5rT", [128, 2, 8, 128])
                    sinT = pb_("s5sin", [128, 2, 8, 128])
                    cosT = pb_("s5cos", [128, 2, 8, 128])
                    BT = pb_("s5BT", [128, 2, 2, 8, 128], BF16)
                    CT = pb_("s5CT", [128, 2, 2, 8, 128], BF16)
                    WGL = pb_("s5WGL", [128, 2, 256], BF16)
                    with ExitStack() as ph2:
                        def pc_(name, shape, dt=F32):
                            return ph2.enter_context(nc.sbuf_tensor(f"{name}_{l}", list(shape), dt))
                        spt = pc_("s5spt", [128, 2, 3, 8])
                        sp2 = pc_("s5sp2", [128, 4, 8])
                        Rp = pc_("s5Rp", [128, 3, 1024])
                        T = [pc_(f"s5T{i}", [128, 1024]) for i in range(8)]
                        Bst = pc_("s5Bst", [128, 2, 8, 128])
                        t1 = T[6]
                        t2 = T[7]
                        for k in range(2):
                            load_w(WGL[:, k, :], 'WGL', w_glu[l, k * 128:(k + 1) * 128, :], 256)
                        P.op('sp', lambda e: e.dma_start(out=spt[:], in_=s5sp[:, l]), writes=['s5spt'], dma=True)
                        for d_ in range(2):
                            P.op('act', lambda e: e.activation(out=sp2[:, 0, :], in_=spt[:, d_, 2, :], func=AF.Exp), reads=['s5spt'], writes=['s5sp2'])
                            P.op('dve', lambda e: e.tensor_tensor(out=sp2[:, 1, :], in0=spt[:, d_, 0, :], in1=sp2[:, 0, :], op=ALU.mult), reads=['s5spt', 's5sp2'], writes=['s5sp2'])
                            P.op('act', lambda e: e.activation(out=sp2[:, 2, :], in_=sp2[:, 1, :], func=AF.Exp), reads=['s5sp2'], writes=['s5sp2'])
                            P.op('dve', lambda e: e.tensor_tensor(out=sp2[:, 3, :], in0=spt[:, d_, 1, :], in1=sp2[:, 0, :], op=ALU.mult), reads=['s5spt', 's5sp2'], writes=['s5sp2'])
                            for j in range(8):
                                P.op('dve', lambda e: e.tensor_scalar(out=rT[:, d_, j, :], in0=onesT[:], scalar1=sp2[:, 2, j:j + 1], scalar2=None, op0=ALU.mult),
                                     reads=['onesT', 's5sp2'], writes=['s5rT'])
                                P.op('dve', lambda e: e.tensor_scalar(out=T[0][:, j * 128:(j + 1) * 128], in0=iota1[:], scalar1=sp2[:, 3, j:j + 1], scalar2=None, op0=ALU.mult),
                                     reads=['iota1', 's5sp2'], writes=['s5T0'])
                            sin_of(sinT[:, d_].rearrange("p j n -> p (j n)"), T[0][:], t1[:], t2[:], ['s5T0'], 's5sin')
                            P.op('dve', lambda e: e.tensor_scalar(out=T[0][:], in0=T[0][:], scalar1=PI / 2, scalar2=None, op0=ALU.add), reads=['s5T0'], writes=['s5T0'])
                            sin_of(cosT[:, d_].rearrange("p j n -> p (j n)"), T[0][:], t1[:], t2[:], ['s5T0'], 's5cos')
                            P.op('sp', lambda e: e.dma_start(out=Rp[:], in_=s5rep[:, l, d_]), writes=['s5Rp'], dma=True)
                            are, aim, ldt = Rp[:, 0, :], Rp[:, 1, :], Rp[:, 2, :]
                            P.op('act', lambda e: e.activation(out=T[0][:], in_=ldt, func=AF.Exp), reads=['s5Rp'], writes=['s5T0'])
                            P.op('dve', lambda e: e.tensor_tensor(out=T[1][:], in0=are, in1=T[0][:], op=ALU.mult), reads=['s5Rp', 's5T0'], writes=['s5T1'])
                            P.op('act', lambda e: e.activation(out=T[1][:], in_=T[1][:], func=AF.Exp), reads=['s5T1'], writes=['s5T1'])
                            P.op('dve', lambda e: e.tensor_tensor(out=T[0][:], in0=aim, in1=T[0][:], op=ALU.mult), reads=['s5Rp', 's5T0'], writes=['s5T0'])
                            sin_of(T[2][:], T[0][:], t1[:], t2[:], ['s5T0'], 's5T2')
                            P.op('dve', lambda e: e.tensor_scalar(out=T[0][:], in0=T[0][:], scalar1=PI / 2, scalar2=None, op0=ALU.add), reads=['s5T0'], writes=['s5T0'])
                            sin_of(T[3][:], T[0][:], t1[:], t2[:], ['s5T0'], 's5T3')
                            P.op('dve', lambda e: e.tensor_tensor(out=T[2][:], in0=T[2][:], in1=T[1][:], op=ALU.mult), reads=['s5T2', 's5T1'], writes=['s5T2'])
                            P.op('dve', lambda e: e.tensor_tensor(out=T[3][:], in0=T[3][:], in1=T[1][:], op=ALU.mult), reads=['s5T3', 's5T1'], writes=['s5T3'])
                            P.op('dve', lambda e: e.tensor_scalar(out=T[3][:], in0=T[3][:], scalar1=-1.0, scalar2=None, op0=ALU.add), reads=['s5T3'], writes=['s5T3'])
                            P.op('dve', lambda e: e.tensor_tensor(out=T[0][:], in0=are, in1=are, op=ALU.mult), reads=['s5Rp'], writes=['s5T0'])
                            P.op('dve', lambda e: e.tensor_tensor(out=T[1][:], in0=aim, in1=aim, op=ALU.mult), reads=['s5Rp'], writes=['s5T1'])
                            P.op('dve', lambda e: e.tensor_tensor(out=T[0][:], in0=T[0][:], in1=T[1][:], op=ALU.add), reads=['s5T0', 's5T1'], writes=['s5T0'])
                            P.op('dve', lambda e: e.reciprocal(out=T[0][:], in_=T[0][:]), reads=['s5T0'], writes=['s5T0'])
                            P.op('dve', lambda e: e.tensor_tensor(out=T[1][:], in0=T[3][:], in1=are, op=ALU.mult), reads=['s5T3', 's5Rp'], writes=['s5T1'])
                            P.op('dve', lambda e: e.tensor_tensor(out=T[4][:], in0=T[2][:], in1=aim, op=ALU.mult), reads=['s5T2', 's5Rp'], writes=['s5T4'])
                            P.op('dve', lambda e: e.tensor_tensor(out=T[1][:], in0=T[1][:], in1=T[4][:], op=ALU.add), reads=['s5T1', 's5T4'], writes=['s5T1'])
                            P.op('dve', lambda e: e.tensor_tensor(out=T[1][:], in0=T[1][:], in1=T[0][:], op=ALU.mult), reads=['s5T1', 's5T0'], writes=['s5T1'])
                            P.op('dve', lambda e: e.tensor_tensor(out=T[4][:], in0=T[2][:], in1=are, op=ALU.mult), reads=['s5T2', 's5Rp'], writes=['s5T4'])
                            P.op('dve', lambda e: e.tensor_tensor(out=T[5][:], in0=T[3][:], in1=aim, op=ALU.mult), reads=['s5T3', 's5Rp'], writes=['s5T5'])
                            P.op('dve', lambda e: e.tensor_tensor(out=T[4][:], in0=T[4][:], in1=T[5][:], op=ALU.subtract), reads=['s5T4', 's5T5'], writes=['s5T4'])
                            P.op('dve', lambda e: e.tensor_tensor(out=T[4][:], in0=T[4][:], in1=T[0][:], op=ALU.mult), reads=['s5T4', 's5T0'], writes=['s5T4'])
                            zre = T[1][:].rearrange("p (j n) -> p j n", j=8)
                            zim = T[4][:].rearrange("p (j n) -> p j n", j=8)
                            u1 = T[2][:].rearrange("p (j n) -> p j n", j=8)
                            u2 = T[3][:].rearrange("p (j n) -> p j n", j=8)
                            for ri in range(2):
                                P.op('sp', lambda e: e.dma_start(out=Bst[:, ri], in_=s5B[l, d_, ri]), writes=[('s5Bst', ri)], dma=True)
                            P.op('dve', lambda e: e.tensor_tensor(out=u1, in0=zre, in1=Bst[:, 0], op=ALU.mult), reads=['s5T1', ('s5Bst', 0)], writes=['s5T2'])
                            P.op('dve', lambda e: e.tensor_tensor(out=u2, in0=zim, in1=Bst[:, 1], op=ALU.mult), reads=['s5T4', ('s5Bst', 1)], writes=['s5T3'])
                            P.op('dve', lambda e: e.tensor_tensor(out=BT[:, d_, 0], in0=u1, in1=u2, op=ALU.subtract), reads=['s5T2', 's5T3'], writes=['s5BT'])
                            P.op('dve', lambda e: e.tensor_tensor(out=u1, in0=zre, in1=Bst[:, 1], op=ALU.mult), reads=['s5T1', ('s5Bst', 1)], writes=['s5T2'])
                            P.op('dve', lambda e: e.tensor_tensor(out=u2, in0=zim, in1=Bst[:, 0], op=ALU.mult), reads=['s5T4', ('s5Bst', 0)], writes=['s5T3'])
                            P.op('dve', lambda e: e.tensor_tensor(out=BT[:, d_, 1], in0=u1, in1=u2, op=ALU.add), reads=['s5T2', 's5T3'], writes=['s5BT'])
                            for ri in range(2):
                                P.op('sp', lambda e: e.dma_start(out=Bst[:, ri], in_=s5C[l, d_, ri]), writes=[('s5Bst', ri)], dma=True)
                                P.op('act', lambda e: e.activation(out=CT[:, d_, ri], in_=Bst[:, ri], func=AF.Copy, scale=(1.0 if ri == 0 else -1.0)),
                                     reads=[('s5Bst', ri)], writes=['s5CT'])
                        P.barrier()
                    P.barrier()
                    uc = pb_("s5uc", [128, 2, 128])
                    ub = pb_("s5ub", [128, 2, 128], BF16)
                    W1 = pb_("s5W1", [128, 128])
                    W2 = pb_("s5W2", [128, 128])
                    W3 = pb_("s5W3", [128, 128])
                    W4 = pb_("s5W4", [128, 128])
                    btr = pb_("s5btr", [128, 128])
                    bti = pb_("s5bti", [128, 128])
                    wre = pb_("s5wre", [128, 128])
                    wim = pb_("s5wim", [128, 128])
                    xre = pb_("s5xre", [128, 128])
                    xim = pb_("s5xim", [128, 128])
                    xb = pb_("s5xb", [128, 2, 8, 128], BF16)
                    xin = pb_("s5xin", [128, 2, 8])
                    yf = pb_("s5yf", [128, 2, 128])
                    yt = pb_("s5yt", [128, 2, 128])
                    g32 = pb_("s5g32", [128, 2, 128])
                    gb = pb_("s5gb", [128, 2, 128], BF16)
                    yast = pb_("s5yast", [128, 2, 128], BF16)
                    for s, n_tok in seqs:
                        uV = scr[s]['uT'].rearrange("(k p) l -> p k l", p=128)
                        ysV = scr[s]['ysT'].rearrange("(k p) l -> p k l", p=128)
                        yaV = scr[s]['yaT'].rearrange("(k p) l -> p k l", p=128)
                        chunks = [(0, NM)] + [(NM + i * 128, 128) for i in range(n_tok // 128)]
                        for d_ in range(2):
                            bwd = (d_ == 1)
                            clist = chunks if not bwd else chunks[::-1]
                            P.op('dve', lambda e: e.memset(xin[:], 0.0), writes=['s5xin'])
                            for pos, n in clist:
                                P.op('sp', lambda e: e.dma_start(out=uc[:, :, :n], in_=uV[:, :, pos:pos + n]), reads=[('uT', s)], writes=['s5uc'], dma=True)
                                if bwd:
                                    P.op('pool', lambda e: e.dma_start(out=yf[:, :, :n], in_=ysV[:, :, pos:pos + n]), reads=[('ysT', s)], writes=['s5yf'], dma=True)
                                    P.op('act', lambda e: e.activation(out=ub[:, :, :n], in_=uc[:, :, n - 1::-1], func=AF.Copy), reads=['s5uc'], writes=['s5ub'])
                                else:
                                    P.op('act', lambda e: e.activation(out=ub[:, :, :n], in_=uc[:, :, :n], func=AF.Copy), reads=['s5uc'], writes=['s5ub'])
                                for j in range(8):
                                    kt = j // 4
                                    psb, pkb = next_ps()
                                    P.op('pe', lambda e: e.matmul(psb[:, 0:n], lhsT=BT[:, d_, 0, j, :], rhs=ub[:, kt, :n], start=True, stop=True),
                                         reads=['s5BT', 's5ub'], writes=[pkb])
                                    P.op('pe', lambda e: e.matmul(psb[:, 128:128 + n], lhsT=BT[:, d_, 1, j, :], rhs=ub[:, kt, :n], start=True, stop=True),
                                         reads=['s5BT', 's5ub'], writes=[pkb])
                                    cs, sn = cosT[:, d_, j, :n], sinT[:, d_, j, :n]
                                    bre, bim = psb[:, 0:n], psb[:, 128:128 + n]
                                    P.op('dve', lambda e: e.tensor_tensor(out=W1[:, :n], in0=bre, in1=cs, op=ALU.mult), reads=[pkb, 's5cos'], writes=['s5W1'])
                                    P.op('dve', lambda e: e.tensor_tensor(out=W2[:, :n], in0=bim, in1=sn, op=ALU.mult), reads=[pkb, 's5sin'], writes=['s5W2'])
                                    P.op('dve', lambda e: e.tensor_tensor(out=btr[:, :n], in0=W1[:, :n], in1=W2[:, :n], op=ALU.add), reads=['s5W1', 's5W2'], writes=['s5btr'])
                                    P.op('dve', lambda e: e.tensor_tensor(out=W3[:, :n], in0=bim, in1=cs, op=ALU.mult), reads=[pkb, 's5cos'], writes=['s5W3'])
                                    P.op('dve', lambda e: e.tensor_tensor(out=W4[:, :n], in0=bre, in1=sn, op=ALU.mult), reads=[pkb, 's5sin'], writes=['s5W4'])
                                    P.op('dve', lambda e: e.tensor_tensor(out=bti[:, :n], in0=W3[:, :n], in1=W4[:, :n], op=ALU.subtract), reads=['s5W3', 's5W4'], writes=['s5bti'])
                                    P.op('dve', lambda e: e.tensor_tensor_scan(out=wre[:, :n], data0=rT[:, d_, j, :n], data1=btr[:, :n], initial=xin[:, 0, j:j + 1],
                                                                              op0=ALU.mult, op1=ALU.add), reads=['s5rT', 's5btr', 's5xin'], writes=['s5wre'])
                                    P.op('dve', lambda e: e.tensor_tensor_scan(out=wim[:, :n], data0=rT[:, d_, j, :n], data1=bti[:, :n], initial=xin[:, 1, j:j + 1],
                                                                              op0=ALU.mult, op1=ALU.add), reads=['s5rT', 's5bti', 's5xin'], writes=['s5wim'])
                                    P.op('dve', lambda e: e.tensor_tensor(out=W1[:, :n], in0=wre[:, :n], in1=cs, op=ALU.mult), reads=['s5wre', 's5cos'], writes=['s5W1'])
                                    P.op('dve', lambda e: e.tensor_tensor(out=W2[:, :n], in0=wim[:, :n], in1=sn, op=ALU.mult), reads=['s5wim', 's5sin'], writes=['s5W2'])
                                    P.op('dve', lambda e: e.tensor_tensor(out=xre[:, :n], in0=W1[:, :n], in1=W2[:, :n], op=ALU.subtract), reads=['s5W1', 's5W2'], writes=['s5xre'])
                                    P.op('dve', lambda e: e.tensor_tensor(out=W3[:, :n], in0=wre[:, :n], in1=sn, op=ALU.mult), reads=['s5wre', 's5sin'], writes=['s5W3'])
                                    P.op('dve', lambda e: e.tensor_tensor(out=W4[:, :n], in0=wim[:, :n], in1=cs, op=ALU.mult), reads=['s5wim', 's5cos'], writes=['s5W4'])
                                    P.op('dve', lambda e: e.tensor_tensor(out=xim[:, :n], in0=W3[:, :n], in1=W4[:, :n], op=ALU.add), reads=['s5W3', 's5W4'], writes=['s5xim'])
                                    P.op('act', lambda e: e.activation(out=xb[:, 0, j, :n], in_=xre[:, :n], func=AF.Copy), reads=['s5xre'], writes=['s5xb'])
                                    P.op('act', lambda e: e.activation(out=xb[:, 1, j, :n], in_=xim[:, :n], func=AF.Copy), reads=['s5xim'], writes=['s5xb'])
                                    P.op('act', lambda e: e.activation(out=xin[:, 0, j:j + 1], in_=xre[:, n - 1:n], func=AF.Copy), reads=['s5xre'], writes=['s5xin'])
                                    P.op('act', lambda e: e.activation(out=xin[:, 1, j:j + 1], in_=xim[:, n - 1:n], func=AF.Copy), reads=['s5xim'], writes=['s5xin'])
                                psy, pky = next_ps()
                                for m in range(2):
                                    first = True
                                    for j in range(4 * m, 4 * m + 4):
                                        for ri in range(2):
                                            last = (j == 4 * m + 3 and ri == 1)
                                            P.op('pe', lambda e: e.matmul(psy[:, m * 128:m * 128 + n], lhsT=CT[:, d_, ri, j, :], rhs=xb[:, ri, j, :n],
                                                                          start=first, stop=last), reads=['s5CT', 's5xb'], writes=[pky])
                                            first = False
                                psy3 = psy[:, 0:256].rearrange("p (m t) -> p m t", m=2)
                                if not bwd:
                                    P.op('act', lambda e: e.activation(out=yt[:, :, :n], in_=psy3[:, :, :n], func=AF.Copy), reads=[pky], writes=['s5yt'])
                                    P.op('sp', lambda e: e.dma_start(out=ysV[:, :, pos:pos + n], in_=yt[:, :, :n]), reads=['s5yt'], writes=[('ysT', s)], dma=True)
                                    continue
                                P.op('dve', lambda e: e.tensor_tensor(out=yt[:, :, :n], in0=psy3[:, :, n - 1::-1], in1=yf[:, :, :n], op=ALU.add),
                                     reads=[pky, 's5yf'], writes=['s5yt'])
                                for m in range(2):
                                    P.op('dve', lambda e: e.scalar_tensor_tensor(out=yt[:, m, :n], in0=uc[:, m, :n], scalar=s5ds[:, l, m:m + 1], in1=yt[:, m, :n],
                                                                                 op0=ALU.mult, op1=ALU.add), reads=['s5uc', 's5yt', 's5ds'], writes=['s5yt'])
                                P.op('dve', lambda e: e.tensor_tensor(out=g32[:, :, :n], in0=yt[:, :, :n], in1=yt[:, :, :n], op=ALU.mult), reads=['s5yt'], writes=['s5g32'])
                                P.op('dve', lambda e: e.tensor_scalar(out=g32[:, :, :n], in0=g32[:, :, :n], scalar1=0.044715, scalar2=1.0, op0=ALU.mult, op1=ALU.add),
                                     reads=['s5g32'], writes=['s5g32'])
                                P.op('dve', lambda e: e.tensor_tensor(out=g32[:, :, :n], in0=g32[:, :, :n], in1=yt[:, :, :n], op=ALU.mult), reads=['s5g32', 's5yt'], writes=['s5g32'])
                                P.op('act', lambda e: e.activation(out=g32[:, :, :n], in_=g32[:, :, :n], func=AF.Sigmoid, scale=1.5957691216057308),
                                     reads=['s5g32'], writes=['s5g32'])
                                P.op('dve', lambda e: e.tensor_tensor(out=g32[:, :, :n], in0=g32[:, :, :n], in1=yt[:, :, :n], op=ALU.mult), reads=['s5g32', 's5yt'], writes=['s5g32'])
                                P.op('act', lambda e: e.activation(out=gb[:, :, :n], in_=g32[:, :, :n], func=AF.Copy), reads=['s5g32'], writes=['s5gb'])
                                psg, pkg = next_ps()
                                for m2 in range(2):
                                    for k2 in range(2):
                                        P.op('pe', lambda e: e.matmul(psg[:, m2 * 128:m2 * 128 + n], lhsT=WGL[:, k2, m2 * 128:(m2 + 1) * 128], rhs=gb[:, k2, :n],
                                                                      start=(k2 == 0), stop=(k2 == 1)), reads=['WGL', 's5gb'], writes=[pkg])
                                psg3 = psg[:, 0:256].rearrange("p (m t) -> p m t", m=2)
                                P.op('act', lambda e: e.activation(out=yt[:, :, :n], in_=psg3[:, :, :n], func=AF.Sigmoid), reads=[pkg], writes=['s5yt'])
                                P.op('dve', lambda e: e.tensor_tensor(out=yast[:, :, :n], in0=g32[:, :, :n], in1=yt[:, :, :n], op=ALU.mult), reads=['s5g32', 's5yt'], writes=['s5yast'])
                                P.op('sp', lambda e: e.dma_start(out=yaV[:, :, pos:pos + n], in_=yast[:, :, :n]), reads=['s5yast'], writes=[('yaT', s)], dma=True)
                    P.barrier()
                P.barrier()
                if 'b' in mixers:
                  with ExitStack() as ph:
                    def pb_(name, shape, dt=F32):
                        return ph.enter_context(nc.sbuf_tensor(f"{name}_{l}", list(shape), dt))
                    negm = pb_("nnegm", [128, 128])
                    BI = [pb_("nBI0", [128, 8, 5, 128]), pb_("nBI1", [128, 8, 5, 128])]
                    tmpb = pb_("ntmpb", [128, 8, 5, 128])
                    Kmeta = pb_("nKm", [64, 8, 16], BF16)
                    Vmeta = pb_("nVm", [16, 8, 65], BF16)
                    Qp = pb_("nQp", [64, 8, 128], BF16)
                    Kp = pb_("nKp", [64, 8, 640], BF16)
                    Vp = pb_("nVp", [128, 5, 8, 65], BF16)
                    sc = pb_("nsc", [128, 640])
                    Pt = pb_("nPt", [128, 5, 128], BF16)
                    Pm = pb_("nPm", [16, 128], BF16)
                    rec = pb_("nrec", [128, 8, 1])
                    Otok = pb_("nOtok", [128, 8, 64])
                    ybst = pb_("nybst", [128, 4, 128], BF16)
                    P.op('sp', lambda e: e.dma_start(out=negm[:], in_=c_negm), writes=['nnegm'], dma=True)
                    P.op('dve', lambda e: e.memset(Vp[:], 1.0), writes=['nVp'])
                    P.op('dve', lambda e: e.memset(Vmeta[:], 1.0), writes=['nVm'])

                    def variant_of(r, rows):
                        a_ = min(max(r - 4, 0), rows - 10)
                        ds = []
                        for j in range(10):
                            for dl in range(2):
                                kr = a_ + j
                                rq = r + dl
                                rs = min(max(rq - 4, 0), rows - 8)
                                ds.append(kr - rq + 7 if rs <= kr < rs + 8 else 15)
                        return a_, tuple(ds)

                    def build_bias(buf, bkey, var):
                        for j in range(10):
                            t, jj = j // 2, j % 2
                            for dl in range(2):
                                dd = var[j * 2 + dl]
                                src = bass.AP(tensor=rpbpad.tensor, offset=(l * 8 * 16 + dd) * 127, ap=[[1, 64], [16 * 127, 8], [1, 64]])
                                P.op('sp', lambda e: e.dma_start(out=tmpb[jj * 64:(jj + 1) * 64, :, t, dl * 64:(dl + 1) * 64], in_=src),
                                     writes=['ntmpb'], dma=True)
                        for h in range(8):
                            for t in range(5):
                                P.op('dve', lambda e: e.tensor_tensor(
                                    out=buf[:, h, t, :].rearrange("p (a q) -> p a q", a=2),
                                    in0=tmpb[:, h, t, :].rearrange("p (a q) -> p a q", a=2)[:, :, ::-1],
                                    in1=negm[:].rearrange("p (a q) -> p a q", a=2), op=ALU.add),
                                    reads=['ntmpb', 'nnegm'], writes=[bkey])

                    for s, n_tok in seqs:
                        rows = n_tok // 64
                        qkV = scr[s]['qkT'].rearrange("(a h d) l -> d a h l", a=2, h=8, d=64)
                        ybV = scr[s]['ybT'].rearrange("(k p) l -> p k l", p=128)
                        vt = scr[s]['vtok']

                        def finish(pso, pko, nq, pos):
                            pso3 = pso[:].rearrange("p (h c) -> p h c", h=8)
                            P.op('dve', lambda e: e.reciprocal(out=rec[:nq], in_=pso3[:nq, :, 64:65]), reads=pko, writes=['nrec'])
                            P.op('dve', lambda e: e.tensor_tensor(out=Otok[:nq], in0=pso3[:nq, :, 0:64], in1=rec[:nq].to_broadcast([nq, 8, 64]), op=ALU.mult),
                                 reads=pko + ['nrec'], writes=['nOtok'])
                            pst, pkt = ps_t[2][:, 0:512], ('ps', 4)
                            Of = Otok[:].rearrange("p h d -> p (h d)")
                            for k in range(4):
                                P.op('pe', lambda e: e.transpose(out=pst[:, k * 128:k * 128 + nq], in_=Of[:nq, k * 128:(k + 1) * 128], identity=ident[:nq, :nq]),
                                     reads=['nOtok', 'ident'], writes=[pkt])
                            P.op('act', lambda e: e.activation(out=ybst[:, :, :nq], in_=pst[:, :].rearrange("p (k t) -> p k t", k=4)[:, :, :nq], func=AF.Copy),
                                 reads=[pkt], writes=['nybst'])
                            P.op('sp', lambda e: e.dma_start(out=ybV[:, :, pos:pos + nq], in_=ybst[:, :, :nq]), reads=['nybst'], writes=[('ybT', s)], dma=True)

                        P.op('sp', lambda e: e.dma_start(out=Kmeta[:], in_=qkV[:, 1, :, 0:NM]), reads=[('qkT', s)], writes=['nKm'], dma=True)
                        P.op('sp', lambda e: e.dma_start(out=Vmeta[:, :, 0:64], in_=vt[0:NM, :].rearrange("t (h d) -> t h d", h=8)),
                             reads=[('vtok', s)], writes=['nVm'], dma=True)
                        P.op('sp', lambda e: e.dma_start(out=Qp[:, :, 0:NM], in_=qkV[:, 0, :, 0:NM]), reads=[('qkT', s)], writes=['nQp'], dma=True)
                        pss, pks = ps_t[0], [('ps', 0), ('ps', 1)]
                        for h in range(8):
                            P.op('pe', lambda e: e.matmul(pss[:NM, h * 16:(h + 1) * 16], lhsT=Kmeta[:, h, :], rhs=Qp[:, h, 0:NM], start=True, stop=True),
                                 reads=['nKm', 'nQp'], writes=pks)
                        P.op('act', lambda e: e.activation(out=Pm[:, :], in_=pss[:NM, 0:128], func=AF.Exp), reads=pks, writes=['nPm'])
                        pso, pko = ps_t[3], [('ps', 6), ('ps', 7)]
                        for h in range(8):
                            P.op('pe', lambda e: e.matmul(pso[:NM, h * 128:h * 128 + 65], lhsT=Pm[:, h * 16:(h + 1) * 16], rhs=Vmeta[:, h, :], start=True, stop=True),
                                 reads=['nPm', 'nVm'], writes=pko)
                        finish(pso, pko, NM, 0)
                        vars_ = [variant_of(r, rows) for r in range(0, rows, 2)]
                        cnt = {}
                        for _, v in vars_:
                            cnt[v] = cnt.get(v, 0) + 1
                        vint = max(cnt, key=cnt.get)
                        build_bias(BI[0], 'nBI0', vint)
                        cur = [None]
                        for pi, (a_, var) in enumerate(vars_):
                            r = pi * 2
                            if var == vint:
                                bi, bkey = BI[0], 'nBI0'
                            else:
                                if cur[0] != var:
                                    build_bias(BI[1], 'nBI1', var)
                                    cur[0] = var
                                bi, bkey = BI[1], 'nBI1'
                            pq = NM + r * 64
                            kp0 = NM + a_ * 64
                            P.op('sp', lambda e: e.dma_start(out=Qp[:], in_=qkV[:, 0, :, pq:pq + 128]), reads=[('qkT', s)], writes=['nQp'], dma=True)
                            P.op('sp', lambda e: e.dma_start(out=Kp[:], in_=qkV[:, 1, :, kp0:kp0 + 640]), reads=[('qkT', s)], writes=['nKp'], dma=True)
                            for t in range(5):
                                P.op('pool', lambda e: e.dma_start(out=Vp[:, t, :, 0:64], in_=vt[kp0 + t * 128:kp0 + (t + 1) * 128, :].rearrange("p (h d) -> p h d", h=8)),
                                     reads=[('vtok', s)], writes=['nVp'], dma=True)
                            pso, pko = ps_t[3], [('ps', 6), ('ps', 7)]
                            for h in range(8):
                                pss, pks = (ps_t[0], [('ps', 0), ('ps', 1)]) if h % 2 == 0 else (ps_t[1], [('ps', 2), ('ps', 3)])
                                for t in range(5):
                                    P.op('pe', lambda e: e.matmul(pss[:, t * 128:(t + 1) * 128], lhsT=Kp[:, h, t * 128:(t + 1) * 128], rhs=Qp[:, h, :], start=True, stop=True),
                                         reads=['nKp', 'nQp'], writes=pks)
                                P.op('pe', lambda e: e.matmul(pss[:NM, 640:768], lhsT=Kmeta[:, h, :], rhs=Qp[:, h, :], start=True, stop=True),
                                     reads=['nKm', 'nQp'], writes=pks)
                                P.op('dve', lambda e: e.tensor_tensor(out=sc[:], in0=pss[:, 0:640], in1=bi[:, h].rearrange("p t q -> p (t q)"), op=ALU.add),
                                     reads=pks + [bkey], writes=['nsc'])
                                P.op('act', lambda e: e.activation(out=Pt[:].rearrange("p t q -> p (t q)"), in_=sc[:], func=AF.Exp), reads=['nsc'], writes=['nPt'])
                                P.op('act', lambda e: e.activation(out=Pm[:, :], in_=pss[:NM, 640:768], func=AF.Exp), reads=pks, writes=['nPm'])
                                for t in range(5):
                                    P.op('pe', lambda e: e.matmul(pso[:, h * 128:h * 128 + 65], lhsT=Pt[:, t, :], rhs=Vp[:, t, h, :], start=(t == 0), stop=False),
                                         reads=['nPt', 'nVp'], writes=pko)
                                P.op('pe', lambda e: e.matmul(pso[:, h * 128:h * 128 + 65], lhsT=Pm[:, :], rhs=Vmeta[:, h, :], start=False, stop=True),
                                     reads=['nPm', 'nVm'], writes=pko)
                            finish(pso, pko, 128, pq)
                    P.barrier()
                P.barrier()
                if 'c' in mixers:
                  with ExitStack() as ph:
                    def pb_(name, shape, dt=F32):
                        return ph.enter_context(nc.sbuf_tensor(f"{name}_{l}", list(shape), dt))
                    qf = pb_("hq", [64, 4, 128])
                    ff = pb_("hf", [64, 4, 128])
                    gl = pb_("hgl", [64, 4, 128])
                    kk = pb_("hk", [64, 4, 128])
                    Bc = pb_("hB", [64, 4, 128])
                    Bl = pb_("hBl", [64, 4, 128])
                    Be = pb_("hBe", [64, 4, 128])
                    REF = pb_("hREF", [64, 4, 4])
                    END = pb_("hEND", [64, 4, 4])
                    DEC = pb_("hDEC", [64, 4, 4])
                    Qt = pb_("hQt", [64, 4, 128], BF16)
                    Kt = pb_("hKt", [64, 4, 128], BF16)
                    Kh = pb_("hKh", [64, 4, 128])
                    Khtok = pb_("hKhtok", [128, 4, 256], BF16)
                    Vtok = pb_("hVtok", [128, 256], BF16)
                    attm = pb_("hattm", [128, 4, 128], BF16)
                    S32 = pb_("hS32", [64, 4, 64])
                    Sbf = pb_("hSbf", [64, 4, 5, 64], BF16)
                    ob_ = pb_("hob", [64, 4, 128])
                    ob2 = pb_("hob2", [64, 4, 128])
                    for s, n_tok in seqs:
                        hgV = scr[s]['hgT'].rearrange("(k h d) l -> d k h l", k=3, h=4, d=64)
                        ycV = scr[s]['ycT'].rearrange("(h d) l -> d h l", d=64)
                        groups = [(0, NM)] + [(NM + i * 128, 128) for i in range(n_tok // 128)]
                        for di in range(2):
                            bwd = (di == 1)
                            glist = groups if not bwd else groups[1:][::-1] + groups[:1]
                            mask = maskb if bwd else maskf
                            P.op('dve', lambda e: e.memset(S32[:], 0.0), writes=['S32'])
                            P.op('dve', lambda e: e.memset(Sbf[:], 0.0), writes=[('Sbf', i) for i in range(5)])
                            for pos, n in glist:
                                csz = min(32, n)
                                nch = n // csz
                                P.op('sp', lambda e: e.dma_start(out=qf[:, :, :n], in_=hgV[:, 0, :, pos:pos + n]), reads=[('hgT', s)], writes=['hq'], dma=True)
                                P.op('sp', lambda e: e.dma_start(out=ff[:, :, :n], in_=hgV[:, (2 if bwd else 1), :, pos:pos + n]),
                                     reads=[('hgT', s)], writes=['hf'], dma=True)
                                P.op('pool', lambda e: e.dma_start(out=Vtok[:n, :], in_=scr[s]['ictok'][pos:pos + n, :]), reads=[('ictok', s)], writes=['hVtok'], dma=True)
                                if bwd:
                                    P.op('pool', lambda e: e.dma_start(out=ob2[:, :, :n], in_=ycV[:, :, pos:pos + n]), reads=[('ycT', s)], writes=['hob2'], dma=True)
                                P.op('act', lambda e: e.activation(out=ff[:, :, :n], in_=ff[:, :, :n], func=AF.Sigmoid), reads=['hf'], writes=['hf'])
                                for h in range(4):
                                    P.op('act', lambda e: e.activation(out=gl[:, h, :n], in_=ff[:, h, :n], func=AF.Ln,
                                                                       bias=lbs[:, h, l:l + 1], scale=oml[:, h, l:l + 1]),
                                         reads=['hf', 'lbs', 'oml'], writes=['hgl'])
                                    P.op('dve', lambda e: e.tensor_scalar(out=kk[:, h, :n], in0=ff[:, h, :n], scalar1=noml[:, h, l:l + 1],
                                                                          scalar2=oml[:, h, l:l + 1], op0=ALU.mult, op1=ALU.add),
                                         reads=['hf', 'oml', 'noml'], writes=['hk'])
                                    if not bwd:
                                        P.op('dve', lambda e: e.tensor_tensor_scan(out=Bc[:, h, :n], data0=onesT[0:64, :n], data1=gl[:, h, :n],
                                                                                  initial=0.0, op0=ALU.mult, op1=ALU.add),
                                             reads=['hgl', 'onesT'], writes=['hB'])
                                    else:
                                        P.op('dve', lambda e: e.tensor_tensor_scan(out=Bc[:, h, n - 1::-1] if n < 128 else Bc[:, h, ::-1],
                                                                                  data0=onesT[0:64, :n],
                                                                                  data1=gl[:, h, n - 1::-1] if n < 128 else gl[:, h, ::-1],
                                                                                  initial=0.0, op0=ALU.mult, op1=ALU.add),
                                             reads=['hgl', 'onesT'], writes=['hB'])
                                P.op('act', lambda e: e.activation(out=qf[:, :, :n], in_=qf[:, :, :n], func=AF.Silu), reads=['hq'], writes=['hq'])
                                if hg_stop < 2:
                                    continue
                                P.op('dve', lambda e: e.memset(REF[:], 0.0), writes=['hREF'])
                                if nch > 1:
                                    if not bwd:
                                        P.op('dve', lambda e: e.tensor_copy(out=REF[:, :, 1:nch], in_=Bc[:, :, csz - 1:n - 1:csz]), reads=['hB'], writes=['hREF'])
                                    else:
                                        P.op('dve', lambda e: e.tensor_copy(out=REF[:, :, 0:nch - 1], in_=Bc[:, :, csz:n:csz]), reads=['hB'], writes=['hREF'])
                                if not bwd:
                                    P.op('dve', lambda e: e.tensor_copy(out=END[:, :, 0:nch], in_=Bc[:, :, csz - 1:n:csz]), reads=['hB'], writes=['hEND'])
                                else:
                                    P.op('dve', lambda e: e.tensor_copy(out=END[:, :, 0:nch], in_=Bc[:, :, 0:n:csz]), reads=['hB'], writes=['hEND'])
                                for h in range(4):
                                    P.op('dve', lambda e: e.tensor_tensor(
                                        out=Bl[:, h, :n].rearrange("p (c j) -> p c j", j=csz), in0=Bc[:, h, :n].rearrange("p (c j) -> p c j", j=csz),
                                        in1=REF[:, h, 0:nch].unsqueeze(2).to_broadcast([64, nch, csz]), op=ALU.subtract),
                                        reads=['hB', 'hREF'], writes=['hBl'])
                                    P.op('dve', lambda e: e.tensor_tensor(
                                        out=Be[:, h, :n].rearrange("p (c j) -> p c j", j=csz), in0=Bc[:, h, :n].rearrange("p (c j) -> p c j", j=csz),
                                        in1=END[:, h, 0:nch].unsqueeze(2).to_broadcast([64, nch, csz]), op=ALU.subtract),
                                        reads=['hB', 'hEND'], writes=['hBe'])
                                P.op('dve', lambda e: e.tensor_tensor(out=DEC[:, :, 0:nch], in0=END[:, :, 0:nch], in1=REF[:, :, 0:nch], op=ALU.subtract),
                                     reads=['hEND', 'hREF'], writes=['hDEC'])
                                P.op('act', lambda e: e.activation(out=DEC[:, :, 0:nch], in_=DEC[:, :, 0:nch], func=AF.Exp), reads=['hDEC'], writes=['hDEC'])
                                P.op('act', lambda e: e.activation(out=Bc[:, :, :n], in_=Bl[:, :, :n], func=AF.Exp), reads=['hBl'], writes=['hB'])
                                P.op('act', lambda e: e.activation(out=Bl[:, :, :n], in_=Bl[:, :, :n], func=AF.Exp, scale=-1.0), reads=['hBl'], writes=['hBl'])
                                P.op('act', lambda e: e.activation(out=Be[:, :, :n], in_=Be[:, :, :n], func=AF.Exp, scale=-1.0), reads=['hBe'], writes=['hBe'])
                                P.op('dve', lambda e: e.tensor_tensor(out=Qt[:, :, :n], in0=qf[:, :, :n], in1=Bc[:, :, :n], op=ALU.mult), reads=['hq', 'hB'], writes=['hQt'])
                                P.op('dve', lambda e: e.tensor_tensor(out=Kt[:, :, :n], in0=kk[:, :, :n], in1=Bl[:, :, :n], op=ALU.mult), reads=['hk', 'hBl'], writes=['hKt'])
                                P.op('dve', lambda e: e.tensor_tensor(out=Kh[:, :, :n], in0=kk[:, :, :n], in1=Be[:, :, :n], op=ALU.mult), reads=['hk', 'hBe'], writes=['hKh'])
                                if hg_stop < 3:
                                    continue
                                pst, pkt = next_ps()
                                for h in range(4):
                                    P.op('pe', lambda e: e.transpose(out=pst[:n, h * 64:(h + 1) * 64], in_=Kh[:, h, :n], identity=ident[0:64, 0:64]),
                                         reads=['hKh', 'ident'], writes=[pkt])
                                for c in range(nch):
                                    P.op('act', lambda e: e.activation(out=Khtok[:n, c, :], in_=pst[:n, 0:256], func=AF.Copy, scale=cmask[:n, c:c + 1]),
                                         reads=[pkt, 'cmask'], writes=['hKhtok'])
                                if hg_stop < 4:
                                    continue
                                psa, pka = next_ps()
                                for h in range(4):
                                    P.op('pe', lambda e: e.matmul(psa[:n, h * 128:h * 128 + n], lhsT=Kt[:, h, :n], rhs=Qt[:, h, :n],
                                                                  start=True, stop=True), reads=['hKt', 'hQt'], writes=[pka])
                                P.op('dve', lambda e: e.tensor_tensor(out=attm[:n, :, :n], in0=psa[:n, :].rearrange("p (h i) -> p h i", h=4)[:, :, :n],
                                                                      in1=mask[:n, :n].unsqueeze(1).to_broadcast([n, 4, n]), op=ALU.mult),
                                     reads=[pka, 'maskf', 'maskb'], writes=['hattm'])
                                if hg_stop < 5:
                                    continue
                                pso, pko = next_ps()
                                psd, pkd = next_ps_big()
                                order = list(range(nch)) if not bwd else list(range(nch))[::-1]
                                for h in range(4):
                                    for c in range(nch):
                                        P.op('pe', lambda e: e.matmul(
                                            psd[0:64, (h * 4 + c) * 64:(h * 4 + c) * 64 + 64], lhsT=Khtok[:n, c, h * 64:(h + 1) * 64],
                                            rhs=Vtok[:n, h * 64:(h + 1) * 64], start=True, stop=True),
                                            reads=['hKhtok', 'hVtok'], writes=pkd)
                                for i, c in enumerate(order):
                                    for h in range(4):
                                        P.op('dve', lambda e: e.scalar_tensor_tensor(
                                            out=S32[:, h, :], in0=S32[:, h, :], scalar=DEC[:, h, c:c + 1], in1=psd[0:64, (h * 4 + c) * 64:(h * 4 + c) * 64 + 64],
                                            op0=ALU.mult, op1=ALU.add), reads=['S32', 'hDEC'] + pkd, writes=['S32'])
                                    P.op('act', lambda e: e.activation(out=Sbf[:, :, i + 1, :], in_=S32[:], func=AF.Copy), reads=['S32'], writes=[('Sbf', i + 1)])
                                if hg_stop < 6:
                                    continue
                                for h in range(4):
                                    for i, c in enumerate(order):
                                        P.op('pe', lambda e: e.matmul(pso[0:64, h * 128 + c * csz:h * 128 + (c + 1) * csz], lhsT=Vtok[:n, h * 64:(h + 1) * 64],
                                                                      rhs=attm[:n, h, c * csz:(c + 1) * csz], start=True, stop=False),
                                             reads=['hVtok', 'hattm'], writes=[pko])
                                        P.op('pe', lambda e: e.matmul(
                                            pso[0:64, h * 128 + c * csz:h * 128 + (c + 1) * csz], lhsT=Sbf[:, h, i, :],
                                            rhs=Qt[:, h, c * csz:(c + 1) * csz], start=False, stop=True),
                                            reads=[('Sbf', i), 'hQt'], writes=[pko])
                                P.op('act', lambda e: e.activation(out=Sbf[:, :, 0, :], in_=S32[:], func=AF.Copy), reads=['S32'] + [('Sbf', i) for i in range(5)],
                                     writes=[('Sbf', 0)])
                                if not bwd:
                                    P.op('dve', lambda e: e.tensor_copy(out=ob_[:, :, :n], in_=pso[0:64, :].rearrange("p (h i) -> p h i", h=4)[:, :, :n]),
                                         reads=[pko], writes=['hob'])
                                else:
                                    P.op('dve', lambda e: e.tensor_tensor(out=ob_[:, :, :n], in0=pso[0:64, :].rearrange("p (h i) -> p h i", h=4)[:, :, :n],
                                                                          in1=ob2[:, :, :n], op=ALU.add), reads=[pko, 'hob2'], writes=['hob'])
                                P.op('sp', lambda e: e.dma_start(out=ycV[:, :, pos:pos + n], in_=ob_[:, :, :n]), reads=['hob'], writes=[('ycT', s)], dma=True)
                    P.barrier()
                P.barrier()
            with ExitStack() as ph:
                def pb_(name, shape, dt=F32):
                    return ph.enter_context(nc.sbuf_tensor(f"{name}_{l}", list(shape), dt))
                WC = pb_("WC", [128, 8, 3328], BF16)
                WUA = pb_("WUA", [128, 2, D], BF16)
                WUB = pb_("WUB", [128, 4, D], BF16)
                WUC = pb_("WUC", [128, 2, D], BF16)
                WO = pb_("WO", [128, 8, D], BF16)
                ya = pb_("ya", [128, 2, TT], BF16)
                yb = pb_("yb", [128, 4, TT], BF16)
                yc = pb_("yc", [128, 2, TT])
                ycs = pb_("ycs", [128, 2, TT])
                ycn = pb_("ycn", [128, 2, TT], BF16)
                rc = pb_("rc", [128, TT])
                sg = pb_("sg", [128, 3, TT])
                m1 = pb_("m1", [128, TT])
                m2 = pb_("m2", [128, TT])
                mix = pb_("mix", [128, 8, TT], BF16)
                for k in range(8):
                    load_w(WC[:, k, :], 'WC', w_in[l, k * 128:(k + 1) * 128, 2816:6144], 3328, scale=g1s[:, l, k:k + 1])
                    load_w(WO[:, k, :], 'WO', w_o[l, k * 128:(k + 1) * 128, :], D)
                for k in range(2):
                    load_w(WUA[:, k, :], 'WUA', w_up_a[l, k * 128:(k + 1) * 128, :], D)
                    load_w(WUC[:, k, :], 'WUC', w_up_c[l, k * 128:(k + 1) * 128, :], D)
                for k in range(4):
                    load_w(WUB[:, k, :], 'WUB', w_up_b[l, k * 128:(k + 1) * 128, :], D)
                for s, n_tok in seqs:
                    yaV = scr[s]['yaT'].rearrange("(k p) l -> p k l", p=128)
                    ybV = scr[s]['ybT'].rearrange("(k p) l -> p k l", p=128)
                    ycV = scr[s]['ycT'].rearrange("(k p) l -> p k l", p=128)
                    for pos, n in tiles_of(n_tok):
                        load_h(s, pos, n)
                        if 'a' in mixers:
                            P.op('pool', lambda e, pos=pos, n=n, yaV=yaV: e.dma_start(out=ya[:, :, :n], in_=yaV[:, :, pos:pos + n]),
                                 reads=[('yaT', s)], writes=['ya'], dma=True)
                        else:
                            P.op('dve', lambda e: e.memset(ya[:], 0.0), writes=['ya'])
                        if 'b' in mixers:
                            P.op('pool', lambda e, pos=pos, n=n, ybV=ybV: e.dma_start(out=yb[:, :, :n], in_=ybV[:, :, pos:pos + n]),
                                 reads=[('ybT', s)], writes=['yb'], dma=True)
                        else:
                            P.op('dve', lambda e: e.memset(yb[:], 0.0), writes=['yb'])
                        if 'c' in mixers:
                            P.op('pool', lambda e, pos=pos, n=n, ycV=ycV: e.dma_start(out=yc[:, :, :n], in_=ycV[:, :, pos:pos + n]),
                                 reads=[('ycT', s)], writes=['yc'], dma=True)
                        else:
                            P.op('dve', lambda e: e.memset(yc[:], 1.0), writes=['yc'])
                        rmsnorm_to_xn(n)
                        P.op('act', lambda e, n=n: e.activation(out=ycs[:, :, :n], in_=yc[:, :, :n], func=AF.Square),
                             reads=['yc'], writes=['ycs'])
                        for t in range(2):
                            ps, pk = next_ps()
                            P.op('pe', lambda e, t=t, n=n, ps=ps: e.matmul(ps[:, :n], lhsT=blk[:], rhs=ycs[:, t, :n],
                                                                        start=True, stop=True),
                                 reads=['ycs', 'blk'], writes=[pk])
                            rsqrt_ps(rc, 'rc', ps, pk, n)
                            P.op('dve', lambda e, t=t, n=n: e.scalar_tensor_tensor(
                                out=yc[:, t, :n], in0=yc[:, t, :n], scalar=onorms[:, l:l + 1], in1=rc[:, :n],
                                op0=ALU.mult, op1=ALU.mult), reads=['yc', 'rc', 'onorms'], writes=['yc'])
                            ps2, pk2 = next_ps()
                            for k in range(8):
                                P.op('pe', lambda e, k=k, t=t, n=n, ps2=ps2: e.matmul(
                                    ps2[:, :n], lhsT=WC[:, k, t * 128:(t + 1) * 128], rhs=xn[:, k, :n],
                                    start=(k == 0), stop=(k == 7)), reads=['WC', 'xn'], writes=[pk2])
                            P.op('act', lambda e, n=n, ps2=ps2: e.activation(out=m1[:, :n], in_=ps2[:, :n], func=AF.Silu),
                                 reads=[pk2], writes=['m1'])
                            P.op('dve', lambda e, t=t, n=n: e.tensor_tensor(out=ycn[:, t, :n], in0=yc[:, t, :n],
                                                                           in1=m1[:, :n], op=ALU.mult),
                                 reads=['yc', 'm1'], writes=['ycn'])
                        for oc in range(8):
                            pss = []
                            for (W, src_, nk, key) in [(WUA, ya, 2, 'ya'), (WUB, yb, 4, 'yb'), (WUC, ycn, 2, 'ycn')]:
                                ps, pk = next_ps()
                                for k in range(nk):
                                    P.op('pe', lambda e, k=k, n=n, ps=ps, W=W, src_=src_, nk=nk: e.matmul(
                                        ps[:, :n], lhsT=W[:, k, oc * 128:(oc + 1) * 128], rhs=src_[:, k, :n],
                                        start=(k == 0), stop=(k == nk - 1)), reads=[key, 'WUA', 'WUB', 'WUC'], writes=[pk])
                                pss.append((ps, pk))
                            for gi in range(3):
                                ps, pk = next_ps()
                                c0 = 256 + gi * 1024 + oc * 128
                                for k in range(8):
                                    P.op('pe', lambda e, k=k, n=n, ps=ps, c0=c0: e.matmul(
                                        ps[:, :n], lhsT=WC[:, k, c0:c0 + 128], rhs=xn[:, k, :n],
                                        start=(k == 0), stop=(k == 7)), reads=['WC', 'xn'], writes=[pk])
                                P.op('act', lambda e, gi=gi, n=n, ps=ps: e.activation(out=sg[:, gi, :n], in_=ps[:, :n],
                                                                                  func=AF.Sigmoid),
                                     reads=[pk], writes=[('sg', gi)])
                            P.op('dve', lambda e, n=n, p0=pss[0][0]: e.tensor_tensor(out=m1[:, :n], in0=p0[:, :n], in1=sg[:, 0, :n], op=ALU.mult),
                                 reads=[pss[0][1], ('sg', 0)], writes=['m1'])
                            P.op('dve', lambda e, n=n, p1=pss[1][0]: e.tensor_tensor(out=m2[:, :n], in0=p1[:, :n], in1=sg[:, 1, :n], op=ALU.mult),
                                 reads=[pss[1][1], ('sg', 1)], writes=['m2'])
                            P.op('dve', lambda e, n=n: e.tensor_tensor(out=m1[:, :n], in0=m1[:, :n], in1=m2[:, :n], op=ALU.add),
                                 reads=['m1', 'm2'], writes=['m1'])
                            P.op('dve', lambda e, n=n, p2=pss[2][0]: e.tensor_tensor(out=m2[:, :n], in0=p2[:, :n], in1=sg[:, 2, :n], op=ALU.mult),
                                 reads=[pss[2][1], ('sg', 2)], writes=['m2'])
                            P.op('dve', lambda e, n=n, oc=oc: e.tensor_tensor(out=mix[:, oc, :n], in0=m1[:, :n], in1=m2[:, :n], op=ALU.add),
                                 reads=['m1', 'm2'], writes=['mix'])
                        for oc in range(8):
                            ps, pk = next_ps()
                            for k in range(8):
                                P.op('pe', lambda e, k=k, n=n, ps=ps, oc=oc: e.matmul(
                                    ps[:, :n], lhsT=WO[:, k, oc * 128:(oc + 1) * 128], rhs=mix[:, k, :n],
                                    start=(k == 0), stop=(k == 7)), reads=['WO', 'mix'], writes=[pk])
                            P.op('dve', lambda e, n=n, ps=ps, oc=oc: e.tensor_tensor(
                                out=hbuf[:, oc, :n], in0=hbuf[:, oc, :n], in1=ps[:, :n], op=ALU.add),
                                reads=[pk, 'hbuf'], writes=['hbuf'])
                        store_h(s, pos, n)
                P.barrier()
            P.barrier()
            with ExitStack() as ph:
                def pb_(name, shape, dt=F32):
                    return ph.enter_context(nc.sbuf_tensor(f"{name}_{l}", list(shape), dt))
                WG = pb_("WG", [128, 8, FF], BF16)
                WU = pb_("WU", [128, 8, FF], BF16)
                WD = pb_("WD", [128, 22, D], BF16)
                act = pb_("act", [128, 22, TT], BF16)
                sgt = pb_("sgt", [128, TT])
                for k in range(8):
                    load_w(WG[:, k, :], 'WG', w_fg[l, k * 128:(k + 1) * 128, :], FF, scale=g2s[:, l, k:k + 1])
                    load_w(WU[:, k, :], 'WU', w_fu[l, k * 128:(k + 1) * 128, :], FF, scale=g2s[:, l, k:k + 1])
                for k in range(22):
                    load_w(WD[:, k, :], 'WD', w_fd[l, k * 128:(k + 1) * 128, :], D)
                for s, n_tok in seqs:
                    for pos, n in tiles_of(n_tok):
                        load_h(s, pos, n)
                        rmsnorm_to_xn(n)
                        for fc in range(22):
                            psg, pkg = next_ps()
                            psu, pku = next_ps()
                            for (W, ps, pk, key) in [(WG, psg, pkg, 'WG'), (WU, psu, pku, 'WU')]:
                                for k in range(8):
                                    P.op('pe', lambda e, k=k, n=n, ps=ps, W=W, fc=fc: e.matmul(
                                        ps[:, :n], lhsT=W[:, k, fc * 128:(fc + 1) * 128], rhs=xn[:, k, :n],
                                        start=(k == 0), stop=(k == 7)), reads=[key, 'xn'], writes=[pk])
                            P.op('act', lambda e, n=n, psg=psg: e.activation(out=sgt[:, :n], in_=psg[:, :n], func=AF.Silu),
                                 reads=[pkg], writes=['sgt'])
                            P.op('dve', lambda e, n=n, psu=psu, fc=fc: e.tensor_tensor(
                                out=act[:, fc, :n], in0=psu[:, :n], in1=sgt[:, :n], op=ALU.mult),
                                reads=[pku, 'sgt'], writes=['act'])
                        for oc in range(8):
                            ps, pk = next_ps()
                            for k in range(22):
                                P.op('pe', lambda e, k=k, n=n, ps=ps, oc=oc: e.matmul(
                                    ps[:, :n], lhsT=WD[:, k, oc * 128:(oc + 1) * 128], rhs=act[:, k, :n],
                                    start=(k == 0), stop=(k == 21)), reads=['WD', 'act'], writes=[pk])
                            P.op('dve', lambda e, n=n, ps=ps, oc=oc: e.tensor_tensor(
                                out=hbuf[:, oc, :n], in0=hbuf[:, oc, :n], in1=ps[:, :n], op=ALU.add),
                                reads=[pk, 'hbuf'], writes=['hbuf'])
                        store_h(s, pos, n)
                P.barrier()
            P.barrier()

        with ExitStack() as ph:
            xo = ph.enter_context(nc.sbuf_tensor("xo", [128, 8, TT], F32))
            ob = ph.enter_context(nc.sbuf_tensor("ob", [128, D], F32))
            for s, n_tok in seqs:
                for pos, n in tiles_of(n_tok)[1:]:
                    load_h(s, pos, n)
                    P.op('act', lambda e, n=n: e.activation(out=sqb[:, :, :n], in_=hbuf[:, :, :n], func=AF.Square),
                         reads=['hbuf'], writes=['sqb'])
                    ps, pk = next_ps()
                    for k in range(8):
                        P.op('pe', lambda e, k=k, n=n, ps=ps: e.matmul(ps[:, :n], lhsT=ones[:], rhs=sqb[:, k, :n],
                                                                    start=(k == 0), stop=(k == 7)),
                             reads=['sqb', 'ones'], writes=[pk])
                    rsqrt_ps(rstd, 'rstd', ps, pk, n)
                    for k in range(8):
                        P.op('dve', lambda e, k=k, n=n: e.scalar_tensor_tensor(
                            out=xo[:, k, :n], in0=hbuf[:, k, :n], scalar=gfs[:, k:k + 1], in1=rstd[:, :n],
                            op0=ALU.mult, op1=ALU.mult), reads=['hbuf', 'rstd', 'gfs'], writes=['xo'])
                    for tb in range(n // 128):
                        pb, pks = next_ps_big()
                        for k in range(8):
                            P.op('pe', lambda e, k=k, tb=tb, pb=pb: e.transpose(
                                out=pb[:, k * 128:(k + 1) * 128], in_=xo[:, k, tb * 128:(tb + 1) * 128],
                                identity=ident[:]), reads=['xo', 'ident'], writes=pks)
                        P.op('act', lambda e, pb=pb: e.activation(out=ob[:], in_=pb[:], func=AF.Copy),
                             reads=pks, writes=['ob'])
                        r0 = pos - NM + tb * 128
                        P.op('sp', lambda e, r0=r0, s=s: e.dma_start(out=y_out[s][r0:r0 + 128, :], in_=ob[:]),
                             reads=['ob'], writes=[('y', s, r0)], dma=True)
        P.barrier()
        P.emit(sems)
    return nc


def host_inputs(inputs, depth, core):
    f = lambda a: np.ascontiguousarray(np.asarray(a, dtype=np.float32))
    d = {}
    d["x_p"] = f(inputs["x_prompt"][core])
    d["x_s"] = f(inputs["x_sample"][core // 4])
    d["meta"] = f(inputs["meta_tokens"])
    for k in ["w_in", "w_up_a", "w_up_b", "w_up_c", "w_o", "w_ffn_gate", "w_ffn_up", "w_ffn_down"]:
        d[k] = f(inputs[k][:depth])
    d["g1"] = f(np.asarray(inputs["norm1_g"])[:depth].reshape(depth, 8, 128).transpose(2, 0, 1))
    d["g2"] = f(np.asarray(inputs["norm2_g"])[:depth].reshape(depth, 8, 128).transpose(2, 0, 1))
    d["gf"] = f(np.asarray(inputs["final_norm_g"]).reshape(8, 128).T)
    d["onorm"] = f(np.tile(np.asarray(inputs["hg_onorm_g"])[:depth], (1, 2)).T)
    d["c_ident"] = np.eye(128, dtype=np.float32)
    d["c_ones"] = np.full((128, 128), 1.0 / 1024, np.float32)
    b = np.zeros((128, 128), np.float32)
    b[:64, :64] = 1.0 / 64
    b[64:, 64:] = 1.0 / 64
    d["c_blk"] = b
    jj, ii = np.meshgrid(np.arange(128), np.arange(128), indexing='ij')
    same = (jj // 32) == (ii // 32)
    d["c_maskf"] = (same & (jj <= ii)).astype(np.float32)
    d["c_maskb"] = (same & (jj >= ii)).astype(np.float32)
    d["c_cmask"] = (np.arange(128)[:, None] // 32 == np.arange(4)[None, :]).astype(np.float32)
    qc = np.arange(64); kc = np.arange(64)
    ws = np.clip(qc - 8, 0, 48)
    cm = (kc[:, None] >= ws[None, :]) & (kc[:, None] < ws[None, :] + 16)
    d["c_negm"] = np.tile(np.where(cm, 0.0, NEGV).astype(np.float32), (2, 2))
    rp = np.zeros((depth, 8, 16, 127), np.float32)
    rp[:, :, :15, 48:79] = np.asarray(inputs["na_rpb"])[:depth]
    rp[:, :, 15, :] = NEGV
    d["rpbpad"] = rp
    d["c_iota"] = np.tile(np.arange(1, 129, dtype=np.float32)[None, :], (128, 1))
    ar = np.asarray(inputs["s5_a_re"])[:depth]; ai = np.asarray(inputs["s5_a_im"])[:depth]
    ld = np.repeat(np.asarray(inputs["s5_log_dt"])[:depth][..., None], 64, axis=-1)
    flat = np.stack([ar, ai, ld], axis=2).reshape(depth, 2, 3, 1024)
    d["s5sp"] = f(flat.reshape(depth, 2, 3, 8, 128).transpose(4, 0, 1, 2, 3))
    d["s5rep"] = f(np.broadcast_to(flat[None], (128, depth, 2, 3, 1024)))
    Bb = np.zeros((depth, 2, 2, 128, 8, 128), np.float32)
    Cb = np.zeros((depth, 2, 2, 128, 8, 128), np.float32)
    for ri, (bsrc, csrc) in enumerate([(inputs["s5_b_re"], inputs["s5_c_re"]), (inputs["s5_b_im"], inputs["s5_c_im"])]):
        bsrc = np.asarray(bsrc)[:depth]; csrc = np.asarray(csrc)[:depth]
        for g in range(16):
            j = g // 2
            r0 = 32 * (j % 4) + 16 * (g % 2)
            s0 = 64 * (g % 2)
            c0 = 16 * (g % 8)
            Bb[:, :, ri, r0:r0 + 16, j, s0:s0 + 64] = bsrc[:, :, g].transpose(0, 1, 3, 2)
            Cb[:, :, ri, s0:s0 + 64, j, c0:c0 + 16] = csrc[:, :, g].transpose(0, 1, 3, 2)
    d["s5B"] = Bb
    d["s5C"] = Cb
    d["s5d"] = f(np.asarray(inputs["s5_d"])[:depth].reshape(depth, 2, 128).transpose(2, 0, 1))
    d["w_glu"] = f(inputs["s5_w_glu"][:depth])
    d["lbl"] = f(np.asarray(inputs["hg_lb_logits"]).T.reshape(4, 64, 4).transpose(1, 0, 2))
    return d


def run(inputs, nP, nS, depth, n_cores=8, **kw):
    nc = build(nP, nS, depth, **kw)
    in_maps = [host_inputs(inputs, depth, c) for c in range(n_cores)]
    res = run_bass_kernel_spmd(nc, in_maps, core_ids=list(range(n_cores)))
    yp = np.stack([res.results[c]["y_p"] for c in range(n_cores)], 0)
    ys = np.stack([res.results[c]["y_s"] for c in range(0, n_cores, 4)], 0)
    return yp.astype(np.float32), ys.astype(np.float32)


def kernel(**inputs):
    return run(inputs, 4096, 16384, 4)
```

```python
import numpy as np
from contextlib import ExitStack
import concourse.bass as bass
import concourse.mybir as mybir
from concourse.bass_utils import run_bass_kernel_spmd

F32 = mybir.dt.float32
BF16 = mybir.dt.bfloat16
AF = mybir.ActivationFunctionType
ALU = mybir.AluOpType

D = 1024
NM = 16
EPS = 1e-6
FF = 2816
TT = 256
TS = 256
NEGV = -30000.0


class _Rec:
    def __getattr__(self, name):
        def f(*a, **k):
            self.call = (name, a, k)
            return self
        return f


NDS = 8


class Prog:
    def __init__(self, nc):
        self.nc = nc
        self.streams = {k: [] for k in ['sp', 'act', 'dve', 'pool', 'pe']}
        self.cnt = {k: 0 for k in ['act', 'dve', 'pool', 'pe']}
        for q in ('sp', 'pooldma'):
            for i in range(NDS):
                self.cnt[f"{q}{i}"] = 0
        self.dma_n = {'sp': 0, 'pooldma': 0}
        self.known = {k: {} for k in self.streams}
        self.lastw = {}
        self.readers = {}

    def op(self, stream, fn, reads=(), writes=(), dma=False):
        deps = {}

        def need(w):
            if w[1] > deps.get(w[0], 0):
                deps[w[0]] = w[1]
        if dma:
            q = 'sp' if stream == 'sp' else 'pooldma'
            i = self.dma_n[q]
            self.dma_n[q] += 1
            semname = f"{q}{i % NDS}"
            if self.cnt[semname] > 0:
                need((semname, self.cnt[semname]))
        else:
            assert stream != 'sp'
            semname = stream
        for k in reads:
            w = self.lastw.get(k)
            if w:
                need(w)
        for k in writes:
            w = self.lastw.get(k)
            if w:
                need(w)
            for sem, c in self.readers.get(k, {}).items():
                need((sem, c))
        waits = []
        for sem, c in deps.items():
            if sem == 'pe' and stream == 'pe':
                continue
            if self.known[stream].get(sem, 0) >= c:
                continue
            self.known[stream][sem] = c
            waits.append((sem, c))
        self.cnt[semname] += 1
        my = self.cnt[semname]
        rec = _Rec()
        fn(rec)
        self.streams[stream].append((waits, rec.call, semname))
        for k in writes:
            self.lastw[k] = (semname, my)
            self.readers[k] = {}
        for k in reads:
            self.readers.setdefault(k, {})[semname] = my

    def barrier(self):
        for st in self.streams:
            waits = []
            for sem, c in self.cnt.items():
                if c > self.known[st].get(sem, 0):
                    self.known[st][sem] = c
                    waits.append((sem, c))
            if waits:
                self.streams[st].append((waits, None, None))

    def emit(self, sems):
        mult = {k: (1 if k in ('act', 'dve', 'pool', 'pe') else 16) for k in self.cnt}
        nc = self.nc
        with nc.Block() as block:
            def run(e, items):
                for waits, fn, semname in items:
                    for sem, c in waits:
                        e.wait_ge(sems[sem], c * mult[sem])
                    if fn is not None:
                        name, a, k = fn
                        getattr(e, name)(*a, **k).then_inc(sems[semname], mult[semname])

            @block.sync
            def _(e):
                run(e, self.streams['sp'])

            @block.scalar
            def _(e):
                run(e, self.streams['act'])

            @block.vector
            def _(e):
                run(e, self.streams['dve'])

            @block.gpsimd
            def _(e):
                run(e, self.streams['pool'])

            @block.tensor
            def _(e):
                run(e, self.streams['pe'])


def tiles_of(n_tok):
    t = [(0, NM)]
    for i in range(n_tok // TT):
        t.append((NM + i * TT, TT))
    return t


def build(nP, nS, depth, mixers=('a', 'b', 'c'), hg_stop=9):
    nc = bass.Bass("TRN2", target_bir_lowering=False)
    P = Prog(nc)
    seqs = [('p', nP), ('s', nS)]

    def din(name, shape, dt=F32):
        return nc.dram_tensor(name, list(shape), dt, kind="ExternalInput").ap()

    x_in = {'p': din("x_p", [nP, D]), 's': din("x_s", [nS, D])}
    meta_in = din("meta", [NM, D])
    y_out = {'p': nc.dram_tensor("y_p", [nP, D], F32, kind="ExternalOutput").ap(),
             's': nc.dram_tensor("y_s", [nS, D], F32, kind="ExternalOutput").ap()}
    w_in = din("w_in", [depth, D, 6144])
    w_up_a = din("w_up_a", [depth, 256, D])
    w_up_b = din("w_up_b", [depth, 512, D])
    w_up_c = din("w_up_c", [depth, 256, D])
    w_o = din("w_o", [depth, D, D])
    w_fg = din("w_ffn_gate", [depth, D, FF])
    w_fu = din("w_ffn_up", [depth, D, FF])
    w_fd = din("w_ffn_down", [depth, FF, D])
    g1 = din("g1", [128, depth, 8])
    g2 = din("g2", [128, depth, 8])
    gf = din("gf", [128, 8])
    onorm = din("onorm", [128, depth])
    c_ident = din("c_ident", [128, 128])
    c_ones = din("c_ones", [128, 128])
    c_blk = din("c_blk", [128, 128])
    c_maskf = din("c_maskf", [128, 128])
    c_maskb = din("c_maskb", [128, 128])
    lbl = din("lbl", [64, 4, 4])
    c_cmask = din("c_cmask", [128, 4])
    c_iota = din("c_iota", [128, TS])
    c_negm = din("c_negm", [128, 128])
    rpbpad = din("rpbpad", [depth, 8, 16, 127])
    s5sp = din("s5sp", [128, depth, 2, 3, 8])
    s5rep = din("s5rep", [128, depth, 2, 3, 1024])
    s5B = din("s5B", [depth, 2, 2, 128, 8, 128])
    s5C = din("s5C", [depth, 2, 2, 128, 8, 128])
    s5d = din("s5d", [128, depth, 2])
    w_glu = din("w_glu", [depth, 256, 256])

    scr = {}
    for s, n in seqs:
        L = NM + n
        scr[s] = dict(
            hT=nc.dram_tensor(f"hT_{s}", [D, L], F32, kind="Internal").ap(),
            yaT=nc.dram_tensor(f"yaT_{s}", [256, L], BF16, kind="Internal").ap(),
            ybT=nc.dram_tensor(f"ybT_{s}", [512, L], BF16, kind="Internal").ap(),
            ycT=nc.dram_tensor(f"ycT_{s}", [256, L], F32, kind="Internal").ap(),
            uT=nc.dram_tensor(f"uT_{s}", [256, L], F32, kind="Internal").ap(),
            ysT=nc.dram_tensor(f"ysT_{s}", [256, L], F32, kind="Internal").ap(),
            qkT=nc.dram_tensor(f"qkT_{s}", [1024, L], BF16, kind="Internal").ap(),
            hgT=nc.dram_tensor(f"hgT_{s}", [768, L], F32, kind="Internal").ap(),
            vtok=nc.dram_tensor(f"vtok_{s}", [L, 512], BF16, kind="Internal").ap(),
            ictok=nc.dram_tensor(f"ictok_{s}", [L, 256], BF16, kind="Internal").ap(),
        )

    es = ExitStack()
    with es:
        def sb(name, shape, dt=F32):
            return es.enter_context(nc.sbuf_tensor(name, list(shape), dt))
        sems = {k: es.enter_context(nc.semaphore(k)) for k in P.cnt}
        ps_t = [es.enter_context(nc.psum_tensor(f"ps{i}", [128, 1024], F32)) for i in range(4)]
        ps_i = [0]

        def next_ps():
            i = ps_i[0]
            ps_i[0] = (i + 1) % 8
            return ps_t[i // 2][:, (i % 2) * 512:(i % 2) * 512 + 512], ('ps', i)

        def next_ps_big():
            i = (ps_i[0] + 1) // 2 * 2 % 8
            ps_i[0] = (i + 2) % 8
            return ps_t[i // 2], [('ps', i), ('ps', i + 1)]

        ident = sb("ident", [128, 128])
        ones = sb("ones", [128, 128])
        blk = sb("blk", [128, 128])
        g1s = sb("g1s", [128, depth, 8])
        g2s = sb("g2s", [128, depth, 8])
        gfs = sb("gfs", [128, 8])
        onorms = sb("onorms", [128, depth])
        epsT = sb("epsT", [128, 1])
        for dst, src, k in [(ident, c_ident, 'ident'), (ones, c_ones, 'ones'), (blk, c_blk, 'blk'),
                            (g1s, g1, 'g1s'), (g2s, g2, 'g2s'), (gfs, gf, 'gfs'), (onorms, onorm, 'onorms')]:
            P.op('sp', lambda e, d=dst, s_=src: e.dma_start(out=d[:], in_=s_), writes=[k], dma=True)
        P.op('dve', lambda e: e.memset(epsT[:], EPS), writes=['epsT'])
        maskf = sb("maskf", [128, 128])
        maskb = sb("maskb", [128, 128])
        lbe = sb("lbe", [64, 4, 4])
        lbs = sb("lbs", [64, 4, 4])
        oml = sb("oml", [64, 4, 4])
        noml = sb("noml", [64, 4, 4])
        lsum = sb("lsum", [64, 4, 1])
        onesT = sb("onesT", [128, TS])
        cmask = sb("cmask", [128, 4])
        iota1 = sb("iota1", [128, TS])
        s5ds = sb("s5ds", [128, depth, 2])
        P.op('sp', lambda e: e.dma_start(out=iota1[:], in_=c_iota), writes=['iota1'], dma=True)
        P.op('sp', lambda e: e.dma_start(out=s5ds[:], in_=s5d), writes=['s5ds'], dma=True)
        P.op('sp', lambda e: e.dma_start(out=cmask[:], in_=c_cmask), writes=['cmask'], dma=True)
        P.op('sp', lambda e: e.dma_start(out=maskf[:], in_=c_maskf), writes=['maskf'], dma=True)
        P.op('sp', lambda e: e.dma_start(out=maskb[:], in_=c_maskb), writes=['maskb'], dma=True)
        P.op('sp', lambda e: e.dma_start(out=lbe[:], in_=lbl), writes=['lbe'], dma=True)
        P.op('dve', lambda e: e.memset(onesT[:], 1.0), writes=['onesT'])
        P.op('act', lambda e: e.activation(out=lbe[:], in_=lbe[:], func=AF.Exp), reads=['lbe'], writes=['lbe'])
        P.op('dve', lambda e: e.tensor_tensor(out=lsum[:], in0=lbe[:, :, 0:1], in1=lbe[:, :, 1:2], op=ALU.add), reads=['lbe'], writes=['lsum'])
        P.op('dve', lambda e: e.tensor_tensor(out=lsum[:], in0=lsum[:], in1=lbe[:, :, 2:3], op=ALU.add), reads=['lbe', 'lsum'], writes=['lsum'])
        P.op('dve', lambda e: e.tensor_tensor(out=lsum[:], in0=lsum[:], in1=lbe[:, :, 3:4], op=ALU.add), reads=['lbe', 'lsum'], writes=['lsum'])
        P.op('dve', lambda e: e.reciprocal(out=lsum[:], in_=lsum[:]), reads=['lsum'], writes=['lsum'])
        P.op('dve', lambda e: e.memset(lbs[:, :, 0:1], 0.0), writes=['lbs'])
        P.op('dve', lambda e: e.tensor_copy(out=lbs[:, :, 1:2], in_=lbe[:, :, 1:2]), reads=['lbe'], writes=['lbs'])
        P.op('dve', lambda e: e.tensor_tensor(out=lbs[:, :, 2:3], in0=lbs[:, :, 1:2], in1=lbe[:, :, 2:3], op=ALU.add), reads=['lbe', 'lbs'], writes=['lbs'])
        P.op('dve', lambda e: e.tensor_tensor(out=lbs[:, :, 3:4], in0=lbs[:, :, 2:3], in1=lbe[:, :, 3:4], op=ALU.add), reads=['lbe', 'lbs'], writes=['lbs'])
        for t_ in range(4):
            P.op('dve', lambda e, t_=t_: e.tensor_scalar(out=lbs[:, t_, :], in0=lbs[:, t_, :], scalar1=lsum[:, t_, 0:1], scalar2=None, op0=ALU.mult),
                 reads=['lbs', 'lsum'], writes=['lbs'])
        P.op('dve', lambda e: e.tensor_scalar(out=oml[:], in0=lbs[:], scalar1=-1.0, scalar2=1.0, op0=ALU.mult, op1=ALU.add), reads=['lbs'], writes=['oml'])
        P.op('dve', lambda e: e.tensor_scalar(out=noml[:], in0=oml[:], scalar1=-1.0, scalar2=None, op0=ALU.mult), reads=['oml'], writes=['noml'])

        hbuf = sb("hbuf", [128, 8, TT])
        sqb = sb("sqb", [128, 8, TT])
        rstd = sb("rstd", [128, TT])
        xn = sb("xn", [128, 8, TT], BF16)
        stage = sb("stage", [128, 3328])

        hTv = {s: scr[s]['hT'].rearrange("(k p) l -> p k l", p=128) for s, _ in seqs}

        def load_h(s, pos, n):
            P.op('sp', lambda e: e.dma_start(out=hbuf[:, :, :n], in_=hTv[s][:, :, pos:pos + n]),
                 reads=[('hT', s, pos)], writes=['hbuf'], dma=True)

        def store_h(s, pos, n):
            P.op('sp', lambda e: e.dma_start(out=hTv[s][:, :, pos:pos + n], in_=hbuf[:, :, :n]),
                 reads=['hbuf'], writes=[('hT', s, pos)], dma=True)

        sqt = sb("sqt", [128, TT])

        def rsqrt_ps(dst, dkey, ps, pk, n):
            P.op('act', lambda e: e.activation(out=sqt[:, :n], in_=ps[:, :n], func=AF.Sqrt, bias=epsT[:, 0:1], scale=1.0),
                 reads=[pk, 'epsT'], writes=['sqt'])
            P.op('dve', lambda e: e.reciprocal(out=dst[:, :n], in_=sqt[:, :n]), reads=['sqt'], writes=[dkey])

        def rmsnorm_to_xn(n):
            P.op('act', lambda e: e.activation(out=sqb[:, :, :n], in_=hbuf[:, :, :n], func=AF.Square),
                 reads=['hbuf'], writes=['sqb'])
            ps, pk = next_ps()
            for k in range(8):
                P.op('pe', lambda e, k=k: e.matmul(ps[:, :n], lhsT=ones[:], rhs=sqb[:, k, :n],
                                                  start=(k == 0), stop=(k == 7)),
                     reads=['sqb', 'ones'], writes=[pk])
            rsqrt_ps(rstd, 'rstd', ps, pk, n)
            for k in range(8):
                P.op('dve', lambda e, k=k: e.tensor_tensor(out=xn[:, k, :n], in0=hbuf[:, k, :n],
                                                           in1=rstd[:, :n], op=ALU.mult),
                     reads=['hbuf', 'rstd'], writes=['xn'])

        cvt_i = [0]

        def load_w(dst, dkey, src, ncols, scale=None):
            c0 = 0
            while c0 < ncols:
                c1 = min(ncols, c0 + 1664)
                half = cvt_i[0] % 2
                cvt_i[0] += 1
                st = stage[:, half * 1664: half * 1664 + (c1 - c0)]
                sk = ('stage', half)
                P.op('sp', lambda e, st=st, c0=c0, c1=c1: e.dma_start(out=st, in_=src[:, c0:c1]),
                     writes=[sk], dma=True)
                if scale is None:
                    P.op('pool', lambda e, st=st, c0=c0, c1=c1: e.tensor_copy(out=dst[:, c0:c1], in_=st),
                         reads=[sk], writes=[dkey])
                else:
                    P.op('act', lambda e, st=st, c0=c0, c1=c1: e.activation(out=dst[:, c0:c1], in_=st,
                                                                           func=AF.Copy, scale=scale),
                         reads=[sk], writes=[dkey])
                c0 = c1

        with ExitStack() as ph:
            xin = ph.enter_context(nc.sbuf_tensor("xin", [128, D], F32))
            for s, n_tok in seqs:
                blocks = [(meta_in, 0, NM, 0)] + [(x_in[s], b * 128, 128, NM + b * 128) for b in range(n_tok // 128)]
                for src, r0, nr, pos in blocks:
                    P.op('sp', lambda e, src=src, r0=r0, nr=nr: e.dma_start(out=xin[:nr, :], in_=src[r0:r0 + nr, :]),
                         writes=['xin'], dma=True)
                    pb, pks = next_ps_big()
                    for k in range(8):
                        P.op('pe', lambda e, k=k, nr=nr, pb=pb: e.transpose(
                            out=pb[:, k * 128:k * 128 + nr], in_=xin[:nr, k * 128:(k + 1) * 128],
                            identity=ident[:nr, :nr]), reads=['xin', 'ident'], writes=pks)
                    P.op('dve', lambda e, nr=nr, pb=pb: e.tensor_copy(
                        out=hbuf[:, :, :nr], in_=pb.rearrange("p (k t) -> p k t", k=8)[:, :, :nr]),
                        reads=pks, writes=['hbuf'])
                    store_h(s, pos, nr)
        P.barrier()

        for l in range(depth):

            if mixers:
                with ExitStack() as ph:
                    def pb_(name, shape, dt=F32):
                        return ph.enter_context(nc.sbuf_tensor(f"{name}_{l}", list(shape), dt))
                    WA = pb_("WA", [128, 8, 2816], BF16)
                    stu = pb_("stu", [128, 2, TT])
                    stqk = pb_("stqk", [128, 8, TT], BF16)
                    sthg = pb_("sthg", [128, 6, TT])
                    stv = pb_("stv", [128, 768], BF16)
                    for k in range(8):
                        load_w(WA[:, k, :], 'WA', w_in[l, k * 128:(k + 1) * 128, 0:2816], 2816, scale=g1s[:, l, k:k + 1])
                    for s, n_tok in seqs:
                        uV = scr[s]['uT'].rearrange("(k p) l -> p k l", p=128)
                        qkV = scr[s]['qkT'].rearrange("(k p) l -> p k l", p=128)
                        hgV = scr[s]['hgT'].rearrange("(k p) l -> p k l", p=128)
                        for pos, n in tiles_of(n_tok):
                            load_h(s, pos, n)
                            rmsnorm_to_xn(n)
                            fm = [(c, 'u', c) for c in (0, 1)] + [(2 + i, 'qk', i) for i in range(8)] + [(14 + i, 'hg', i) for i in range(6)]
                            for cc, kind, idx in fm:
                                ps, pk = next_ps()
                                for k in range(8):
                                    P.op('pe', lambda e, k=k: e.matmul(ps[:, :n], lhsT=WA[:, k, cc * 128:(cc + 1) * 128], rhs=xn[:, k, :n],
                                                                      start=(k == 0), stop=(k == 7)), reads=['WA', 'xn'], writes=[pk])
                                if kind == 'u':
                                    P.op('act', lambda e: e.activation(out=stu[:, idx, :n], in_=ps[:, :n], func=AF.Copy), reads=[pk], writes=['stu'])
                                elif kind == 'qk':
                                    P.op('act', lambda e: e.activation(out=stqk[:, idx, :n], in_=ps[:, :n], func=AF.Copy,
                                                                       scale=(0.125 if idx < 4 else 1.0)), reads=[pk], writes=['stqk'])
                                else:
                                    P.op('dve', lambda e: e.tensor_copy(out=sthg[:, idx, :n], in_=ps[:, :n]), reads=[pk], writes=['sthg'])
                            P.op('sp', lambda e: e.dma_start(out=uV[:, :, pos:pos + n], in_=stu[:, :, :n]), reads=['stu'], writes=[('uT', s)], dma=True)
                            P.op('sp', lambda e: e.dma_start(out=qkV[:, :, pos:pos + n], in_=stqk[:, :, :n]), reads=['stqk'], writes=[('qkT', s)], dma=True)
                            P.op('sp', lambda e: e.dma_start(out=hgV[:, :, pos:pos + n], in_=sthg[:, :, :n]), reads=['sthg'], writes=[('hgT', s)], dma=True)
                            for tb in range((n + 127) // 128):
                                nt = min(128, n - tb * 128)
                                ps, pk = next_ps()
                                ps2, pk2 = next_ps()
                                for k in range(8):
                                    P.op('pe', lambda e, k=k: e.matmul(ps[:nt, 0:512], lhsT=xn[:, k, tb * 128:tb * 128 + nt], rhs=WA[:, k, 1280:1792],
                                                                      start=(k == 0), stop=(k == 7)), reads=['WA', 'xn'], writes=[pk])
                                for k in range(8):
                                    P.op('pe', lambda e, k=k: e.matmul(ps2[:nt, 0:256], lhsT=xn[:, k, tb * 128:tb * 128 + nt], rhs=WA[:, k, 2560:2816],
                                                                      start=(k == 0), stop=(k == 7)), reads=['WA', 'xn'], writes=[pk2])
                                P.op('act', lambda e: e.activation(out=stv[:nt, 0:512], in_=ps[:nt, 0:512], func=AF.Copy), reads=[pk], writes=['stv'])
                                P.op('dve', lambda e: e.tensor_copy(out=stv[:nt, 512:768], in_=ps2[:nt, 0:256]), reads=[pk2], writes=['stv'])
                                r0 = pos + tb * 128
                                P.op('sp', lambda e: e.dma_start(out=scr[s]['vtok'][r0:r0 + nt, :], in_=stv[:nt, 0:512]), reads=['stv'], writes=[('vtok', s)], dma=True)
                                P.op('sp', lambda e: e.dma_start(out=scr[s]['ictok'][r0:r0 + nt, :], in_=stv[:nt, 512:768]), reads=['stv'], writes=[('ictok', s)], dma=True)
                    P.barrier()
                P.barrier()
                if 'a' in mixers:
                  with ExitStack() as ph:
                    def pb_(name, shape, dt=F32):
                        return ph.enter_context(nc.sbuf_tensor(f"{name}_{l}", list(shape), dt))
                    PI = 3.14159265358979
                    MAGIC = 12582912.0

                    def sin_of(dst, src, t1, t2, keys_r, key_w):
                        P.op('dve', lambda e: e.tensor_scalar(out=t1, in0=src, scalar1=1.0 / (2 * PI), scalar2=MAGIC, op0=ALU.mult, op1=ALU.add),
                             reads=keys_r, writes=['s5t1'])
                        P.op('dve', lambda e: e.tensor_scalar(out=t1, in0=t1, scalar1=MAGIC, scalar2=2 * PI, op0=ALU.subtract, op1=ALU.mult),
                             reads=['s5t1'], writes=['s5t1'])
                        P.op('dve', lambda e: e.tensor_tensor(out=t2, in0=src, in1=t1, op=ALU.subtract), reads=keys_r + ['s5t1'], writes=['s5t2'])
                        P.op('dve', lambda e: e.tensor_scalar(out=t2, in0=t2, scalar1=-3.141592, scalar2=3.141592, op0=ALU.max, op1=ALU.min),
                             reads=['s5t2'], writes=['s5t2'])
                        P.op('act', lambda e: e.activation(out=dst, in_=t2, func=AF.Sin), reads=['s5t2'], writes=[key_w])

                    rT = pb_("s5rT", [128, 2, 8, TS])
                    sinT = pb_("s5sin", [128, 2, 8, TS])
                    cosT = pb_("s5cos", [128, 2, 8, TS])
                    BT = pb_("s5BT", [128, 2, 2, 8, 128], BF16)
                    CT = pb_("s5CT", [128, 2, 2, 8, 128], BF16)
                    WGL = pb_("s5WGL", [128, 2, 256], BF16)
                    with ExitStack() as ph2:
                        def pc_(name, shape, dt=F32):
                            return ph2.enter_context(nc.sbuf_tensor(f"{name}_{l}", list(shape), dt))
                        spt = pc_("s5spt", [128, 2, 3, 8])
                        sp2 = pc_("s5sp2", [128, 4, 8])
                        Rp = pc_("s5Rp", [128, 3, 1024])
                        T = [pc_(f"s5T{i}", [128, 1024]) for i in range(8)]
                        Bst = pc_("s5Bst", [128, 2, 8, 128])
                        t1 = T[6]
                        t2 = T[7]
                        for k in range(2):
                            load_w(WGL[:, k, :], 'WGL', w_glu[l, k * 128:(k + 1) * 128, :], 256)
                        P.op('sp', lambda e: e.dma_start(out=spt[:], in_=s5sp[:, l]), writes=['s5spt'], dma=True)
                        for d_ in range(2):
                            P.op('act', lambda e: e.activation(out=sp2[:, 0, :], in_=spt[:, d_, 2, :], func=AF.Exp), reads=['s5spt'], writes=['s5sp2'])
                            P.op('dve', lambda e: e.tensor_tensor(out=sp2[:, 1, :], in0=spt[:, d_, 0, :], in1=sp2[:, 0, :], op=ALU.mult), reads=['s5spt', 's5sp2'], writes=['s5sp2'])
                            P.op('act', lambda e: e.activation(out=sp2[:, 2, :], in_=sp2[:, 1, :], func=AF.Exp), reads=['s5sp2'], writes=['s5sp2'])
                            P.op('dve', lambda e: e.tensor_tensor(out=sp2[:, 3, :], in0=spt[:, d_, 1, :], in1=sp2[:, 0, :], op=ALU.mult), reads=['s5spt', 's5sp2'], writes=['s5sp2'])
                            for j in range(8):
                                P.op('dve', lambda e: e.tensor_scalar(out=rT[:, d_, j, :], in0=onesT[:], scalar1=sp2[:, 2, j:j + 1], scalar2=None, op0=ALU.mult),
                                     reads=['onesT', 's5sp2'], writes=['s5rT'])
                            JH = 1024 // TS
                            for hf in range(8 // JH):
                                for jq in range(JH):
                                    j = hf * JH + jq
                                    P.op('dve', lambda e: e.tensor_scalar(out=T[0][:, jq * TS:(jq + 1) * TS], in0=iota1[:], scalar1=sp2[:, 3, j:j + 1], scalar2=None, op0=ALU.mult),
                                         reads=['iota1', 's5sp2'], writes=['s5T0'])
                                sin_of(sinT[:, d_, hf * JH:(hf + 1) * JH].rearrange("p j n -> p (j n)"), T[0][:], t1[:], t2[:], ['s5T0'], 's5sin')
                                P.op('dve', lambda e: e.tensor_scalar(out=T[0][:], in0=T[0][:], scalar1=PI / 2, scalar2=None, op0=ALU.add), reads=['s5T0'], writes=['s5T0'])
                                sin_of(cosT[:, d_, hf * JH:(hf + 1) * JH].rearrange("p j n -> p (j n)"), T[0][:], t1[:], t2[:], ['s5T0'], 's5cos')
                            P.op('sp', lambda e: e.dma_start(out=Rp[:], in_=s5rep[:, l, d_]), writes=['s5Rp'], dma=True)
                            are, aim, ldt = Rp[:, 0, :], Rp[:, 1, :], Rp[:, 2, :]
                            P.op('act', lambda e: e.activation(out=T[0][:], in_=ldt, func=AF.Exp), reads=['s5Rp'], writes=['s5T0'])
                            P.op('dve', lambda e: e.tensor_tensor(out=T[1][:], in0=are, in1=T[0][:], op=ALU.mult), reads=['s5Rp', 's5T0'], writes=['s5T1'])
                            P.op('act', lambda e: e.activation(out=T[1][:], in_=T[1][:], func=AF.Exp), reads=['s5T1'], writes=['s5T1'])
                            P.op('dve', lambda e: e.tensor_tensor(out=T[0][:], in0=aim, in1=T[0][:], op=ALU.mult), reads=['s5Rp', 's5T0'], writes=['s5T0'])
                            sin_of(T[2][:], T[0][:], t1[:], t2[:], ['s5T0'], 's5T2')
                            P.op('dve', lambda e: e.tensor_scalar(out=T[0][:], in0=T[0][:], scalar1=PI / 2, scalar2=None, op0=ALU.add), reads=['s5T0'], writes=['s5T0'])
                            sin_of(T[3][:], T[0][:], t1[:], t2[:], ['s5T0'], 's5T3')
                            P.op('dve', lambda e: e.tensor_tensor(out=T[2][:], in0=T[2][:], in1=T[1][:], op=ALU.mult), reads=['s5T2', 's5T1'], writes=['s5T2'])
                            P.op('dve', lambda e: e.tensor_tensor(out=T[3][:], in0=T[3][:], in1=T[1][:], op=ALU.mult), reads=['s5T3', 's5T1'], writes=['s5T3'])
                            P.op('dve', lambda e: e.tensor_scalar(out=T[3][:], in0=T[3][:], scalar1=-1.0, scalar2=None, op0=ALU.add), reads=['s5T3'], writes=['s5T3'])
                            P.op('dve', lambda e: e.tensor_tensor(out=T[0][:], in0=are, in1=are, op=ALU.mult), reads=['s5Rp'], writes=['s5T0'])
                            P.op('dve', lambda e: e.tensor_tensor(out=T[1][:], in0=aim, in1=aim, op=ALU.mult), reads=['s5Rp'], writes=['s5T1'])
                            P.op('dve', lambda e: e.tensor_tensor(out=T[0][:], in0=T[0][:], in1=T[1][:], op=ALU.add), reads=['s5T0', 's5T1'], writes=['s5T0'])
                            P.op('dve', lambda e: e.reciprocal(out=T[0][:], in_=T[0][:]), reads=['s5T0'], writes=['s5T0'])
                            P.op('dve', lambda e: e.tensor_tensor(out=T[1][:], in0=T[3][:], in1=are, op=ALU.mult), reads=['s5T3', 's5Rp'], writes=['s5T1'])
                            P.op('dve', lambda e: e.tensor_tensor(out=T[4][:], in0=T[2][:], in1=aim, op=ALU.mult), reads=['s5T2', 's5Rp'], writes=['s5T4'])
                            P.op('dve', lambda e: e.tensor_tensor(out=T[1][:], in0=T[1][:], in1=T[4][:], op=ALU.add), reads=['s5T1', 's5T4'], writes=['s5T1'])
                            P.op('dve', lambda e: e.tensor_tensor(out=T[1][:], in0=T[1][:], in1=T[0][:], op=ALU.mult), reads=['s5T1', 's5T0'], writes=['s5T1'])
                            P.op('dve', lambda e: e.tensor_tensor(out=T[4][:], in0=T[2][:], in1=are, op=ALU.mult), reads=['s5T2', 's5Rp'], writes=['s5T4'])
                            P.op('dve', lambda e: e.tensor_tensor(out=T[5][:], in0=T[3][:], in1=aim, op=ALU.mult), reads=['s5T3', 's5Rp'], writes=['s5T5'])
                            P.op('dve', lambda e: e.tensor_tensor(out=T[4][:], in0=T[4][:], in1=T[5][:], op=ALU.subtract), reads=['s5T4', 's5T5'], writes=['s5T4'])
                            P.op('dve', lambda e: e.tensor_tensor(out=T[4][:], in0=T[4][:], in1=T[0][:], op=ALU.mult), reads=['s5T4', 's5T0'], writes=['s5T4'])
                            zre = T[1][:].rearrange("p (j n) -> p j n", j=8)
                            zim = T[4][:].rearrange("p (j n) -> p j n", j=8)
                            u1 = T[2][:].rearrange("p (j n) -> p j n", j=8)
                            u2 = T[3][:].rearrange("p (j n) -> p j n", j=8)
                            for ri in range(2):
                                P.op('sp', lambda e: e.dma_start(out=Bst[:, ri], in_=s5B[l, d_, ri]), writes=[('s5Bst', ri)], dma=True)
                            P.op('dve', lambda e: e.tensor_tensor(out=u1, in0=zre, in1=Bst[:, 0], op=ALU.mult), reads=['s5T1', ('s5Bst', 0)], writes=['s5T2'])
                            P.op('dve', lambda e: e.tensor_tensor(out=u2, in0=zim, in1=Bst[:, 1], op=ALU.mult), reads=['s5T4', ('s5Bst', 1)], writes=['s5T3'])
                            P.op('dve', lambda e: e.tensor_tensor(out=BT[:, d_, 0], in0=u1, in1=u2, op=ALU.subtract), reads=['s5T2', 's5T3'], writes=['s5BT'])
                            P.op('dve', lambda e: e.tensor_tensor(out=u1, in0=zre, in1=Bst[:, 1], op=ALU.mult), reads=['s5T1', ('s5Bst', 1)], writes=['s5T2'])
                            P.op('dve', lambda e: e.tensor_tensor(out=u2, in0=zim, in1=Bst[:, 0], op=ALU.mult), reads=['s5T4', ('s5Bst', 0)], writes=['s5T3'])
                            P.op('dve', lambda e: e.tensor_tensor(out=BT[:, d_, 1], in0=u1, in1=u2, op=ALU.add), reads=['s5T2', 's5T3'], writes=['s5BT'])
                            for ri in range(2):
                                P.op('sp', lambda e: e.dma_start(out=Bst[:, ri], in_=s5C[l, d_, ri]), writes=[('s5Bst', ri)], dma=True)
                                P.op('act', lambda e: e.activation(out=CT[:, d_, ri], in_=Bst[:, ri], func=AF.Copy, scale=(1.0 if ri == 0 else -1.0)),
                                     reads=[('s5Bst', ri)], writes=['s5CT'])
                        P.barrier()
                    P.barrier()
                    uc = pb_("s5uc", [128, 2, TS])
                    ub = pb_("s5ub", [128, 2, TS], BF16)
                    W1 = pb_("s5W1", [128, TS])
                    W2 = pb_("s5W2", [128, TS])
                    W3 = pb_("s5W3", [128, TS])
                    W4 = pb_("s5W4", [128, TS])
                    V1 = pb_("s5V1", [128, TS])
                    V2 = pb_("s5V2", [128, TS])
                    V3 = pb_("s5V3", [128, TS])
                    V4 = pb_("s5V4", [128, TS])
                    btr = pb_("s5btr", [128, TS])
                    bti = pb_("s5bti", [128, TS])
                    wre = pb_("s5wre", [128, TS])
                    wim = pb_("s5wim", [128, TS])
                    xre = pb_("s5xre", [128, TS])
                    xim = pb_("s5xim", [128, TS])
                    xb = pb_("s5xb", [128, 2, 8, TS], BF16)
                    xin = pb_("s5xin", [128, 2, 8])
                    yf = pb_("s5yf", [128, 2, TS])
                    yt = pb_("s5yt", [128, 2, TS])
                    g32 = pb_("s5g32", [128, 2, TS])
                    gb = pb_("s5gb", [128, 2, TS], BF16)
                    yast = pb_("s5yast", [128, 2, TS], BF16)
                    for s, n_tok in seqs:
                        uV = scr[s]['uT'].rearrange("(k p) l -> p k l", p=128)
                        ysV = scr[s]['ysT'].rearrange("(k p) l -> p k l", p=128)
                        yaV = scr[s]['yaT'].rearrange("(k p) l -> p k l", p=128)
                        chunks = [(0, NM)] + [(NM + i * TS, TS) for i in range(n_tok // TS)]
                        for d_ in range(2):
                            bwd = (d_ == 1)
                            clist = chunks if not bwd else chunks[::-1]
                            P.op('dve', lambda e: e.memset(xin[:], 0.0), writes=['s5xin'])
                            for pos, n in clist:
                                P.op('sp', lambda e: e.dma_start(out=uc[:, :, :n], in_=uV[:, :, pos:pos + n]), reads=[('uT', s)], writes=['s5uc'], dma=True)
                                if bwd:
                                    P.op('pool', lambda e: e.dma_start(out=yf[:, :, :n], in_=ysV[:, :, pos:pos + n]), reads=[('ysT', s)], writes=['s5yf'], dma=True)
                                    P.op('act', lambda e: e.activation(out=ub[:, :, :n], in_=uc[:, :, n - 1::-1], func=AF.Copy), reads=['s5uc'], writes=['s5ub'])
                                else:
                                    P.op('act', lambda e: e.activation(out=ub[:, :, :n], in_=uc[:, :, :n], func=AF.Copy), reads=['s5uc'], writes=['s5ub'])
                                for j in range(8):
                                    kt = j // 4
                                    psb, pkb = next_ps()
                                    P.op('pe', lambda e: e.matmul(psb[:, 0:n], lhsT=BT[:, d_, 0, j, :], rhs=ub[:, kt, :n], start=True, stop=True),
                                         reads=['s5BT', 's5ub'], writes=[pkb])
                                    P.op('pe', lambda e: e.matmul(psb[:, TS:TS + n], lhsT=BT[:, d_, 1, j, :], rhs=ub[:, kt, :n], start=True, stop=True),
                                         reads=['s5BT', 's5ub'], writes=[pkb])
                                    cs, sn = cosT[:, d_, j, :n], sinT[:, d_, j, :n]
                                    bre, bim = psb[:, 0:n], psb[:, TS:TS + n]
                                    P.op('dve', lambda e: e.tensor_tensor(out=W1[:, :n], in0=bre, in1=cs, op=ALU.mult), reads=[pkb, 's5cos'], writes=['s5W1'])
                                    P.op('dve', lambda e: e.tensor_tensor(out=W2[:, :n], in0=bim, in1=sn, op=ALU.mult), reads=[pkb, 's5sin'], writes=['s5W2'])
                                    P.op('dve', lambda e: e.tensor_tensor(out=btr[:, :n], in0=W1[:, :n], in1=W2[:, :n], op=ALU.add), reads=['s5W1', 's5W2'], writes=['s5btr'])
                                    P.op('dve', lambda e: e.tensor_tensor(out=W3[:, :n], in0=bim, in1=cs, op=ALU.mult), reads=[pkb, 's5cos'], writes=['s5W3'])
                                    P.op('dve', lambda e: e.tensor_tensor(out=W4[:, :n], in0=bre, in1=sn, op=ALU.mult), reads=[pkb, 's5sin'], writes=['s5W4'])
                                    P.op('dve', lambda e: e.tensor_tensor(out=bti[:, :n], in0=W3[:, :n], in1=W4[:, :n], op=ALU.subtract), reads=['s5W3', 's5W4'], writes=['s5bti'])
                                    P.op('dve', lambda e: e.tensor_tensor_scan(out=wre[:, :n], data0=rT[:, d_, j, :n], data1=btr[:, :n], initial=xin[:, 0, j:j + 1],
                                                                              op0=ALU.mult, op1=ALU.add), reads=['s5rT', 's5btr', 's5xin'], writes=['s5wre'])
                                    P.op('dve', lambda e: e.tensor_tensor_scan(out=wim[:, :n], data0=rT[:, d_, j, :n], data1=bti[:, :n], initial=xin[:, 1, j:j + 1],
                                                                              op0=ALU.mult, op1=ALU.add), reads=['s5rT', 's5bti', 's5xin'], writes=['s5wim'])
                                    P.op('pool', lambda e: e.tensor_tensor(out=V1[:, :n], in0=wre[:, :n], in1=cs, op=ALU.mult), reads=['s5wre', 's5cos'], writes=['s5V1'])
                                    P.op('pool', lambda e: e.tensor_tensor(out=V2[:, :n], in0=wim[:, :n], in1=sn, op=ALU.mult), reads=['s5wim', 's5sin'], writes=['s5V2'])
                                    P.op('pool', lambda e: e.tensor_tensor(out=xre[:, :n], in0=V1[:, :n], in1=V2[:, :n], op=ALU.subtract), reads=['s5V1', 's5V2'], writes=['s5xre'])
                                    P.op('pool', lambda e: e.tensor_tensor(out=V3[:, :n], in0=wre[:, :n], in1=sn, op=ALU.mult), reads=['s5wre', 's5sin'], writes=['s5V3'])
                                    P.op('pool', lambda e: e.tensor_tensor(out=V4[:, :n], in0=wim[:, :n], in1=cs, op=ALU.mult), reads=['s5wim', 's5cos'], writes=['s5V4'])
                                    P.op('pool', lambda e: e.tensor_tensor(out=xim[:, :n], in0=V3[:, :n], in1=V4[:, :n], op=ALU.add), reads=['s5V3', 's5V4'], writes=['s5xim'])
                                    P.op('act', lambda e: e.activation(out=xb[:, 0, j, :n], in_=xre[:, :n], func=AF.Copy), reads=['s5xre'], writes=['s5xb'])
                                    P.op('act', lambda e: e.activation(out=xb[:, 1, j, :n], in_=xim[:, :n], func=AF.Copy), reads=['s5xim'], writes=['s5xb'])
                                    P.op('act', lambda e: e.activation(out=xin[:, 0, j:j + 1], in_=xre[:, n - 1:n], func=AF.Copy), reads=['s5xre'], writes=['s5xin'])
                                    P.op('act', lambda e: e.activation(out=xin[:, 1, j:j + 1], in_=xim[:, n - 1:n], func=AF.Copy), reads=['s5xim'], writes=['s5xin'])
                                psy, pky = next_ps()
                                for m in range(2):
                                    first = True
                                    for j in range(4 * m, 4 * m + 4):
                                        for ri in range(2):
                                            last = (j == 4 * m + 3 and ri == 1)
                                            P.op('pe', lambda e: e.matmul(psy[:, m * TS:m * TS + n], lhsT=CT[:, d_, ri, j, :], rhs=xb[:, ri, j, :n],
                                                                          start=first, stop=last), reads=['s5CT', 's5xb'], writes=[pky])
                                            first = False
                                psy3 = psy[:, 0:2 * TS].rearrange("p (m t) -> p m t", m=2)
                                if not bwd:
                                    P.op('act', lambda e: e.activation(out=yt[:, :, :n], in_=psy3[:, :, :n], func=AF.Copy), reads=[pky], writes=['s5yt'])
                                    P.op('sp', lambda e: e.dma_start(out=ysV[:, :, pos:pos + n], in_=yt[:, :, :n]), reads=['s5yt'], writes=[('ysT', s)], dma=True)
                                    continue
                                P.op('dve', lambda e: e.tensor_tensor(out=yt[:, :, :n], in0=psy3[:, :, n - 1::-1], in1=yf[:, :, :n], op=ALU.add),
                                     reads=[pky, 's5yf'], writes=['s5yt'])
                                for m in range(2):
                                    P.op('dve', lambda e: e.scalar_tensor_tensor(out=yt[:, m, :n], in0=uc[:, m, :n], scalar=s5ds[:, l, m:m + 1], in1=yt[:, m, :n],
                                                                                 op0=ALU.mult, op1=ALU.add), reads=['s5uc', 's5yt', 's5ds'], writes=['s5yt'])
                                P.op('dve', lambda e: e.tensor_tensor(out=g32[:, :, :n], in0=yt[:, :, :n], in1=yt[:, :, :n], op=ALU.mult), reads=['s5yt'], writes=['s5g32'])
                                P.op('dve', lambda e: e.tensor_scalar(out=g32[:, :, :n], in0=g32[:, :, :n], scalar1=0.044715, scalar2=1.0, op0=ALU.mult, op1=ALU.add),
                                     reads=['s5g32'], writes=['s5g32'])
                                P.op('dve', lambda e: e.tensor_tensor(out=g32[:, :, :n], in0=g32[:, :, :n], in1=yt[:, :, :n], op=ALU.mult), reads=['s5g32', 's5yt'], writes=['s5g32'])
                                P.op('act', lambda e: e.activation(out=g32[:, :, :n], in_=g32[:, :, :n], func=AF.Sigmoid, scale=1.5957691216057308),
                                     reads=['s5g32'], writes=['s5g32'])
                                P.op('dve', lambda e: e.tensor_tensor(out=g32[:, :, :n], in0=g32[:, :, :n], in1=yt[:, :, :n], op=ALU.mult), reads=['s5g32', 's5yt'], writes=['s5g32'])
                                P.op('act', lambda e: e.activation(out=gb[:, :, :n], in_=g32[:, :, :n], func=AF.Copy), reads=['s5g32'], writes=['s5gb'])
                                psg, pkg = next_ps()
                                for m2 in range(2):
                                    for k2 in range(2):
                                        P.op('pe', lambda e: e.matmul(psg[:, m2 * TS:m2 * TS + n], lhsT=WGL[:, k2, m2 * 128:(m2 + 1) * 128], rhs=gb[:, k2, :n],
                                                                      start=(k2 == 0), stop=(k2 == 1)), reads=['WGL', 's5gb'], writes=[pkg])
                                psg3 = psg[:, 0:2 * TS].rearrange("p (m t) -> p m t", m=2)
                                P.op('act', lambda e: e.activation(out=yt[:, :, :n], in_=psg3[:, :, :n], func=AF.Sigmoid), reads=[pkg], writes=['s5yt'])
                                P.op('dve', lambda e: e.tensor_tensor(out=yast[:, :, :n], in0=g32[:, :, :n], in1=yt[:, :, :n], op=ALU.mult), reads=['s5g32', 's5yt'], writes=['s5yast'])
                                P.op('sp', lambda e: e.dma_start(out=yaV[:, :, pos:pos + n], in_=yast[:, :, :n]), reads=['s5yast'], writes=[('yaT', s)], dma=True)
                    P.barrier()
                P.barrier()
                if 'b' in mixers:
                  with ExitStack() as ph:
                    def pb_(name, shape, dt=F32):
                        return ph.enter_context(nc.sbuf_tensor(f"{name}_{l}", list(shape), dt))
                    negm = pb_("nnegm", [128, 128])
                    BI = [pb_("nBI0", [128, 8, 5, 128]), pb_("nBI1", [128, 8, 5, 128])]
                    tmpb = pb_("ntmpb", [128, 8, 5, 128])
                    Kmeta = pb_("nKm", [64, 8, 16], BF16)
                    Vmeta = pb_("nVm", [16, 8, 65], BF16)
                    Qp = pb_("nQp", [64, 8, 128], BF16)
                    Kp = pb_("nKp", [64, 8, 640], BF16)
                    Vp = pb_("nVp", [128, 5, 8, 65], BF16)
                    sc = pb_("nsc", [128, 640])
                    Pt = pb_("nPt", [128, 5, 128], BF16)
                    Pm = pb_("nPm", [16, 128], BF16)
                    rec = pb_("nrec", [128, 8, 1])
                    Otok = pb_("nOtok", [128, 8, 64])
                    ybst = pb_("nybst", [128, 4, 128], BF16)
                    P.op('sp', lambda e: e.dma_start(out=negm[:], in_=c_negm), writes=['nnegm'], dma=True)
                    P.op('dve', lambda e: e.memset(Vp[:], 1.0), writes=['nVp'])
                    P.op('dve', lambda e: e.memset(Vmeta[:], 1.0), writes=['nVm'])

                    def variant_of(r, rows):
                        a_ = min(max(r - 4, 0), rows - 10)
                        ds = []
                        for j in range(10):
                            for dl in range(2):
                                kr = a_ + j
                                rq = r + dl
                                rs = min(max(rq - 4, 0), rows - 8)
                                ds.append(kr - rq + 7 if rs <= kr < rs + 8 else 15)
                        return a_, tuple(ds)

                    def build_bias(buf, bkey, var):
                        for j in range(10):
                            t, jj = j // 2, j % 2
                            for dl in range(2):
                                dd = var[j * 2 + dl]
                                src = bass.AP(tensor=rpbpad.tensor, offset=(l * 8 * 16 + dd) * 127, ap=[[1, 64], [16 * 127, 8], [1, 64]])
                                P.op('sp', lambda e: e.dma_start(out=tmpb[jj * 64:(jj + 1) * 64, :, t, dl * 64:(dl + 1) * 64], in_=src),
                                     writes=['ntmpb'], dma=True)
                        for h in range(8):
                            for t in range(5):
                                P.op('dve', lambda e: e.tensor_tensor(
                                    out=buf[:, h, t, :].rearrange("p (a q) -> p a q", a=2),
                                    in0=tmpb[:, h, t, :].rearrange("p (a q) -> p a q", a=2)[:, :, ::-1],
                                    in1=negm[:].rearrange("p (a q) -> p a q", a=2), op=ALU.add),
                                    reads=['ntmpb', 'nnegm'], writes=[bkey])

                    for s, n_tok in seqs:
                        rows = n_tok // 64
                        qkV = scr[s]['qkT'].rearrange("(a h d) l -> d a h l", a=2, h=8, d=64)
                        ybV = scr[s]['ybT'].rearrange("(k p) l -> p k l", p=128)
                        vt = scr[s]['vtok']

                        def finish(pso, pko, nq, pos):
                            pso3 = pso[:].rearrange("p (h c) -> p h c", h=8)
                            P.op('dve', lambda e: e.reciprocal(out=rec[:nq], in_=pso3[:nq, :, 64:65]), reads=pko, writes=['nrec'])
                            P.op('dve', lambda e: e.tensor_tensor(out=Otok[:nq], in0=pso3[:nq, :, 0:64], in1=rec[:nq].to_broadcast([nq, 8, 64]), op=ALU.mult),
                                 reads=pko + ['nrec'], writes=['nOtok'])
                            pst, pkt = ps_t[2][:, 0:512], ('ps', 4)
                            Of = Otok[:].rearrange("p h d -> p (h d)")
                            for k in range(4):
                                P.op('pe', lambda e: e.transpose(out=pst[:, k * 128:k * 128 + nq], in_=Of[:nq, k * 128:(k + 1) * 128], identity=ident[:nq, :nq]),
                                     reads=['nOtok', 'ident'], writes=[pkt])
                            P.op('act', lambda e: e.activation(out=ybst[:, :, :nq], in_=pst[:, :].rearrange("p (k t) -> p k t", k=4)[:, :, :nq], func=AF.Copy),
                                 reads=[pkt], writes=['nybst'])
                            P.op('sp', lambda e: e.dma_start(out=ybV[:, :, pos:pos + nq], in_=ybst[:, :, :nq]), reads=['nybst'], writes=[('ybT', s)], dma=True)

                        P.op('sp', lambda e: e.dma_start(out=Kmeta[:], in_=qkV[:, 1, :, 0:NM]), reads=[('qkT', s)], writes=['nKm'], dma=True)
                        P.op('sp', lambda e: e.dma_start(out=Vmeta[:, :, 0:64], in_=vt[0:NM, :].rearrange("t (h d) -> t h d", h=8)),
                             reads=[('vtok', s)], writes=['nVm'], dma=True)
                        P.op('sp', lambda e: e.dma_start(out=Qp[:, :, 0:NM], in_=qkV[:, 0, :, 0:NM]), reads=[('qkT', s)], writes=['nQp'], dma=True)
                        pss, pks = ps_t[0], [('ps', 0), ('ps', 1)]
                        for h in range(8):
                            P.op('pe', lambda e: e.matmul(pss[:NM, h * 16:(h + 1) * 16], lhsT=Kmeta[:, h, :], rhs=Qp[:, h, 0:NM], start=True, stop=True),
                                 reads=['nKm', 'nQp'], writes=pks)
                        P.op('act', lambda e: e.activation(out=Pm[:, :], in_=pss[:NM, 0:128], func=AF.Exp), reads=pks, writes=['nPm'])
                        pso, pko = ps_t[3], [('ps', 6), ('ps', 7)]
                        for h in range(8):
                            P.op('pe', lambda e: e.matmul(pso[:NM, h * 128:h * 128 + 65], lhsT=Pm[:, h * 16:(h + 1) * 16], rhs=Vmeta[:, h, :], start=True, stop=True),
                                 reads=['nPm', 'nVm'], writes=pko)
                        finish(pso, pko, NM, 0)
                        vars_ = [variant_of(r, rows) for r in range(0, rows, 2)]
                        cnt = {}
                        for _, v in vars_:
                            cnt[v] = cnt.get(v, 0) + 1
                        vint = max(cnt, key=cnt.get)
                        build_bias(BI[0], 'nBI0', vint)
                        cur = [None]
                        for pi, (a_, var) in enumerate(vars_):
                            r = pi * 2
                            if var == vint:
                                bi, bkey = BI[0], 'nBI0'
                            else:
                                if cur[0] != var:
                                    build_bias(BI[1], 'nBI1', var)
                                    cur[0] = var
                                bi, bkey = BI[1], 'nBI1'
                            pq = NM + r * 64
                            kp0 = NM + a_ * 64
                            P.op('sp', lambda e: e.dma_start(out=Qp[:], in_=qkV[:, 0, :, pq:pq + 128]), reads=[('qkT', s)], writes=['nQp'], dma=True)
                            P.op('sp', lambda e: e.dma_start(out=Kp[:], in_=qkV[:, 1, :, kp0:kp0 + 640]), reads=[('qkT', s)], writes=['nKp'], dma=True)
                            for t in range(5):
                                P.op('pool', lambda e: e.dma_start(out=Vp[:, t, :, 0:64], in_=vt[kp0 + t * 128:kp0 + (t + 1) * 128, :].rearrange("p (h d) -> p h d", h=8)),
                                     reads=[('vtok', s)], writes=['nVp'], dma=True)
                            pso, pko = ps_t[3], [('ps', 6), ('ps', 7)]
                            for h in range(8):
                                pss, pks = (ps_t[0], [('ps', 0), ('ps', 1)]) if h % 2 == 0 else (ps_t[1], [('ps', 2), ('ps', 3)])
                                for t in range(5):
                                    P.op('pe', lambda e: e.matmul(pss[:, t * 128:(t + 1) * 128], lhsT=Kp[:, h, t * 128:(t + 1) * 128], rhs=Qp[:, h, :], start=True, stop=True),
                                         reads=['nKp', 'nQp'], writes=pks)
                                P.op('pe', lambda e: e.matmul(pss[:NM, 640:768], lhsT=Kmeta[:, h, :], rhs=Qp[:, h, :], start=True, stop=True),
                                     reads=['nKm', 'nQp'], writes=pks)
                                P.op('dve', lambda e: e.tensor_tensor(out=sc[:], in0=pss[:, 0:640], in1=bi[:, h].rearrange("p t q -> p (t q)"), op=ALU.add),
                                     reads=pks + [bkey], writes=['nsc'])
                                P.op('act', lambda e: e.activation(out=Pt[:].rearrange("p t q -> p (t q)"), in_=sc[:], func=AF.Exp), reads=['nsc'], writes=['nPt'])
                                P.op('act', lambda e: e.activation(out=Pm[:, :], in_=pss[:NM, 640:768], func=AF.Exp), reads=pks, writes=['nPm'])
                                for t in range(5):
                                    P.op('pe', lambda e: e.matmul(pso[:, h * 128:h * 128 + 65], lhsT=Pt[:, t, :], rhs=Vp[:, t, h, :], start=(t == 0), stop=False),
                                         reads=['nPt', 'nVp'], writes=pko)
                                P.op('pe', lambda e: e.matmul(pso[:, h * 128:h * 128 + 65], lhsT=Pm[:, :], rhs=Vmeta[:, h, :], start=False, stop=True),
                                     reads=['nPm', 'nVm'], writes=pko)
                            finish(pso, pko, 128, pq)
                    P.barrier()
                P.barrier()
                if 'c' in mixers:
                  with ExitStack() as ph:
                    def pb_(name, shape, dt=F32):
                        return ph.enter_context(nc.sbuf_tensor(f"{name}_{l}", list(shape), dt))
                    qf = pb_("hq", [64, 4, 128])
                    ff = pb_("hf", [64, 4, 128])
                    gl = pb_("hgl", [64, 4, 128])
                    kk = pb_("hk", [64, 4, 128])
                    Bc = pb_("hB", [64, 4, 128])
                    Bl = pb_("hBl", [64, 4, 128])
                    Be = pb_("hBe", [64, 4, 128])
                    REF = pb_("hREF", [64, 4, 4])
                    END = pb_("hEND", [64, 4, 4])
                    DEC = pb_("hDEC", [64, 4, 4])
                    Qt = pb_("hQt", [64, 4, 128], BF16)
                    Kt = pb_("hKt", [64, 4, 128], BF16)
                    Kh = pb_("hKh", [64, 4, 128])
                    Khtok = pb_("hKhtok", [128, 4, 256], BF16)
                    Vtok = pb_("hVtok", [128, 256], BF16)
                    attm = pb_("hattm", [128, 4, 128], BF16)
                    S32 = pb_("hS32", [64, 4, 64])
                    Sbf = pb_("hSbf", [64, 4, 5, 64], BF16)
                    ob_ = pb_("hob", [64, 4, 128])
                    ob2 = pb_("hob2", [64, 4, 128])
                    for s, n_tok in seqs:
                        hgV = scr[s]['hgT'].rearrange("(k h d) l -> d k h l", k=3, h=4, d=64)
                        ycV = scr[s]['ycT'].rearrange("(h d) l -> d h l", d=64)
                        groups = [(0, NM)] + [(NM + i * 128, 128) for i in range(n_tok // 128)]
                        for di in range(2):
                            bwd = (di == 1)
                            glist = groups if not bwd else groups[1:][::-1] + groups[:1]
                            mask = maskb if bwd else maskf
                            P.op('dve', lambda e: e.memset(S32[:], 0.0), writes=['S32'])
                            P.op('dve', lambda e: e.memset(Sbf[:], 0.0), writes=[('Sbf', i) for i in range(5)])
                            for pos, n in glist:
                                csz = min(32, n)
                                nch = n // csz
                                P.op('sp', lambda e: e.dma_start(out=qf[:, :, :n], in_=hgV[:, 0, :, pos:pos + n]), reads=[('hgT', s)], writes=['hq'], dma=True)
                                P.op('sp', lambda e: e.dma_start(out=ff[:, :, :n], in_=hgV[:, (2 if bwd else 1), :, pos:pos + n]),
                                     reads=[('hgT', s)], writes=['hf'], dma=True)
                                P.op('pool', lambda e: e.dma_start(out=Vtok[:n, :], in_=scr[s]['ictok'][pos:pos + n, :]), reads=[('ictok', s)], writes=['hVtok'], dma=True)
                                if bwd:
                                    P.op('pool', lambda e: e.dma_start(out=ob2[:, :, :n], in_=ycV[:, :, pos:pos + n]), reads=[('ycT', s)], writes=['hob2'], dma=True)
                                P.op('act', lambda e: e.activation(out=ff[:, :, :n], in_=ff[:, :, :n], func=AF.Sigmoid), reads=['hf'], writes=['hf'])
                                for h in range(4):
                                    P.op('act', lambda e: e.activation(out=gl[:, h, :n], in_=ff[:, h, :n], func=AF.Ln,
                                                                       bias=lbs[:, h, l:l + 1], scale=oml[:, h, l:l + 1]),
                                         reads=['hf', 'lbs', 'oml'], writes=['hgl'])
                                    P.op('dve', lambda e: e.tensor_scalar(out=kk[:, h, :n], in0=ff[:, h, :n], scalar1=noml[:, h, l:l + 1],
                                                                          scalar2=oml[:, h, l:l + 1], op0=ALU.mult, op1=ALU.add),
                                         reads=['hf', 'oml', 'noml'], writes=['hk'])
                                    if not bwd:
                                        P.op('dve', lambda e: e.tensor_tensor_scan(out=Bc[:, h, :n], data0=onesT[0:64, :n], data1=gl[:, h, :n],
                                                                                  initial=0.0, op0=ALU.mult, op1=ALU.add),
                                             reads=['hgl', 'onesT'], writes=['hB'])
                                    else:
                                        P.op('dve', lambda e: e.tensor_tensor_scan(out=Bc[:, h, n - 1::-1] if n < 128 else Bc[:, h, ::-1],
                                                                                  data0=onesT[0:64, :n],
                                                                                  data1=gl[:, h, n - 1::-1] if n < 128 else gl[:, h, ::-1],
                                                                                  initial=0.0, op0=ALU.mult, op1=ALU.add),
                                             reads=['hgl', 'onesT'], writes=['hB'])
                                P.op('act', lambda e: e.activation(out=qf[:, :, :n], in_=qf[:, :, :n], func=AF.Silu), reads=['hq'], writes=['hq'])
                                if hg_stop < 2:
                                    continue
                                P.op('dve', lambda e: e.memset(REF[:], 0.0), writes=['hREF'])
                                if nch > 1:
                                    if not bwd:
                                        P.op('dve', lambda e: e.tensor_copy(out=REF[:, :, 1:nch], in_=Bc[:, :, csz - 1:n - 1:csz]), reads=['hB'], writes=['hREF'])
                                    else:
                                        P.op('dve', lambda e: e.tensor_copy(out=REF[:, :, 0:nch - 1], in_=Bc[:, :, csz:n:csz]), reads=['hB'], writes=['hREF'])
                                if not bwd:
                                    P.op('dve', lambda e: e.tensor_copy(out=END[:, :, 0:nch], in_=Bc[:, :, csz - 1:n:csz]), reads=['hB'], writes=['hEND'])
                                else:
                                    P.op('dve', lambda e: e.tensor_copy(out=END[:, :, 0:nch], in_=Bc[:, :, 0:n:csz]), reads=['hB'], writes=['hEND'])
                                for h in range(4):
                                    P.op('dve', lambda e: e.tensor_tensor(
                                        out=Bl[:, h, :n].rearrange("p (c j) -> p c j", j=csz), in0=Bc[:, h, :n].rearrange("p (c j) -> p c j", j=csz),
                                        in1=REF[:, h, 0:nch].unsqueeze(2).to_broadcast([64, nch, csz]), op=ALU.subtract),
                                        reads=['hB', 'hREF'], writes=['hBl'])
                                    P.op('dve', lambda e: e.tensor_tensor(
                                        out=Be[:, h, :n].rearrange("p (c j) -> p c j", j=csz), in0=Bc[:, h, :n].rearrange("p (c j) -> p c j", j=csz),
                                        in1=END[:, h, 0:nch].unsqueeze(2).to_broadcast([64, nch, csz]), op=ALU.subtract),
                                        reads=['hB', 'hEND'], writes=['hBe'])
                                P.op('dve', lambda e: e.tensor_tensor(out=DEC[:, :, 0:nch], in0=END[:, :, 0:nch], in1=REF[:, :, 0:nch], op=ALU.subtract),
                                     reads=['hEND', 'hREF'], writes=['hDEC'])
                                P.op('act', lambda e: e.activation(out=DEC[:, :, 0:nch], in_=DEC[:, :, 0:nch], func=AF.Exp), reads=['hDEC'], writes=['hDEC'])
                                P.op('act', lambda e: e.activation(out=Bc[:, :, :n], in_=Bl[:, :, :n], func=AF.Exp), reads=['hBl'], writes=['hB'])
                                P.op('act', lambda e: e.activation(out=Bl[:, :, :n], in_=Bl[:, :, :n], func=AF.Exp, scale=-1.0), reads=['hBl'], writes=['hBl'])
                                P.op('act', lambda e: e.activation(out=Be[:, :, :n], in_=Be[:, :, :n], func=AF.Exp, scale=-1.0), reads=['hBe'], writes=['hBe'])
                                P.op('dve', lambda e: e.tensor_tensor(out=Qt[:, :, :n], in0=qf[:, :, :n], in1=Bc[:, :, :n], op=ALU.mult), reads=['hq', 'hB'], writes=['hQt'])
                                P.op('dve', lambda e: e.tensor_tensor(out=Kt[:, :, :n], in0=kk[:, :, :n], in1=Bl[:, :, :n], op=ALU.mult), reads=['hk', 'hBl'], writes=['hKt'])
                                P.op('dve', lambda e: e.tensor_tensor(out=Kh[:, :, :n], in0=kk[:, :, :n], in1=Be[:, :, :n], op=ALU.mult), reads=['hk', 'hBe'], writes=['hKh'])
                                if hg_stop < 3:
                                    continue
                                pst, pkt = next_ps()
                                for h in range(4):
                                    P.op('pe', lambda e: e.transpose(out=pst[:n, h * 64:(h + 1) * 64], in_=Kh[:, h, :n], identity=ident[0:64, 0:64]),
                                         reads=['hKh', 'ident'], writes=[pkt])
                                for c in range(nch):
                                    P.op('act', lambda e: e.activation(out=Khtok[:n, c, :], in_=pst[:n, 0:256], func=AF.Copy, scale=cmask[:n, c:c + 1]),
                                         reads=[pkt, 'cmask'], writes=['hKhtok'])
                                if hg_stop < 4:
                                    continue
                                psa, pka = next_ps()
                                for h in range(4):
                                    P.op('pe', lambda e: e.matmul(psa[:n, h * 128:h * 128 + n], lhsT=Kt[:, h, :n], rhs=Qt[:, h, :n],
                                                                  start=True, stop=True), reads=['hKt', 'hQt'], writes=[pka])
                                P.op('dve', lambda e: e.tensor_tensor(out=attm[:n, :, :n], in0=psa[:n, :].rearrange("p (h i) -> p h i", h=4)[:, :, :n],
                                                                      in1=mask[:n, :n].unsqueeze(1).to_broadcast([n, 4, n]), op=ALU.mult),
                                     reads=[pka, 'maskf', 'maskb'], writes=['hattm'])
                                if hg_stop < 5:
                                    continue
                                pso, pko = next_ps()
                                psd, pkd = next_ps_big()
                                order = list(range(nch)) if not bwd else list(range(nch))[::-1]
                                for h in range(4):
                                    for c in range(nch):
                                        P.op('pe', lambda e: e.matmul(
                                            psd[0:64, (h * 4 + c) * 64:(h * 4 + c) * 64 + 64], lhsT=Khtok[:n, c, h * 64:(h + 1) * 64],
                                            rhs=Vtok[:n, h * 64:(h + 1) * 64], start=True, stop=True),
                                            reads=['hKhtok', 'hVtok'], writes=pkd)
                                for i, c in enumerate(order):
                                    for h in range(4):
                                        P.op('dve', lambda e: e.scalar_tensor_tensor(
                                            out=S32[:, h, :], in0=S32[:, h, :], scalar=DEC[:, h, c:c + 1], in1=psd[0:64, (h * 4 + c) * 64:(h * 4 + c) * 64 + 64],
                                            op0=ALU.mult, op1=ALU.add), reads=['S32', 'hDEC'] + pkd, writes=['S32'])
                                    P.op('act', lambda e: e.activation(out=Sbf[:, :, i + 1, :], in_=S32[:], func=AF.Copy), reads=['S32'], writes=[('Sbf', i + 1)])
                                if hg_stop < 6:
                                    continue
                                for h in range(4):
                                    for i, c in enumerate(order):
                                        P.op('pe', lambda e: e.matmul(pso[0:64, h * 128 + c * csz:h * 128 + (c + 1) * csz], lhsT=Vtok[:n, h * 64:(h + 1) * 64],
                                                                      rhs=attm[:n, h, c * csz:(c + 1) * csz], start=True, stop=False),
                                             reads=['hVtok', 'hattm'], writes=[pko])
                                        P.op('pe', lambda e: e.matmul(
                                            pso[0:64, h * 128 + c * csz:h * 128 + (c + 1) * csz], lhsT=Sbf[:, h, i, :],
                                            rhs=Qt[:, h, c * csz:(c + 1) * csz], start=False, stop=True),
                                            reads=[('Sbf', i), 'hQt'], writes=[pko])
                                P.op('act', lambda e: e.activation(out=Sbf[:, :, 0, :], in_=S32[:], func=AF.Copy), reads=['S32'] + [('Sbf', i) for i in range(5)],
                                     writes=[('Sbf', 0)])
                                if not bwd:
                                    P.op('dve', lambda e: e.tensor_copy(out=ob_[:, :, :n], in_=pso[0:64, :].rearrange("p (h i) -> p h i", h=4)[:, :, :n]),
                                         reads=[pko], writes=['hob'])
                                else:
                                    P.op('dve', lambda e: e.tensor_tensor(out=ob_[:, :, :n], in0=pso[0:64, :].rearrange("p (h i) -> p h i", h=4)[:, :, :n],
                                                                          in1=ob2[:, :, :n], op=ALU.add), reads=[pko, 'hob2'], writes=['hob'])
                                P.op('sp', lambda e: e.dma_start(out=ycV[:, :, pos:pos + n], in_=ob_[:, :, :n]), reads=['hob'], writes=[('ycT', s)], dma=True)
                    P.barrier()
                P.barrier()
            with ExitStack() as ph:
                def pb_(name, shape, dt=F32):
                    return ph.enter_context(nc.sbuf_tensor(f"{name}_{l}", list(shape), dt))
                WC = pb_("WC", [128, 8, 3328], BF16)
                WUA = pb_("WUA", [128, 2, D], BF16)
                WUB = pb_("WUB", [128, 4, D], BF16)
                WUC = pb_("WUC", [128, 2, D], BF16)
                WO = pb_("WO", [128, 8, D], BF16)
                ya = pb_("ya", [128, 2, TT], BF16)
                yb = pb_("yb", [128, 4, TT], BF16)
                yc = pb_("yc", [128, 2, TT])
                ycs = pb_("ycs", [128, 2, TT])
                ycn = pb_("ycn", [128, 2, TT], BF16)
                rc = pb_("rc", [128, TT])
                sg = pb_("sg", [128, 3, TT])
                m1 = pb_("m1", [128, TT])
                m2 = pb_("m2", [128, TT])
                mix = pb_("mix", [128, 8, TT], BF16)
                for k in range(8):
                    load_w(WC[:, k, :], 'WC', w_in[l, k * 128:(k + 1) * 128, 2816:6144], 3328, scale=g1s[:, l, k:k + 1])
                    load_w(WO[:, k, :], 'WO', w_o[l, k * 128:(k + 1) * 128, :], D)
                for k in range(2):
                    load_w(WUA[:, k, :], 'WUA', w_up_a[l, k * 128:(k + 1) * 128, :], D)
                    load_w(WUC[:, k, :], 'WUC', w_up_c[l, k * 128:(k + 1) * 128, :], D)
                for k in range(4):
                    load_w(WUB[:, k, :], 'WUB', w_up_b[l, k * 128:(k + 1) * 128, :], D)
                for s, n_tok in seqs:
                    yaV = scr[s]['yaT'].rearrange("(k p) l -> p k l", p=128)
                    ybV = scr[s]['ybT'].rearrange("(k p) l -> p k l", p=128)
                    ycV = scr[s]['ycT'].rearrange("(k p) l -> p k l", p=128)
                    for pos, n in tiles_of(n_tok):
                        load_h(s, pos, n)
                        if 'a' in mixers:
                            P.op('pool', lambda e, pos=pos, n=n, yaV=yaV: e.dma_start(out=ya[:, :, :n], in_=yaV[:, :, pos:pos + n]),
                                 reads=[('yaT', s)], writes=['ya'], dma=True)
                        else:
                            P.op('dve', lambda e: e.memset(ya[:], 0.0), writes=['ya'])
                        if 'b' in mixers:
                            P.op('pool', lambda e, pos=pos, n=n, ybV=ybV: e.dma_start(out=yb[:, :, :n], in_=ybV[:, :, pos:pos + n]),
                                 reads=[('ybT', s)], writes=['yb'], dma=True)
                        else:
                            P.op('dve', lambda e: e.memset(yb[:], 0.0), writes=['yb'])
                        if 'c' in mixers:
                            P.op('pool', lambda e, pos=pos, n=n, ycV=ycV: e.dma_start(out=yc[:, :, :n], in_=ycV[:, :, pos:pos + n]),
                                 reads=[('ycT', s)], writes=['yc'], dma=True)
                        else:
                            P.op('dve', lambda e: e.memset(yc[:], 1.0), writes=['yc'])
                        rmsnorm_to_xn(n)
                        P.op('act', lambda e, n=n: e.activation(out=ycs[:, :, :n], in_=yc[:, :, :n], func=AF.Square),
                             reads=['yc'], writes=['ycs'])
                        for t in range(2):
                            ps, pk = next_ps()
                            P.op('pe', lambda e, t=t, n=n, ps=ps: e.matmul(ps[:, :n], lhsT=blk[:], rhs=ycs[:, t, :n],
                                                                        start=True, stop=True),
                                 reads=['ycs', 'blk'], writes=[pk])
                            rsqrt_ps(rc, 'rc', ps, pk, n)
                            P.op('dve', lambda e, t=t, n=n: e.scalar_tensor_tensor(
                                out=yc[:, t, :n], in0=yc[:, t, :n], scalar=onorms[:, l:l + 1], in1=rc[:, :n],
                                op0=ALU.mult, op1=ALU.mult), reads=['yc', 'rc', 'onorms'], writes=['yc'])
                            ps2, pk2 = next_ps()
                            for k in range(8):
                                P.op('pe', lambda e, k=k, t=t, n=n, ps2=ps2: e.matmul(
                                    ps2[:, :n], lhsT=WC[:, k, t * 128:(t + 1) * 128], rhs=xn[:, k, :n],
                                    start=(k == 0), stop=(k == 7)), reads=['WC', 'xn'], writes=[pk2])
                            P.op('act', lambda e, n=n, ps2=ps2: e.activation(out=m1[:, :n], in_=ps2[:, :n], func=AF.Silu),
                                 reads=[pk2], writes=['m1'])
                            P.op('dve', lambda e, t=t, n=n: e.tensor_tensor(out=ycn[:, t, :n], in0=yc[:, t, :n],
                                                                           in1=m1[:, :n], op=ALU.mult),
                                 reads=['yc', 'm1'], writes=['ycn'])
                        for oc in range(8):
                            pss = []
                            for (W, src_, nk, key) in [(WUA, ya, 2, 'ya'), (WUB, yb, 4, 'yb'), (WUC, ycn, 2, 'ycn')]:
                                ps, pk = next_ps()
                                for k in range(nk):
                                    P.op('pe', lambda e, k=k, n=n, ps=ps, W=W, src_=src_, nk=nk: e.matmul(
                                        ps[:, :n], lhsT=W[:, k, oc * 128:(oc + 1) * 128], rhs=src_[:, k, :n],
                                        start=(k == 0), stop=(k == nk - 1)), reads=[key, 'WUA', 'WUB', 'WUC'], writes=[pk])
                                pss.append((ps, pk))
                            for gi in range(3):
                                ps, pk = next_ps()
                                c0 = 256 + gi * 1024 + oc * 128
                                for k in range(8):
                                    P.op('pe', lambda e, k=k, n=n, ps=ps, c0=c0: e.matmul(
                                        ps[:, :n], lhsT=WC[:, k, c0:c0 + 128], rhs=xn[:, k, :n],
                                        start=(k == 0), stop=(k == 7)), reads=['WC', 'xn'], writes=[pk])
                                P.op('act', lambda e, gi=gi, n=n, ps=ps: e.activation(out=sg[:, gi, :n], in_=ps[:, :n],
                                                                                  func=AF.Sigmoid),
                                     reads=[pk], writes=[('sg', gi)])
                            P.op('dve', lambda e, n=n, p0=pss[0][0]: e.tensor_tensor(out=m1[:, :n], in0=p0[:, :n], in1=sg[:, 0, :n], op=ALU.mult),
                                 reads=[pss[0][1], ('sg', 0)], writes=['m1'])
                            P.op('dve', lambda e, n=n, p1=pss[1][0]: e.tensor_tensor(out=m2[:, :n], in0=p1[:, :n], in1=sg[:, 1, :n], op=ALU.mult),
                                 reads=[pss[1][1], ('sg', 1)], writes=['m2'])
                            P.op('dve', lambda e, n=n: e.tensor_tensor(out=m1[:, :n], in0=m1[:, :n], in1=m2[:, :n], op=ALU.add),
                                 reads=['m1', 'm2'], writes=['m1'])
                            P.op('dve', lambda e, n=n, p2=pss[2][0]: e.tensor_tensor(out=m2[:, :n], in0=p2[:, :n], in1=sg[:, 2, :n], op=ALU.mult),
                                 reads=[pss[2][1], ('sg', 2)], writes=['m2'])
                            P.op('dve', lambda e, n=n, oc=oc: e.tensor_tensor(out=mix[:, oc, :n], in0=m1[:, :n], in1=m2[:, :n], op=ALU.add),
                                 reads=['m1', 'm2'], writes=['mix'])
                        for oc in range(8):
                            ps, pk = next_ps()
                            for k in range(8):
                                P.op('pe', lambda e, k=k, n=n, ps=ps, oc=oc: e.matmul(
                                    ps[:, :n], lhsT=WO[:, k, oc * 128:(oc + 1) * 128], rhs=mix[:, k, :n],
                                    start=(k == 0), stop=(k == 7)), reads=['WO', 'mix'], writes=[pk])
                            P.op('dve', lambda e, n=n, ps=ps, oc=oc: e.tensor_tensor(
                                out=hbuf[:, oc, :n], in0=hbuf[:, oc, :n], in1=ps[:, :n], op=ALU.add),
                                reads=[pk, 'hbuf'], writes=['hbuf'])
                        store_h(s, pos, n)
                P.barrier()
            P.barrier()
            with ExitStack() as ph:
                def pb_(name, shape, dt=F32):
                    return ph.enter_context(nc.sbuf_tensor(f"{name}_{l}", list(shape), dt))
                WG = pb_("WG", [128, 8, FF], BF16)
                WU = pb_("WU", [128, 8, FF], BF16)
                WD = pb_("WD", [128, 22, D], BF16)
                act = pb_("act", [128, 22, TT], BF16)
                sgt = pb_("sgt", [128, TT])
                for k in range(8):
                    load_w(WG[:, k, :], 'WG', w_fg[l, k * 128:(k + 1) * 128, :], FF, scale=g2s[:, l, k:k + 1])
                    load_w(WU[:, k, :], 'WU', w_fu[l, k * 128:(k + 1) * 128, :], FF, scale=g2s[:, l, k:k + 1])
                for k in range(22):
                    load_w(WD[:, k, :], 'WD', w_fd[l, k * 128:(k + 1) * 128, :], D)
                for s, n_tok in seqs:
                    for pos, n in tiles_of(n_tok):
                        load_h(s, pos, n)
                        rmsnorm_to_xn(n)
                        for fc in range(22):
                            psg, pkg = next_ps()
                            psu, pku = next_ps()
                            for (W, ps, pk, key) in [(WG, psg, pkg, 'WG'), (WU, psu, pku, 'WU')]:
                                for k in range(8):
                                    P.op('pe', lambda e, k=k, n=n, ps=ps, W=W, fc=fc: e.matmul(
                                        ps[:, :n], lhsT=W[:, k, fc * 128:(fc + 1) * 128], rhs=xn[:, k, :n],
                                        start=(k == 0), stop=(k == 7)), reads=[key, 'xn'], writes=[pk])
                            P.op('act', lambda e, n=n, psg=psg: e.activation(out=sgt[:, :n], in_=psg[:, :n], func=AF.Silu),
                                 reads=[pkg], writes=['sgt'])
                            P.op('dve', lambda e, n=n, psu=psu, fc=fc: e.tensor_tensor(
                                out=act[:, fc, :n], in0=psu[:, :n], in1=sgt[:, :n], op=ALU.mult),
                                reads=[pku, 'sgt'], writes=['act'])
                        for oc in range(8):
                            ps, pk = next_ps()
                            for k in range(22):
                                P.op('pe', lambda e, k=k, n=n, ps=ps, oc=oc: e.matmul(
                                    ps[:, :n], lhsT=WD[:, k, oc * 128:(oc + 1) * 128], rhs=act[:, k, :n],
                                    start=(k == 0), stop=(k == 21)), reads=['WD', 'act'], writes=[pk])
                            P.op('dve', lambda e, n=n, ps=ps, oc=oc: e.tensor_tensor(
                                out=hbuf[:, oc, :n], in0=hbuf[:, oc, :n], in1=ps[:, :n], op=ALU.add),
                                reads=[pk, 'hbuf'], writes=['hbuf'])
                        store_h(s, pos, n)
                P.barrier()
            P.barrier()

        with ExitStack() as ph:
            xo = ph.enter_context(nc.sbuf_tensor("xo", [128, 8, TT], F32))
            ob = ph.enter_context(nc.sbuf_tensor("ob", [128, D], F32))
            for s, n_tok in seqs:
                for pos, n in tiles_of(n_tok)[1:]:
                    load_h(s, pos, n)
                    P.op('act', lambda e, n=n: e.activation(out=sqb[:, :, :n], in_=hbuf[:, :, :n], func=AF.Square),
                         reads=['hbuf'], writes=['sqb'])
                    ps, pk = next_ps()
                    for k in range(8):
                        P.op('pe', lambda e, k=k, n=n, ps=ps: e.matmul(ps[:, :n], lhsT=ones[:], rhs=sqb[:, k, :n],
                                                                    start=(k == 0), stop=(k == 7)),
                             reads=['sqb', 'ones'], writes=[pk])
                    rsqrt_ps(rstd, 'rstd', ps, pk, n)
                    for k in range(8):
                        P.op('dve', lambda e, k=k, n=n: e.scalar_tensor_tensor(
                            out=xo[:, k, :n], in0=hbuf[:, k, :n], scalar=gfs[:, k:k + 1], in1=rstd[:, :n],
                            op0=ALU.mult, op1=ALU.mult), reads=['hbuf', 'rstd', 'gfs'], writes=['xo'])
                    for tb in range(n // 128):
                        pb, pks = next_ps_big()
                        for k in range(8):
                            P.op('pe', lambda e, k=k, tb=tb, pb=pb: e.transpose(
                                out=pb[:, k * 128:(k + 1) * 128], in_=xo[:, k, tb * 128:(tb + 1) * 128],
                                identity=ident[:]), reads=['xo', 'ident'], writes=pks)
                        P.op('act', lambda e, pb=pb: e.activation(out=ob[:], in_=pb[:], func=AF.Copy),
                             reads=pks, writes=['ob'])
                        r0 = pos - NM + tb * 128
                        P.op('sp', lambda e, r0=r0, s=s: e.dma_start(out=y_out[s][r0:r0 + 128, :], in_=ob[:]),
                             reads=['ob'], writes=[('y', s, r0)], dma=True)
        P.barrier()
        P.emit(sems)
    return nc


def host_inputs(inputs, depth, core):
    f = lambda a: np.ascontiguousarray(np.asarray(a, dtype=np.float32))
    d = {}
    d["x_p"] = f(inputs["x_prompt"][core])
    d["x_s"] = f(inputs["x_sample"][core // 4])
    d["meta"] = f(inputs["meta_tokens"])
    for k in ["w_in", "w_up_a", "w_up_b", "w_up_c", "w_o", "w_ffn_gate", "w_ffn_up", "w_ffn_down"]:
        d[k] = f(inputs[k][:depth])
    d["g1"] = f(np.asarray(inputs["norm1_g"])[:depth].reshape(depth, 8, 128).transpose(2, 0, 1))
    d["g2"] = f(np.asarray(inputs["norm2_g"])[:depth].reshape(depth, 8, 128).transpose(2, 0, 1))
    d["gf"] = f(np.asarray(inputs["final_norm_g"]).reshape(8, 128).T)
    d["onorm"] = f(np.tile(np.asarray(inputs["hg_onorm_g"])[:depth], (1, 2)).T)
    d["c_ident"] = np.eye(128, dtype=np.float32)
    d["c_ones"] = np.full((128, 128), 1.0 / 1024, np.float32)
    b = np.zeros((128, 128), np.float32)
    b[:64, :64] = 1.0 / 64
    b[64:, 64:] = 1.0 / 64
    d["c_blk"] = b
    jj, ii = np.meshgrid(np.arange(128), np.arange(128), indexing='ij')
    same = (jj // 32) == (ii // 32)
    d["c_maskf"] = (same & (jj <= ii)).astype(np.float32)
    d["c_maskb"] = (same & (jj >= ii)).astype(np.float32)
    d["c_cmask"] = (np.arange(128)[:, None] // 32 == np.arange(4)[None, :]).astype(np.float32)
    qc = np.arange(64); kc = np.arange(64)
    ws = np.clip(qc - 8, 0, 48)
    cm = (kc[:, None] >= ws[None, :]) & (kc[:, None] < ws[None, :] + 16)
    d["c_negm"] = np.tile(np.where(cm, 0.0, NEGV).astype(np.float32), (2, 2))
    rp = np.zeros((depth, 8, 16, 127), np.float32)
    rp[:, :, :15, 48:79] = np.asarray(inputs["na_rpb"])[:depth]
    rp[:, :, 15, :] = NEGV
    d["rpbpad"] = rp
    d["c_iota"] = np.tile(np.arange(1, TS + 1, dtype=np.float32)[None, :], (128, 1))
    ar = np.asarray(inputs["s5_a_re"])[:depth]; ai = np.asarray(inputs["s5_a_im"])[:depth]
    ld = np.repeat(np.asarray(inputs["s5_log_dt"])[:depth][..., None], 64, axis=-1)
    flat = np.stack([ar, ai, ld], axis=2).reshape(depth, 2, 3, 1024)
    d["s5sp"] = f(flat.reshape(depth, 2, 3, 8, 128).transpose(4, 0, 1, 2, 3))
    d["s5rep"] = f(np.broadcast_to(flat[None], (128, depth, 2, 3, 1024)))
    Bb = np.zeros((depth, 2, 2, 128, 8, 128), np.float32)
    Cb = np.zeros((depth, 2, 2, 128, 8, 128), np.float32)
    for ri, (bsrc, csrc) in enumerate([(inputs["s5_b_re"], inputs["s5_c_re"]), (inputs["s5_b_im"], inputs["s5_c_im"])]):
        bsrc = np.asarray(bsrc)[:depth]; csrc = np.asarray(csrc)[:depth]
        for g in range(16):
            j = g // 2
            r0 = 32 * (j % 4) + 16 * (g % 2)
            s0 = 64 * (g % 2)
            c0 = 16 * (g % 8)
            Bb[:, :, ri, r0:r0 + 16, j, s0:s0 + 64] = bsrc[:, :, g].transpose(0, 1, 3, 2)
            Cb[:, :, ri, s0:s0 + 64, j, c0:c0 + 16] = csrc[:, :, g].transpose(0, 1, 3, 2)
    d["s5B"] = Bb
    d["s5C"] = Cb
    d["s5d"] = f(np.asarray(inputs["s5_d"])[:depth].reshape(depth, 2, 128).transpose(2, 0, 1))
    d["w_glu"] = f(inputs["s5_w_glu"][:depth])
    d["lbl"] = f(np.asarray(inputs["hg_lb_logits"]).T.reshape(4, 64, 4).transpose(1, 0, 2))
    return d


def run(inputs, nP, nS, depth, n_cores=8, **kw):
    nc = build(nP, nS, depth, **kw)
    in_maps = [host_inputs(inputs, depth, c) for c in range(n_cores)]
    res = run_bass_kernel_spmd(nc, in_maps, core_ids=list(range(n_cores)))
    yp = np.stack([res.results[c]["y_p"] for c in range(n_cores)], 0)
    ys = np.stack([res.results[c]["y_s"] for c in range(0, n_cores, 4)], 0)
    return yp.astype(np.float32), ys.astype(np.float32)


def kernel(**inputs):
    return run(inputs, 4096, 16384, 4)
```

```python
import numpy as np
from contextlib import ExitStack
import concourse.bass as bass
import concourse.mybir as mybir
from concourse.bass_utils import run_bass_kernel_spmd

F32 = mybir.dt.float32
BF16 = mybir.dt.bfloat16
AF = mybir.ActivationFunctionType
ALU = mybir.AluOpType

D = 1024
NM = 16
EPS = 1e-6
FF = 2816
TT = 256
TS = 256
NEGV = -30000.0


class _Rec:
    def __getattr__(self, name):
        def f(*a, **k):
            self.call = (name, a, k)
            return self
        return f


NDS = 8


class Prog:
    def __init__(self, nc):
        self.nc = nc
        self.streams = {k: [] for k in ['sp', 'act', 'dve', 'pool', 'pe']}
        self.cnt = {k: 0 for k in ['act', 'dve', 'pool', 'pe']}
        for q in ('sp', 'pooldma'):
            for i in range(NDS):
                self.cnt[f"{q}{i}"] = 0
        self.dma_n = {'sp': 0, 'pooldma': 0}
        self.known = {k: {} for k in self.streams}
        self.lastw = {}
        self.readers = {}
        self.alias = {}

    def op(self, stream, fn, reads=(), writes=(), dma=False):
        reads = [self.alias.get(k, k) if isinstance(k, str) else k for k in reads]
        writes = [self.alias.get(k, k) if isinstance(k, str) else k for k in writes]
        deps = {}

        def need(w):
            if w[1] > deps.get(w[0], 0):
                deps[w[0]] = w[1]
        if dma:
            q = 'sp' if stream == 'sp' else 'pooldma'
            i = self.dma_n[q]
            self.dma_n[q] += 1
            semname = f"{q}{i % NDS}"
            if self.cnt[semname] > 0:
                need((semname, self.cnt[semname]))
        else:
            assert stream != 'sp'
            semname = stream
        for k in reads:
            w = self.lastw.get(k)
            if w:
                need(w)
        for k in writes:
            w = self.lastw.get(k)
            if w:
                need(w)
            for sem, c in self.readers.get(k, {}).items():
                need((sem, c))
        waits = []
        for sem, c in deps.items():
            if sem == 'pe' and stream == 'pe':
                continue
            if self.known[stream].get(sem, 0) >= c:
                continue
            self.known[stream][sem] = c
            waits.append((sem, c))
        self.cnt[semname] += 1
        my = self.cnt[semname]
        rec = _Rec()
        fn(rec)
        self.streams[stream].append((waits, rec.call, semname))
        for k in writes:
            self.lastw[k] = (semname, my)
            self.readers[k] = {}
        for k in reads:
            self.readers.setdefault(k, {})[semname] = my

    def barrier(self):
        for st in self.streams:
            waits = []
            for sem, c in self.cnt.items():
                if c > self.known[st].get(sem, 0):
                    self.known[st][sem] = c
                    waits.append((sem, c))
            if waits:
                self.streams[st].append((waits, None, None))

    def emit(self, sems):
        mult = {k: (1 if k in ('act', 'dve', 'pool', 'pe') else 16) for k in self.cnt}
        nc = self.nc
        with nc.Block() as block:
            def run(e, items):
                for waits, fn, semname in items:
                    for sem, c in waits:
                        e.wait_ge(sems[sem], c * mult[sem])
                    if fn is not None:
                        name, a, k = fn
                        getattr(e, name)(*a, **k).then_inc(sems[semname], mult[semname])

            @block.sync
            def _(e):
                run(e, self.streams['sp'])

            @block.scalar
            def _(e):
                run(e, self.streams['act'])

            @block.vector
            def _(e):
                run(e, self.streams['dve'])

            @block.gpsimd
            def _(e):
                run(e, self.streams['pool'])

            @block.tensor
            def _(e):
                run(e, self.streams['pe'])


def tiles_of(n_tok):
    t = [(0, NM)]
    for i in range(n_tok // TT):
        t.append((NM + i * TT, TT))
    return t


def build(nP, nS, depth, mixers=('a', 'b', 'c'), hg_stop=9):
    nc = bass.Bass("TRN2", target_bir_lowering=False)
    P = Prog(nc)
    seqs = [('p', nP), ('s', nS)]

    def din(name, shape, dt=F32):
        return nc.dram_tensor(name, list(shape), dt, kind="ExternalInput").ap()

    x_in = {'p': din("x_p", [nP, D]), 's': din("x_s", [nS, D])}
    meta_in = din("meta", [NM, D])
    y_out = {'p': nc.dram_tensor("y_p", [nP, D], F32, kind="ExternalOutput").ap(),
             's': nc.dram_tensor("y_s", [nS, D], F32, kind="ExternalOutput").ap()}
    w_in = din("w_in", [depth, D, 6144])
    w_up_a = din("w_up_a", [depth, 256, D])
    w_up_b = din("w_up_b", [depth, 512, D])
    w_up_c = din("w_up_c", [depth, 256, D])
    w_o = din("w_o", [depth, D, D])
    w_fg = din("w_ffn_gate", [depth, D, FF])
    w_fu = din("w_ffn_up", [depth, D, FF])
    w_fd = din("w_ffn_down", [depth, FF, D])
    g1 = din("g1", [128, depth, 8])
    g2 = din("g2", [128, depth, 8])
    gf = din("gf", [128, 8])
    onorm = din("onorm", [128, depth])
    c_ident = din("c_ident", [128, 128])
    c_ones = din("c_ones", [128, 128])
    c_blk = din("c_blk", [128, 128])
    c_maskf = din("c_maskf", [128, 128])
    c_maskb = din("c_maskb", [128, 128])
    lbl = din("lbl", [64, 4, 4])
    c_cmask = din("c_cmask", [128, 4])
    c_iota = din("c_iota", [128, TS])
    c_negm = din("c_negm", [128, 128])
    rpbpad = din("rpbpad", [depth, 8, 16, 127])
    s5sp = din("s5sp", [128, depth, 2, 3, 8])
    s5rep = din("s5rep", [128, depth, 2, 3, 1024])
    s5B = din("s5B", [depth, 2, 2, 128, 8, 128])
    s5C = din("s5C", [depth, 2, 2, 128, 8, 128])
    s5d = din("s5d", [128, depth, 2])
    w_glu = din("w_glu", [depth, 256, 256])

    scr = {}
    for s, n in seqs:
        L = NM + n
        scr[s] = dict(
            hT=nc.dram_tensor(f"hT_{s}", [D, L], F32, kind="Internal").ap(),
            yaT=nc.dram_tensor(f"yaT_{s}", [256, L], BF16, kind="Internal").ap(),
            ybT=nc.dram_tensor(f"ybT_{s}", [512, L], BF16, kind="Internal").ap(),
            ycT=nc.dram_tensor(f"ycT_{s}", [256, L], F32, kind="Internal").ap(),
            uT=nc.dram_tensor(f"uT_{s}", [256, L], F32, kind="Internal").ap(),
            ysT=nc.dram_tensor(f"ysT_{s}", [256, L], F32, kind="Internal").ap(),
            qkT=nc.dram_tensor(f"qkT_{s}", [1024, L], BF16, kind="Internal").ap(),
            hgT=nc.dram_tensor(f"hgT_{s}", [768, L], F32, kind="Internal").ap(),
            vtok=nc.dram_tensor(f"vtok_{s}", [L, 512], BF16, kind="Internal").ap(),
            ictok=nc.dram_tensor(f"ictok_{s}", [L, 256], BF16, kind="Internal").ap(),
        )

    es = ExitStack()
    with es:
        def sb(name, shape, dt=F32):
            return es.enter_context(nc.sbuf_tensor(name, list(shape), dt))
        sems = {k: es.enter_context(nc.semaphore(k)) for k in P.cnt}
        ps_t = [es.enter_context(nc.psum_tensor(f"ps{i}", [128, 1024], F32)) for i in range(4)]
        ps_i = [0]

        def next_ps():
            i = ps_i[0]
            ps_i[0] = (i + 1) % 8
            return ps_t[i // 2][:, (i % 2) * 512:(i % 2) * 512 + 512], ('ps', i)

        def next_ps_big():
            i = (ps_i[0] + 1) // 2 * 2 % 8
            ps_i[0] = (i + 2) % 8
            return ps_t[i // 2], [('ps', i), ('ps', i + 1)]

        ident = sb("ident", [128, 128])
        ones = sb("ones", [128, 128])
        blk = sb("blk", [128, 128])
        g1s = sb("g1s", [128, depth, 8])
        g2s = sb("g2s", [128, depth, 8])
        gfs = sb("gfs", [128, 8])
        onorms = sb("onorms", [128, depth])
        epsT = sb("epsT", [128, 1])
        for dst, src, k in [(ident, c_ident, 'ident'), (ones, c_ones, 'ones'), (blk, c_blk, 'blk'),
                            (g1s, g1, 'g1s'), (g2s, g2, 'g2s'), (gfs, gf, 'gfs'), (onorms, onorm, 'onorms')]:
            P.op('sp', lambda e, d=dst, s_=src: e.dma_start(out=d[:], in_=s_), writes=[k], dma=True)
        P.op('dve', lambda e: e.memset(epsT[:], EPS), writes=['epsT'])
        maskf = sb("maskf", [128, 128])
        maskb = sb("maskb", [128, 128])
        lbe = sb("lbe", [64, 4, 4])
        lbs = sb("lbs", [64, 4, 4])
        oml = sb("oml", [64, 4, 4])
        noml = sb("noml", [64, 4, 4])
        lsum = sb("lsum", [64, 4, 1])
        onesT = sb("onesT", [128, TS])
        cmask = sb("cmask", [128, 4])
        iota1 = sb("iota1", [128, TS])
        s5ds = sb("s5ds", [128, depth, 2])
        P.op('sp', lambda e: e.dma_start(out=iota1[:], in_=c_iota), writes=['iota1'], dma=True)
        P.op('sp', lambda e: e.dma_start(out=s5ds[:], in_=s5d), writes=['s5ds'], dma=True)
        P.op('sp', lambda e: e.dma_start(out=cmask[:], in_=c_cmask), writes=['cmask'], dma=True)
        P.op('sp', lambda e: e.dma_start(out=maskf[:], in_=c_maskf), writes=['maskf'], dma=True)
        P.op('sp', lambda e: e.dma_start(out=maskb[:], in_=c_maskb), writes=['maskb'], dma=True)
        P.op('sp', lambda e: e.dma_start(out=lbe[:], in_=lbl), writes=['lbe'], dma=True)
        P.op('dve', lambda e: e.memset(onesT[:], 1.0), writes=['onesT'])
        P.op('act', lambda e: e.activation(out=lbe[:], in_=lbe[:], func=AF.Exp), reads=['lbe'], writes=['lbe'])
        P.op('dve', lambda e: e.tensor_tensor(out=lsum[:], in0=lbe[:, :, 0:1], in1=lbe[:, :, 1:2], op=ALU.add), reads=['lbe'], writes=['lsum'])
        P.op('dve', lambda e: e.tensor_tensor(out=lsum[:], in0=lsum[:], in1=lbe[:, :, 2:3], op=ALU.add), reads=['lbe', 'lsum'], writes=['lsum'])
        P.op('dve', lambda e: e.tensor_tensor(out=lsum[:], in0=lsum[:], in1=lbe[:, :, 3:4], op=ALU.add), reads=['lbe', 'lsum'], writes=['lsum'])
        P.op('dve', lambda e: e.reciprocal(out=lsum[:], in_=lsum[:]), reads=['lsum'], writes=['lsum'])
        P.op('dve', lambda e: e.memset(lbs[:, :, 0:1], 0.0), writes=['lbs'])
        P.op('dve', lambda e: e.tensor_copy(out=lbs[:, :, 1:2], in_=lbe[:, :, 1:2]), reads=['lbe'], writes=['lbs'])
        P.op('dve', lambda e: e.tensor_tensor(out=lbs[:, :, 2:3], in0=lbs[:, :, 1:2], in1=lbe[:, :, 2:3], op=ALU.add), reads=['lbe', 'lbs'], writes=['lbs'])
        P.op('dve', lambda e: e.tensor_tensor(out=lbs[:, :, 3:4], in0=lbs[:, :, 2:3], in1=lbe[:, :, 3:4], op=ALU.add), reads=['lbe', 'lbs'], writes=['lbs'])
        for t_ in range(4):
            P.op('dve', lambda e, t_=t_: e.tensor_scalar(out=lbs[:, t_, :], in0=lbs[:, t_, :], scalar1=lsum[:, t_, 0:1], scalar2=None, op0=ALU.mult),
                 reads=['lbs', 'lsum'], writes=['lbs'])
        P.op('dve', lambda e: e.tensor_scalar(out=oml[:], in0=lbs[:], scalar1=-1.0, scalar2=1.0, op0=ALU.mult, op1=ALU.add), reads=['lbs'], writes=['oml'])
        P.op('dve', lambda e: e.tensor_scalar(out=noml[:], in0=oml[:], scalar1=-1.0, scalar2=None, op0=ALU.mult), reads=['oml'], writes=['noml'])

        hbuf = sb("hbuf", [128, 8, TT])
        sqb = sb("sqb", [128, 8, TT])
        rstd = sb("rstd", [128, TT])
        xn = sb("xn", [128, 8, TT], BF16)
        stage = sb("stage", [128, 3328])

        hTv = {s: scr[s]['hT'].rearrange("(k p) l -> p k l", p=128) for s, _ in seqs}

        def load_h(s, pos, n):
            P.op('sp', lambda e: e.dma_start(out=hbuf[:, :, :n], in_=hTv[s][:, :, pos:pos + n]),
                 reads=[('hT', s, pos)], writes=['hbuf'], dma=True)

        def store_h(s, pos, n):
            P.op('sp', lambda e: e.dma_start(out=hTv[s][:, :, pos:pos + n], in_=hbuf[:, :, :n]),
                 reads=['hbuf'], writes=[('hT', s, pos)], dma=True)

        sqt = sb("sqt", [128, TT])

        def rsqrt_ps(dst, dkey, ps, pk, n):
            P.op('act', lambda e: e.activation(out=sqt[:, :n], in_=ps[:, :n], func=AF.Sqrt, bias=epsT[:, 0:1], scale=1.0),
                 reads=[pk, 'epsT'], writes=['sqt'])
            P.op('dve', lambda e: e.reciprocal(out=dst[:, :n], in_=sqt[:, :n]), reads=['sqt'], writes=[dkey])

        def rmsnorm_to_xn(n):
            P.op('act', lambda e: e.activation(out=sqb[:, :, :n], in_=hbuf[:, :, :n], func=AF.Square),
                 reads=['hbuf'], writes=['sqb'])
            ps, pk = next_ps()
            for k in range(8):
                P.op('pe', lambda e, k=k: e.matmul(ps[:, :n], lhsT=ones[:], rhs=sqb[:, k, :n],
                                                  start=(k == 0), stop=(k == 7)),
                     reads=['sqb', 'ones'], writes=[pk])
            rsqrt_ps(rstd, 'rstd', ps, pk, n)
            for k in range(8):
                P.op('dve', lambda e, k=k: e.tensor_tensor(out=xn[:, k, :n], in0=hbuf[:, k, :n],
                                                           in1=rstd[:, :n], op=ALU.mult),
                     reads=['hbuf', 'rstd'], writes=['xn'])

        cvt_i = [0]

        def load_w(dst, dkey, src, ncols, scale=None):
            c0 = 0
            while c0 < ncols:
                c1 = min(ncols, c0 + 1664)
                half = cvt_i[0] % 2
                cvt_i[0] += 1
                st = stage[:, half * 1664: half * 1664 + (c1 - c0)]
                sk = ('stage', half)
                P.op('sp', lambda e, st=st, c0=c0, c1=c1: e.dma_start(out=st, in_=src[:, c0:c1]),
                     writes=[sk], dma=True)
                if scale is None:
                    P.op('pool', lambda e, st=st, c0=c0, c1=c1: e.tensor_copy(out=dst[:, c0:c1], in_=st),
                         reads=[sk], writes=[dkey])
                else:
                    P.op('act', lambda e, st=st, c0=c0, c1=c1: e.activation(out=dst[:, c0:c1], in_=st,
                                                                           func=AF.Copy, scale=scale),
                         reads=[sk], writes=[dkey])
                c0 = c1

        with ExitStack() as ph:
            xin = ph.enter_context(nc.sbuf_tensor("xin", [128, D], F32))
            for s, n_tok in seqs:
                blocks = [(meta_in, 0, NM, 0)] + [(x_in[s], b * 128, 128, NM + b * 128) for b in range(n_tok // 128)]
                for src, r0, nr, pos in blocks:
                    P.op('sp', lambda e, src=src, r0=r0, nr=nr: e.dma_start(out=xin[:nr, :], in_=src[r0:r0 + nr, :]),
                         writes=['xin'], dma=True)
                    pb, pks = next_ps_big()
                    for k in range(8):
                        P.op('pe', lambda e, k=k, nr=nr, pb=pb: e.transpose(
                            out=pb[:, k * 128:k * 128 + nr], in_=xin[:nr, k * 128:(k + 1) * 128],
                            identity=ident[:nr, :nr]), reads=['xin', 'ident'], writes=pks)
                    P.op('dve', lambda e, nr=nr, pb=pb: e.tensor_copy(
                        out=hbuf[:, :, :nr], in_=pb.rearrange("p (k t) -> p k t", k=8)[:, :, :nr]),
                        reads=pks, writes=['hbuf'])
                    store_h(s, pos, nr)
        P.barrier()

        for l in range(depth):

            if mixers:
                with ExitStack() as ph:
                    def pb_(name, shape, dt=F32):
                        return ph.enter_context(nc.sbuf_tensor(f"{name}_{l}", list(shape), dt))
                    WA = pb_("WA", [128, 8, 2816], BF16)
                    stu = pb_("stu", [128, 2, TT])
                    stqk = pb_("stqk", [128, 8, TT], BF16)
                    sthg = pb_("sthg", [128, 6, TT])
                    stv = pb_("stv", [128, 768], BF16)
                    for k in range(8):
                        load_w(WA[:, k, :], 'WA', w_in[l, k * 128:(k + 1) * 128, 0:2816], 2816, scale=g1s[:, l, k:k + 1])
                    for s, n_tok in seqs:
                        uV = scr[s]['uT'].rearrange("(k p) l -> p k l", p=128)
                        qkV = scr[s]['qkT'].rearrange("(k p) l -> p k l", p=128)
                        hgV = scr[s]['hgT'].rearrange("(k p) l -> p k l", p=128)
                        for pos, n in tiles_of(n_tok):
                            load_h(s, pos, n)
                            rmsnorm_to_xn(n)
                            fm = [(c, 'u', c) for c in (0, 1)] + [(2 + i, 'qk', i) for i in range(8)] + [(14 + i, 'hg', i) for i in range(6)]
                            for cc, kind, idx in fm:
                                ps, pk = next_ps()
                                for k in range(8):
                                    P.op('pe', lambda e, k=k: e.matmul(ps[:, :n], lhsT=WA[:, k, cc * 128:(cc + 1) * 128], rhs=xn[:, k, :n],
                                                                      start=(k == 0), stop=(k == 7)), reads=['WA', 'xn'], writes=[pk])
                                if kind == 'u':
                                    P.op('act', lambda e: e.activation(out=stu[:, idx, :n], in_=ps[:, :n], func=AF.Copy), reads=[pk], writes=['stu'])
                                elif kind == 'qk':
                                    P.op('act', lambda e: e.activation(out=stqk[:, idx, :n], in_=ps[:, :n], func=AF.Copy,
                                                                       scale=(0.125 if idx < 4 else 1.0)), reads=[pk], writes=['stqk'])
                                else:
                                    P.op('dve', lambda e: e.tensor_copy(out=sthg[:, idx, :n], in_=ps[:, :n]), reads=[pk], writes=['sthg'])
                            P.op('sp', lambda e: e.dma_start(out=uV[:, :, pos:pos + n], in_=stu[:, :, :n]), reads=['stu'], writes=[('uT', s)], dma=True)
                            P.op('sp', lambda e: e.dma_start(out=qkV[:, :, pos:pos + n], in_=stqk[:, :, :n]), reads=['stqk'], writes=[('qkT', s)], dma=True)
                            P.op('sp', lambda e: e.dma_start(out=hgV[:, :, pos:pos + n], in_=sthg[:, :, :n]), reads=['sthg'], writes=[('hgT', s)], dma=True)
                            for tb in range((n + 127) // 128):
                                nt = min(128, n - tb * 128)
                                ps, pk = next_ps()
                                ps2, pk2 = next_ps()
                                for k in range(8):
                                    P.op('pe', lambda e, k=k: e.matmul(ps[:nt, 0:512], lhsT=xn[:, k, tb * 128:tb * 128 + nt], rhs=WA[:, k, 1280:1792],
                                                                      start=(k == 0), stop=(k == 7)), reads=['WA', 'xn'], writes=[pk])
                                for k in range(8):
                                    P.op('pe', lambda e, k=k: e.matmul(ps2[:nt, 0:256], lhsT=xn[:, k, tb * 128:tb * 128 + nt], rhs=WA[:, k, 2560:2816],
                                                                      start=(k == 0), stop=(k == 7)), reads=['WA', 'xn'], writes=[pk2])
                                P.op('act', lambda e: e.activation(out=stv[:nt, 0:512], in_=ps[:nt, 0:512], func=AF.Copy), reads=[pk], writes=['stv'])
                                P.op('dve', lambda e: e.tensor_copy(out=stv[:nt, 512:768], in_=ps2[:nt, 0:256]), reads=[pk2], writes=['stv'])
                                r0 = pos + tb * 128
                                P.op('sp', lambda e: e.dma_start(out=scr[s]['vtok'][r0:r0 + nt, :], in_=stv[:nt, 0:512]), reads=['stv'], writes=[('vtok', s)], dma=True)
                                P.op('sp', lambda e: e.dma_start(out=scr[s]['ictok'][r0:r0 + nt, :], in_=stv[:nt, 512:768]), reads=['stv'], writes=[('ictok', s)], dma=True)
                    P.barrier()
                P.barrier()
                if 'a' in mixers:
                  with ExitStack() as ph:
                    def pb_(name, shape, dt=F32):
                        return ph.enter_context(nc.sbuf_tensor(f"{name}_{l}", list(shape), dt))
                    PI = 3.14159265358979
                    MAGIC = 12582912.0

                    def sin_of(dst, src, t1, t2, keys_r, key_w):
                        P.op('dve', lambda e: e.tensor_scalar(out=t1, in0=src, scalar1=1.0 / (2 * PI), scalar2=MAGIC, op0=ALU.mult, op1=ALU.add),
                             reads=keys_r, writes=['s5t1'])
                        P.op('dve', lambda e: e.tensor_scalar(out=t1, in0=t1, scalar1=MAGIC, scalar2=2 * PI, op0=ALU.subtract, op1=ALU.mult),
                             reads=['s5t1'], writes=['s5t1'])
                        P.op('dve', lambda e: e.tensor_tensor(out=t2, in0=src, in1=t1, op=ALU.subtract), reads=keys_r + ['s5t1'], writes=['s5t2'])
                        P.op('dve', lambda e: e.tensor_scalar(out=t2, in0=t2, scalar1=-3.141592, scalar2=3.141592, op0=ALU.max, op1=ALU.min),
                             reads=['s5t2'], writes=['s5t2'])
                        P.op('act', lambda e: e.activation(out=dst, in_=t2, func=AF.Sin), reads=['s5t2'], writes=[key_w])

                    rT = pb_("s5rT", [128, 2, 8, TS])
                    sinT = pb_("s5sin", [128, 2, 8, TS])
                    cosT = pb_("s5cos", [128, 2, 8, TS])
                    BT = pb_("s5BT", [128, 2, 2, 8, 128], BF16)
                    CT = pb_("s5CT", [128, 2, 2, 8, 128], BF16)
                    WGL = pb_("s5WGL", [128, 2, 256], BF16)
                    with ExitStack() as ph2:
                        def pc_(name, shape, dt=F32):
                            return ph2.enter_context(nc.sbuf_tensor(f"{name}_{l}", list(shape), dt))
                        spt = pc_("s5spt", [128, 2, 3, 8])
                        sp2 = pc_("s5sp2", [128, 4, 8])
                        Rp = pc_("s5Rp", [128, 3, 1024])
                        T = [pc_(f"s5T{i}", [128, 1024]) for i in range(8)]
                        Bst = pc_("s5Bst", [128, 2, 8, 128])
                        t1 = T[6]
                        t2 = T[7]
                        for k in range(2):
                            load_w(WGL[:, k, :], 'WGL', w_glu[l, k * 128:(k + 1) * 128, :], 256)
                        P.op('sp', lambda e: e.dma_start(out=spt[:], in_=s5sp[:, l]), writes=['s5spt'], dma=True)
                        for d_ in range(2):
                            P.op('act', lambda e: e.activation(out=sp2[:, 0, :], in_=spt[:, d_, 2, :], func=AF.Exp), reads=['s5spt'], writes=['s5sp2'])
                            P.op('dve', lambda e: e.tensor_tensor(out=sp2[:, 1, :], in0=spt[:, d_, 0, :], in1=sp2[:, 0, :], op=ALU.mult), reads=['s5spt', 's5sp2'], writes=['s5sp2'])
                            P.op('act', lambda e: e.activation(out=sp2[:, 2, :], in_=sp2[:, 1, :], func=AF.Exp), reads=['s5sp2'], writes=['s5sp2'])
                            P.op('dve', lambda e: e.tensor_tensor(out=sp2[:, 3, :], in0=spt[:, d_, 1, :], in1=sp2[:, 0, :], op=ALU.mult), reads=['s5spt', 's5sp2'], writes=['s5sp2'])
                            for j in range(8):
                                P.op('dve', lambda e: e.tensor_scalar(out=rT[:, d_, j, :], in0=onesT[:], scalar1=sp2[:, 2, j:j + 1], scalar2=None, op0=ALU.mult),
                                     reads=['onesT', 's5sp2'], writes=['s5rT'])
                            JH = 1024 // TS
                            for hf in range(8 // JH):
                                for jq in range(JH):
                                    j = hf * JH + jq
                                    P.op('dve', lambda e: e.tensor_scalar(out=T[0][:, jq * TS:(jq + 1) * TS], in0=iota1[:], scalar1=sp2[:, 3, j:j + 1], scalar2=None, op0=ALU.mult),
                                         reads=['iota1', 's5sp2'], writes=['s5T0'])
                                sin_of(sinT[:, d_, hf * JH:(hf + 1) * JH].rearrange("p j n -> p (j n)"), T[0][:], t1[:], t2[:], ['s5T0'], 's5sin')
                                P.op('dve', lambda e: e.tensor_scalar(out=T[0][:], in0=T[0][:], scalar1=PI / 2, scalar2=None, op0=ALU.add), reads=['s5T0'], writes=['s5T0'])
                                sin_of(cosT[:, d_, hf * JH:(hf + 1) * JH].rearrange("p j n -> p (j n)"), T[0][:], t1[:], t2[:], ['s5T0'], 's5cos')
                            P.op('sp', lambda e: e.dma_start(out=Rp[:], in_=s5rep[:, l, d_]), writes=['s5Rp'], dma=True)
                            are, aim, ldt = Rp[:, 0, :], Rp[:, 1, :], Rp[:, 2, :]
                            P.op('act', lambda e: e.activation(out=T[0][:], in_=ldt, func=AF.Exp), reads=['s5Rp'], writes=['s5T0'])
                            P.op('dve', lambda e: e.tensor_tensor(out=T[1][:], in0=are, in1=T[0][:], op=ALU.mult), reads=['s5Rp', 's5T0'], writes=['s5T1'])
                            P.op('act', lambda e: e.activation(out=T[1][:], in_=T[1][:], func=AF.Exp), reads=['s5T1'], writes=['s5T1'])
                            P.op('dve', lambda e: e.tensor_tensor(out=T[0][:], in0=aim, in1=T[0][:], op=ALU.mult), reads=['s5Rp', 's5T0'], writes=['s5T0'])
                            sin_of(T[2][:], T[0][:], t1[:], t2[:], ['s5T0'], 's5T2')
                            P.op('dve', lambda e: e.tensor_scalar(out=T[0][:], in0=T[0][:], scalar1=PI / 2, scalar2=None, op0=ALU.add), reads=['s5T0'], writes=['s5T0'])
                            sin_of(T[3][:], T[0][:], t1[:], t2[:], ['s5T0'], 's5T3')
                            P.op('dve', lambda e: e.tensor_tensor(out=T[2][:], in0=T[2][:], in1=T[1][:], op=ALU.mult), reads=['s5T2', 's5T1'], writes=['s5T2'])
                            P.op('dve', lambda e: e.tensor_tensor(out=T[3][:], in0=T[3][:], in1=T[1][:], op=ALU.mult), reads=['s5T3', 's5T1'], writes=['s5T3'])
                            P.op('dve', lambda e: e.tensor_scalar(out=T[3][:], in0=T[3][:], scalar1=-1.0, scalar2=None, op0=ALU.add), reads=['s5T3'], writes=['s5T3'])
                            P.op('dve', lambda e: e.tensor_tensor(out=T[0][:], in0=are, in1=are, op=ALU.mult), reads=['s5Rp'], writes=['s5T0'])
                            P.op('dve', lambda e: e.tensor_tensor(out=T[1][:], in0=aim, in1=aim, op=ALU.mult), reads=['s5Rp'], writes=['s5T1'])
                            P.op('dve', lambda e: e.tensor_tensor(out=T[0][:], in0=T[0][:], in1=T[1][:], op=ALU.add), reads=['s5T0', 's5T1'], writes=['s5T0'])
                            P.op('dve', lambda e: e.reciprocal(out=T[0][:], in_=T[0][:]), reads=['s5T0'], writes=['s5T0'])
                            P.op('dve', lambda e: e.tensor_tensor(out=T[1][:], in0=T[3][:], in1=are, op=ALU.mult), reads=['s5T3', 's5Rp'], writes=['s5T1'])
                            P.op('dve', lambda e: e.tensor_tensor(out=T[4][:], in0=T[2][:], in1=aim, op=ALU.mult), reads=['s5T2', 's5Rp'], writes=['s5T4'])
                            P.op('dve', lambda e: e.tensor_tensor(out=T[1][:], in0=T[1][:], in1=T[4][:], op=ALU.add), reads=['s5T1', 's5T4'], writes=['s5T1'])
                            P.op('dve', lambda e: e.tensor_tensor(out=T[1][:], in0=T[1][:], in1=T[0][:], op=ALU.mult), reads=['s5T1', 's5T0'], writes=['s5T1'])
                            P.op('dve', lambda e: e.tensor_tensor(out=T[4][:], in0=T[2][:], in1=are, op=ALU.mult), reads=['s5T2', 's5Rp'], writes=['s5T4'])
                            P.op('dve', lambda e: e.tensor_tensor(out=T[5][:], in0=T[3][:], in1=aim, op=ALU.mult), reads=['s5T3', 's5Rp'], writes=['s5T5'])
                            P.op('dve', lambda e: e.tensor_tensor(out=T[4][:], in0=T[4][:], in1=T[5][:], op=ALU.subtract), reads=['s5T4', 's5T5'], writes=['s5T4'])
                            P.op('dve', lambda e: e.tensor_tensor(out=T[4][:], in0=T[4][:], in1=T[0][:], op=ALU.mult), reads=['s5T4', 's5T0'], writes=['s5T4'])
                            zre = T[1][:].rearrange("p (j n) -> p j n", j=8)
                            zim = T[4][:].rearrange("p (j n) -> p j n", j=8)
                            u1 = T[2][:].rearrange("p (j n) -> p j n", j=8)
                            u2 = T[3][:].rearrange("p (j n) -> p j n", j=8)
                            for ri in range(2):
                                P.op('sp', lambda e: e.dma_start(out=Bst[:, ri], in_=s5B[l, d_, ri]), writes=[('s5Bst', ri)], dma=True)
                            P.op('dve', lambda e: e.tensor_tensor(out=u1, in0=zre, in1=Bst[:, 0], op=ALU.mult), reads=['s5T1', ('s5Bst', 0)], writes=['s5T2'])
                            P.op('dve', lambda e: e.tensor_tensor(out=u2, in0=zim, in1=Bst[:, 1], op=ALU.mult), reads=['s5T4', ('s5Bst', 1)], writes=['s5T3'])
                            P.op('dve', lambda e: e.tensor_tensor(out=BT[:, d_, 0], in0=u1, in1=u2, op=ALU.subtract), reads=['s5T2', 's5T3'], writes=['s5BT'])
                            P.op('dve', lambda e: e.tensor_tensor(out=u1, in0=zre, in1=Bst[:, 1], op=ALU.mult), reads=['s5T1', ('s5Bst', 1)], writes=['s5T2'])
                            P.op('dve', lambda e: e.tensor_tensor(out=u2, in0=zim, in1=Bst[:, 0], op=ALU.mult), reads=['s5T4', ('s5Bst', 0)], writes=['s5T3'])
                            P.op('dve', lambda e: e.tensor_tensor(out=BT[:, d_, 1], in0=u1, in1=u2, op=ALU.add), reads=['s5T2', 's5T3'], writes=['s5BT'])
                            for ri in range(2):
                                P.op('sp', lambda e: e.dma_start(out=Bst[:, ri], in_=s5C[l, d_, ri]), writes=[('s5Bst', ri)], dma=True)
                                P.op('act', lambda e: e.activation(out=CT[:, d_, ri], in_=Bst[:, ri], func=AF.Copy, scale=(1.0 if ri == 0 else -1.0)),
                                     reads=[('s5Bst', ri)], writes=['s5CT'])
                        P.barrier()
                    P.barrier()
                    uc = pb_("s5uc", [128, 2, TS])
                    ub = pb_("s5ub", [128, 2, TS], BF16)
                    W1 = pb_("s5W1", [128, TS])
                    W2 = pb_("s5W2", [128, TS])
                    W3 = pb_("s5W3", [128, TS])
                    W4 = pb_("s5W4", [128, TS])
                    V1 = pb_("s5V1", [128, TS])
                    V2 = pb_("s5V2", [128, TS])
                    V3 = pb_("s5V3", [128, TS])
                    V4 = pb_("s5V4", [128, TS])
                    btr = pb_("s5btr", [128, TS])
                    bti = pb_("s5bti", [128, TS])
                    wre = pb_("s5wre", [128, TS])
                    wim = pb_("s5wim", [128, TS])
                    xre = pb_("s5xre", [128, TS])
                    xim = pb_("s5xim", [128, TS])
                    xb = pb_("s5xb", [128, 2, 8, TS], BF16)
                    xin = pb_("s5xin", [128, 2, 8])
                    yf = pb_("s5yf", [128, 2, TS])
                    yt = pb_("s5yt", [128, 2, TS])
                    g32 = pb_("s5g32", [128, 2, TS])
                    gb = pb_("s5gb", [128, 2, TS], BF16)
                    yast = pb_("s5yast", [128, 2, TS], BF16)
                    for s, n_tok in seqs:
                        uV = scr[s]['uT'].rearrange("(k p) l -> p k l", p=128)
                        ysV = scr[s]['ysT'].rearrange("(k p) l -> p k l", p=128)
                        yaV = scr[s]['yaT'].rearrange("(k p) l -> p k l", p=128)
                        chunks = [(0, NM)] + [(NM + i * TS, TS) for i in range(n_tok // TS)]
                        for d_ in range(2):
                            bwd = (d_ == 1)
                            clist = chunks if not bwd else chunks[::-1]
                            P.op('dve', lambda e: e.memset(xin[:], 0.0), writes=['s5xin'])
                            for pos, n in clist:
                                P.op('sp', lambda e: e.dma_start(out=uc[:, :, :n], in_=uV[:, :, pos:pos + n]), reads=[('uT', s)], writes=['s5uc'], dma=True)
                                if bwd:
                                    P.op('pool', lambda e: e.dma_start(out=yf[:, :, :n], in_=ysV[:, :, pos:pos + n]), reads=[('ysT', s)], writes=['s5yf'], dma=True)
                                    P.op('act', lambda e: e.activation(out=ub[:, :, :n], in_=uc[:, :, n - 1::-1], func=AF.Copy), reads=['s5uc'], writes=['s5ub'])
                                else:
                                    P.op('act', lambda e: e.activation(out=ub[:, :, :n], in_=uc[:, :, :n], func=AF.Copy), reads=['s5uc'], writes=['s5ub'])
                                for j in range(8):
                                    kt = j // 4
                                    psb, pkb = next_ps()
                                    P.op('pe', lambda e: e.matmul(psb[:, 0:n], lhsT=BT[:, d_, 0, j, :], rhs=ub[:, kt, :n], start=True, stop=True),
                                         reads=['s5BT', 's5ub'], writes=[pkb])
                                    P.op('pe', lambda e: e.matmul(psb[:, TS:TS + n], lhsT=BT[:, d_, 1, j, :], rhs=ub[:, kt, :n], start=True, stop=True),
                                         reads=['s5BT', 's5ub'], writes=[pkb])
                                    cs, sn = cosT[:, d_, j, :n], sinT[:, d_, j, :n]
                                    bre, bim = psb[:, 0:n], psb[:, TS:TS + n]
                                    P.op('dve', lambda e: e.tensor_tensor(out=W1[:, :n], in0=bre, in1=cs, op=ALU.mult), reads=[pkb, 's5cos'], writes=['s5W1'])
                                    P.op('dve', lambda e: e.tensor_tensor(out=W2[:, :n], in0=bim, in1=sn, op=ALU.mult), reads=[pkb, 's5sin'], writes=['s5W2'])
                                    P.op('dve', lambda e: e.tensor_tensor(out=btr[:, :n], in0=W1[:, :n], in1=W2[:, :n], op=ALU.add), reads=['s5W1', 's5W2'], writes=['s5btr'])
                                    P.op('dve', lambda e: e.tensor_tensor(out=W3[:, :n], in0=bim, in1=cs, op=ALU.mult), reads=[pkb, 's5cos'], writes=['s5W3'])
                                    P.op('dve', lambda e: e.tensor_tensor(out=W4[:, :n], in0=bre, in1=sn, op=ALU.mult), reads=[pkb, 's5sin'], writes=['s5W4'])
                                    P.op('dve', lambda e: e.tensor_tensor(out=bti[:, :n], in0=W3[:, :n], in1=W4[:, :n], op=ALU.subtract), reads=['s5W3', 's5W4'], writes=['s5bti'])
                                    P.op('dve', lambda e: e.tensor_tensor_scan(out=wre[:, :n], data0=rT[:, d_, j, :n], data1=btr[:, :n], initial=xin[:, 0, j:j + 1],
                                                                              op0=ALU.mult, op1=ALU.add), reads=['s5rT', 's5btr', 's5xin'], writes=['s5wre'])
                                    P.op('dve', lambda e: e.tensor_tensor_scan(out=wim[:, :n], data0=rT[:, d_, j, :n], data1=bti[:, :n], initial=xin[:, 1, j:j + 1],
                                                                              op0=ALU.mult, op1=ALU.add), reads=['s5rT', 's5bti', 's5xin'], writes=['s5wim'])
                                    P.op('pool', lambda e: e.tensor_tensor(out=V1[:, :n], in0=wre[:, :n], in1=cs, op=ALU.mult), reads=['s5wre', 's5cos'], writes=['s5V1'])
                                    P.op('pool', lambda e: e.tensor_tensor(out=V2[:, :n], in0=wim[:, :n], in1=sn, op=ALU.mult), reads=['s5wim', 's5sin'], writes=['s5V2'])
                                    P.op('pool', lambda e: e.tensor_tensor(out=xre[:, :n], in0=V1[:, :n], in1=V2[:, :n], op=ALU.subtract), reads=['s5V1', 's5V2'], writes=['s5xre'])
                                    P.op('pool', lambda e: e.tensor_tensor(out=V3[:, :n], in0=wre[:, :n], in1=sn, op=ALU.mult), reads=['s5wre', 's5sin'], writes=['s5V3'])
                                    P.op('pool', lambda e: e.tensor_tensor(out=V4[:, :n], in0=wim[:, :n], in1=cs, op=ALU.mult), reads=['s5wim', 's5cos'], writes=['s5V4'])
                                    P.op('pool', lambda e: e.tensor_tensor(out=xim[:, :n], in0=V3[:, :n], in1=V4[:, :n], op=ALU.add), reads=['s5V3', 's5V4'], writes=['s5xim'])
                                    P.op('act', lambda e: e.activation(out=xb[:, 0, j, :n], in_=xre[:, :n], func=AF.Copy), reads=['s5xre'], writes=['s5xb'])
                                    P.op('act', lambda e: e.activation(out=xb[:, 1, j, :n], in_=xim[:, :n], func=AF.Copy), reads=['s5xim'], writes=['s5xb'])
                                    P.op('act', lambda e: e.activation(out=xin[:, 0, j:j + 1], in_=xre[:, n - 1:n], func=AF.Copy), reads=['s5xre'], writes=['s5xin'])
                                    P.op('act', lambda e: e.activation(out=xin[:, 1, j:j + 1], in_=xim[:, n - 1:n], func=AF.Copy), reads=['s5xim'], writes=['s5xin'])
                                psy, pky = next_ps()
                                for m in range(2):
                                    first = True
                                    for j in range(4 * m, 4 * m + 4):
                                        for ri in range(2):
                                            last = (j == 4 * m + 3 and ri == 1)
                                            P.op('pe', lambda e: e.matmul(psy[:, m * TS:m * TS + n], lhsT=CT[:, d_, ri, j, :], rhs=xb[:, ri, j, :n],
                                                                          start=first, stop=last), reads=['s5CT', 's5xb'], writes=[pky])
                                            first = False
                                psy3 = psy[:, 0:2 * TS].rearrange("p (m t) -> p m t", m=2)
                                if not bwd:
                                    P.op('act', lambda e: e.activation(out=yt[:, :, :n], in_=psy3[:, :, :n], func=AF.Copy), reads=[pky], writes=['s5yt'])
                                    P.op('sp', lambda e: e.dma_start(out=ysV[:, :, pos:pos + n], in_=yt[:, :, :n]), reads=['s5yt'], writes=[('ysT', s)], dma=True)
                                    continue
                                P.op('dve', lambda e: e.tensor_tensor(out=yt[:, :, :n], in0=psy3[:, :, n - 1::-1], in1=yf[:, :, :n], op=ALU.add),
                                     reads=[pky, 's5yf'], writes=['s5yt'])
                                for m in range(2):
                                    P.op('dve', lambda e: e.scalar_tensor_tensor(out=yt[:, m, :n], in0=uc[:, m, :n], scalar=s5ds[:, l, m:m + 1], in1=yt[:, m, :n],
                                                                                 op0=ALU.mult, op1=ALU.add), reads=['s5uc', 's5yt', 's5ds'], writes=['s5yt'])
                                P.op('dve', lambda e: e.tensor_tensor(out=g32[:, :, :n], in0=yt[:, :, :n], in1=yt[:, :, :n], op=ALU.mult), reads=['s5yt'], writes=['s5g32'])
                                P.op('dve', lambda e: e.tensor_scalar(out=g32[:, :, :n], in0=g32[:, :, :n], scalar1=0.044715, scalar2=1.0, op0=ALU.mult, op1=ALU.add),
                                     reads=['s5g32'], writes=['s5g32'])
                                P.op('dve', lambda e: e.tensor_tensor(out=g32[:, :, :n], in0=g32[:, :, :n], in1=yt[:, :, :n], op=ALU.mult), reads=['s5g32', 's5yt'], writes=['s5g32'])
                                P.op('act', lambda e: e.activation(out=g32[:, :, :n], in_=g32[:, :, :n], func=AF.Sigmoid, scale=1.5957691216057308),
                                     reads=['s5g32'], writes=['s5g32'])
                                P.op('dve', lambda e: e.tensor_tensor(out=g32[:, :, :n], in0=g32[:, :, :n], in1=yt[:, :, :n], op=ALU.mult), reads=['s5g32', 's5yt'], writes=['s5g32'])
                                P.op('act', lambda e: e.activation(out=gb[:, :, :n], in_=g32[:, :, :n], func=AF.Copy), reads=['s5g32'], writes=['s5gb'])
                                psg, pkg = next_ps()
                                for m2 in range(2):
                                    for k2 in range(2):
                                        P.op('pe', lambda e: e.matmul(psg[:, m2 * TS:m2 * TS + n], lhsT=WGL[:, k2, m2 * 128:(m2 + 1) * 128], rhs=gb[:, k2, :n],
                                                                      start=(k2 == 0), stop=(k2 == 1)), reads=['WGL', 's5gb'], writes=[pkg])
                                psg3 = psg[:, 0:2 * TS].rearrange("p (m t) -> p m t", m=2)
                                P.op('act', lambda e: e.activation(out=yt[:, :, :n], in_=psg3[:, :, :n], func=AF.Sigmoid), reads=[pkg], writes=['s5yt'])
                                P.op('dve', lambda e: e.tensor_tensor(out=yast[:, :, :n], in0=g32[:, :, :n], in1=yt[:, :, :n], op=ALU.mult), reads=['s5g32', 's5yt'], writes=['s5yast'])
                                P.op('sp', lambda e: e.dma_start(out=yaV[:, :, pos:pos + n], in_=yast[:, :, :n]), reads=['s5yast'], writes=[('yaT', s)], dma=True)
                    P.barrier()
                P.barrier()
                if 'b' in mixers:
                  with ExitStack() as ph:
                    def pb_(name, shape, dt=F32):
                        return ph.enter_context(nc.sbuf_tensor(f"{name}_{l}", list(shape), dt))
                    negm = pb_("nnegm", [128, 128])
                    BI = [pb_("nBI0", [128, 8, 5, 128]), pb_("nBI1", [128, 8, 5, 128])]
                    tmpb = pb_("ntmpb", [128, 8, 5, 128])
                    Kmeta = pb_("nKm", [64, 8, 16], BF16)
                    Vmeta = pb_("nVm", [16, 8, 65], BF16)
                    Qp = pb_("nQp", [64, 8, 128], BF16)
                    Kp = pb_("nKp", [64, 8, 640], BF16)
                    Vp = pb_("nVp", [128, 5, 8, 65], BF16)
                    sc = pb_("nsc", [128, 640])
                    Pt = pb_("nPt", [128, 5, 128], BF16)
                    Pm = pb_("nPm", [16, 128], BF16)
                    rec = pb_("nrec", [128, 8, 1])
                    Otok = pb_("nOtok", [128, 8, 64])
                    ybst = pb_("nybst", [128, 4, 128], BF16)
                    P.op('sp', lambda e: e.dma_start(out=negm[:], in_=c_negm), writes=['nnegm'], dma=True)
                    P.op('dve', lambda e: e.memset(Vp[:], 1.0), writes=['nVp'])
                    P.op('dve', lambda e: e.memset(Vmeta[:], 1.0), writes=['nVm'])

                    def variant_of(r, rows):
                        a_ = min(max(r - 4, 0), rows - 10)
                        ds = []
                        for j in range(10):
                            for dl in range(2):
                                kr = a_ + j
                                rq = r + dl
                                rs = min(max(rq - 4, 0), rows - 8)
                                ds.append(kr - rq + 7 if rs <= kr < rs + 8 else 15)
                        return a_, tuple(ds)

                    def build_bias(buf, bkey, var):
                        for j in range(10):
                            t, jj = j // 2, j % 2
                            for dl in range(2):
                                dd = var[j * 2 + dl]
                                src = bass.AP(tensor=rpbpad.tensor, offset=(l * 8 * 16 + dd) * 127, ap=[[1, 64], [16 * 127, 8], [1, 64]])
                                P.op('sp', lambda e: e.dma_start(out=tmpb[jj * 64:(jj + 1) * 64, :, t, dl * 64:(dl + 1) * 64], in_=src),
                                     writes=['ntmpb'], dma=True)
                        for h in range(8):
                            for t in range(5):
                                P.op('dve', lambda e: e.tensor_tensor(
                                    out=buf[:, h, t, :].rearrange("p (a q) -> p a q", a=2),
                                    in0=tmpb[:, h, t, :].rearrange("p (a q) -> p a q", a=2)[:, :, ::-1],
                                    in1=negm[:].rearrange("p (a q) -> p a q", a=2), op=ALU.add),
                                    reads=['ntmpb', 'nnegm'], writes=[bkey])

                    for s, n_tok in seqs:
                        rows = n_tok // 64
                        qkV = scr[s]['qkT'].rearrange("(a h d) l -> d a h l", a=2, h=8, d=64)
                        ybV = scr[s]['ybT'].rearrange("(k p) l -> p k l", p=128)
                        vt = scr[s]['vtok']

                        def finish(pso, pko, nq, pos):
                            pso3 = pso[:].rearrange("p (h c) -> p h c", h=8)
                            P.op('dve', lambda e: e.reciprocal(out=rec[:nq], in_=pso3[:nq, :, 64:65]), reads=pko, writes=['nrec'])
                            P.op('dve', lambda e: e.tensor_tensor(out=Otok[:nq], in0=pso3[:nq, :, 0:64], in1=rec[:nq].to_broadcast([nq, 8, 64]), op=ALU.mult),
                                 reads=pko + ['nrec'], writes=['nOtok'])
                            pst, pkt = ps_t[2][:, 0:512], ('ps', 4)
                            Of = Otok[:].rearrange("p h d -> p (h d)")
                            for k in range(4):
                                P.op('pe', lambda e: e.transpose(out=pst[:, k * 128:k * 128 + nq], in_=Of[:nq, k * 128:(k + 1) * 128], identity=ident[:nq, :nq]),
                                     reads=['nOtok', 'ident'], writes=[pkt])
                            P.op('act', lambda e: e.activation(out=ybst[:, :, :nq], in_=pst[:, :].rearrange("p (k t) -> p k t", k=4)[:, :, :nq], func=AF.Copy),
                                 reads=[pkt], writes=['nybst'])
                            P.op('sp', lambda e: e.dma_start(out=ybV[:, :, pos:pos + nq], in_=ybst[:, :, :nq]), reads=['nybst'], writes=[('ybT', s)], dma=True)

                        P.op('sp', lambda e: e.dma_start(out=Kmeta[:], in_=qkV[:, 1, :, 0:NM]), reads=[('qkT', s)], writes=['nKm'], dma=True)
                        P.op('sp', lambda e: e.dma_start(out=Vmeta[:, :, 0:64], in_=vt[0:NM, :].rearrange("t (h d) -> t h d", h=8)),
                             reads=[('vtok', s)], writes=['nVm'], dma=True)
                        P.op('sp', lambda e: e.dma_start(out=Qp[:, :, 0:NM], in_=qkV[:, 0, :, 0:NM]), reads=[('qkT', s)], writes=['nQp'], dma=True)
                        pss, pks = ps_t[0], [('ps', 0), ('ps', 1)]
                        for h in range(8):
                            P.op('pe', lambda e: e.matmul(pss[:NM, h * 16:(h + 1) * 16], lhsT=Kmeta[:, h, :], rhs=Qp[:, h, 0:NM], start=True, stop=True),
                                 reads=['nKm', 'nQp'], writes=pks)
                        P.op('act', lambda e: e.activation(out=Pm[:, :], in_=pss[:NM, 0:128], func=AF.Exp), reads=pks, writes=['nPm'])
                        pso, pko = ps_t[3], [('ps', 6), ('ps', 7)]
                        for h in range(8):
                            P.op('pe', lambda e: e.matmul(pso[:NM, h * 128:h * 128 + 65], lhsT=Pm[:, h * 16:(h + 1) * 16], rhs=Vmeta[:, h, :], start=True, stop=True),
                                 reads=['nPm', 'nVm'], writes=pko)
                        finish(pso, pko, NM, 0)
                        vars_ = [variant_of(r, rows) for r in range(0, rows, 2)]
                        cnt = {}
                        for _, v in vars_:
                            cnt[v] = cnt.get(v, 0) + 1
                        vint = max(cnt, key=cnt.get)
                        build_bias(BI[0], 'nBI0', vint)
                        cur = [None]
                        for pi, (a_, var) in enumerate(vars_):
                            r = pi * 2
                            if var == vint:
                                bi, bkey = BI[0], 'nBI0'
                            else:
                                if cur[0] != var:
                                    build_bias(BI[1], 'nBI1', var)
                                    cur[0] = var
                                bi, bkey = BI[1], 'nBI1'
                            pq = NM + r * 64
                            kp0 = NM + a_ * 64
                            P.op('sp', lambda e: e.dma_start(out=Qp[:], in_=qkV[:, 0, :, pq:pq + 128]), reads=[('qkT', s)], writes=['nQp'], dma=True)
                            P.op('sp', lambda e: e.dma_start(out=Kp[:], in_=qkV[:, 1, :, kp0:kp0 + 640]), reads=[('qkT', s)], writes=['nKp'], dma=True)
                            for t in range(5):
                                P.op('pool', lambda e: e.dma_start(out=Vp[:, t, :, 0:64], in_=vt[kp0 + t * 128:kp0 + (t + 1) * 128, :].rearrange("p (h d) -> p h d", h=8)),
                                     reads=[('vtok', s)], writes=['nVp'], dma=True)
                            pso, pko = ps_t[3], [('ps', 6), ('ps', 7)]
                            for h in range(8):
                                pss, pks = (ps_t[0], [('ps', 0), ('ps', 1)]) if h % 2 == 0 else (ps_t[1], [('ps', 2), ('ps', 3)])
                                for t in range(5):
                                    P.op('pe', lambda e: e.matmul(pss[:, t * 128:(t + 1) * 128], lhsT=Kp[:, h, t * 128:(t + 1) * 128], rhs=Qp[:, h, :], start=True, stop=True),
                                         reads=['nKp', 'nQp'], writes=pks)
                                P.op('pe', lambda e: e.matmul(pss[:NM, 640:768], lhsT=Kmeta[:, h, :], rhs=Qp[:, h, :], start=True, stop=True),
                                     reads=['nKm', 'nQp'], writes=pks)
                                P.op('dve', lambda e: e.tensor_tensor(out=sc[:], in0=pss[:, 0:640], in1=bi[:, h].rearrange("p t q -> p (t q)"), op=ALU.add),
                                     reads=pks + [bkey], writes=['nsc'])
                                P.op('act', lambda e: e.activation(out=Pt[:].rearrange("p t q -> p (t q)"), in_=sc[:], func=AF.Exp), reads=['nsc'], writes=['nPt'])
                                P.op('act', lambda e: e.activation(out=Pm[:, :], in_=pss[:NM, 640:768], func=AF.Exp), reads=pks, writes=['nPm'])
                                for t in range(5):
                                    P.op('pe', lambda e: e.matmul(pso[:, h * 128:h * 128 + 65], lhsT=Pt[:, t, :], rhs=Vp[:, t, h, :], start=(t == 0), stop=False),
                                         reads=['nPt', 'nVp'], writes=pko)
                                P.op('pe', lambda e: e.matmul(pso[:, h * 128:h * 128 + 65], lhsT=Pm[:, :], rhs=Vmeta[:, h, :], start=False, stop=True),
                                     reads=['nPm', 'nVm'], writes=pko)
                            finish(pso, pko, 128, pq)
                    P.barrier()
                P.barrier()
                if 'c' in mixers:
                  with ExitStack() as ph:
                    def pb_(name, shape, dt=F32):
                        return ph.enter_context(nc.sbuf_tensor(f"{name}_{l}", list(shape), dt))
                    qf_2 = [pb_("hqa", [64, 4, 128]), pb_("hqb", [64, 4, 128])]
                    ff_2 = [pb_("hfa", [64, 4, 128]), pb_("hfb", [64, 4, 128])]
                    gl_2 = [pb_("hgla", [64, 4, 128]), pb_("hglb", [64, 4, 128])]
                    kk_2 = [pb_("hka", [64, 4, 128]), pb_("hkb", [64, 4, 128])]
                    Bc_2 = [pb_("hBa", [64, 4, 128]), pb_("hBb", [64, 4, 128])]
                    Bl_2 = [pb_("hBla", [64, 4, 128]), pb_("hBlb", [64, 4, 128])]
                    Be_2 = [pb_("hBea", [64, 4, 128]), pb_("hBeb", [64, 4, 128])]
                    REF_2 = [pb_("hREFa", [64, 4, 4]), pb_("hREFb", [64, 4, 4])]
                    END_2 = [pb_("hENDa", [64, 4, 4]), pb_("hENDb", [64, 4, 4])]
                    DEC_2 = [pb_("hDECa", [64, 4, 4]), pb_("hDECb", [64, 4, 4])]
                    Qt_2 = [pb_("hQta", [64, 4, 128], BF16), pb_("hQtb", [64, 4, 128], BF16)]
                    Kt_2 = [pb_("hKta", [64, 4, 128], BF16), pb_("hKtb", [64, 4, 128], BF16)]
                    Kh_2 = [pb_("hKha", [64, 4, 128]), pb_("hKhb", [64, 4, 128])]
                    Khtok_2 = [pb_("hKhtoka", [128, 4, 256], BF16), pb_("hKhtokb", [128, 4, 256], BF16)]
                    Vtok_2 = [pb_("hVtoka", [128, 256], BF16), pb_("hVtokb", [128, 256], BF16)]
                    attm_2 = [pb_("hattma", [128, 4, 128], BF16), pb_("hattmb", [128, 4, 128], BF16)]
                    S32 = pb_("hS32", [64, 4, 64])
                    Sbf = pb_("hSbf", [64, 4, 5, 64], BF16)
                    ob__2 = [pb_("hoba", [64, 4, 128]), pb_("hobb", [64, 4, 128])]
                    ob2_2 = [pb_("hob2a", [64, 4, 128]), pb_("hob2b", [64, 4, 128])]
                    for s, n_tok in seqs:
                        hgV = scr[s]['hgT'].rearrange("(k h d) l -> d k h l", k=3, h=4, d=64)
                        ycV = scr[s]['ycT'].rearrange("(h d) l -> d h l", d=64)
                        groups = [(0, NM)] + [(NM + i * 128, 128) for i in range(n_tok // 128)]
                        for di in range(2):
                            bwd = (di == 1)
                            glist = groups if not bwd else groups[1:][::-1] + groups[:1]
                            mask = maskb if bwd else maskf
                            P.op('dve', lambda e: e.memset(S32[:], 0.0), writes=['S32'])
                            P.op('dve', lambda e: e.memset(Sbf[:], 0.0), writes=[('Sbf', i) for i in range(5)])
                            for gi_, (pos, n) in enumerate(glist):
                                par = gi_ % 2
                                qf = qf_2[par]
                                ff = ff_2[par]
                                gl = gl_2[par]
                                kk = kk_2[par]
                                Bc = Bc_2[par]
                                Bl = Bl_2[par]
                                Be = Be_2[par]
                                REF = REF_2[par]
                                END = END_2[par]
                                DEC = DEC_2[par]
                                Qt = Qt_2[par]
                                Kt = Kt_2[par]
                                Kh = Kh_2[par]
                                Khtok = Khtok_2[par]
                                Vtok = Vtok_2[par]
                                attm = attm_2[par]
                                ob_ = ob__2[par]
                                ob2 = ob2_2[par]
                                P.alias = {k_: k_ + '#' + str(par) for k_ in ['hq', 'hf', 'hgl', 'hk', 'hB', 'hBl', 'hBe', 'hREF', 'hEND', 'hDEC', 'hQt', 'hKt', 'hKh', 'hKhtok', 'hVtok', 'hattm', 'hob', 'hob2']}
                                csz = min(32, n)
                                nch = n // csz
                                P.op('sp', lambda e: e.dma_start(out=qf[:, :, :n], in_=hgV[:, 0, :, pos:pos + n]), reads=[('hgT', s)], writes=['hq'], dma=True)
                                P.op('sp', lambda e: e.dma_start(out=ff[:, :, :n], in_=hgV[:, (2 if bwd else 1), :, pos:pos + n]),
                                     reads=[('hgT', s)], writes=['hf'], dma=True)
                                P.op('pool', lambda e: e.dma_start(out=Vtok[:n, :], in_=scr[s]['ictok'][pos:pos + n, :]), reads=[('ictok', s)], writes=['hVtok'], dma=True)
                                if bwd:
                                    P.op('pool', lambda e: e.dma_start(out=ob2[:, :, :n], in_=ycV[:, :, pos:pos + n]), reads=[('ycT', s)], writes=['hob2'], dma=True)
                                P.op('act', lambda e: e.activation(out=ff[:, :, :n], in_=ff[:, :, :n], func=AF.Sigmoid), reads=['hf'], writes=['hf'])
                                for h in range(4):
                                    P.op('act', lambda e: e.activation(out=gl[:, h, :n], in_=ff[:, h, :n], func=AF.Ln,
                                                                       bias=lbs[:, h, l:l + 1], scale=oml[:, h, l:l + 1]),
                                         reads=['hf', 'lbs', 'oml'], writes=['hgl'])
                                    P.op('dve', lambda e: e.tensor_scalar(out=kk[:, h, :n], in0=ff[:, h, :n], scalar1=noml[:, h, l:l + 1],
                                                                          scalar2=oml[:, h, l:l + 1], op0=ALU.mult, op1=ALU.add),
                                         reads=['hf', 'oml', 'noml'], writes=['hk'])
                                    if not bwd:
                                        P.op('dve', lambda e: e.tensor_tensor_scan(out=Bc[:, h, :n], data0=onesT[0:64, :n], data1=gl[:, h, :n],
                                                                                  initial=0.0, op0=ALU.mult, op1=ALU.add),
                                             reads=['hgl', 'onesT'], writes=['hB'])
                                    else:
                                        P.op('dve', lambda e: e.tensor_tensor_scan(out=Bc[:, h, n - 1::-1] if n < 128 else Bc[:, h, ::-1],
                                                                                  data0=onesT[0:64, :n],
                                                                                  data1=gl[:, h, n - 1::-1] if n < 128 else gl[:, h, ::-1],
                                                                                  initial=0.0, op0=ALU.mult, op1=ALU.add),
                                             reads=['hgl', 'onesT'], writes=['hB'])
                                P.op('act', lambda e: e.activation(out=qf[:, :, :n], in_=qf[:, :, :n], func=AF.Silu), reads=['hq'], writes=['hq'])
                                if hg_stop < 2:
                                    continue
                                P.op('dve', lambda e: e.memset(REF[:], 0.0), writes=['hREF'])
                                if nch > 1:
                                    if not bwd:
                                        P.op('dve', lambda e: e.tensor_copy(out=REF[:, :, 1:nch], in_=Bc[:, :, csz - 1:n - 1:csz]), reads=['hB'], writes=['hREF'])
                                    else:
                                        P.op('dve', lambda e: e.tensor_copy(out=REF[:, :, 0:nch - 1], in_=Bc[:, :, csz:n:csz]), reads=['hB'], writes=['hREF'])
                                if not bwd:
                                    P.op('dve', lambda e: e.tensor_copy(out=END[:, :, 0:nch], in_=Bc[:, :, csz - 1:n:csz]), reads=['hB'], writes=['hEND'])
                                else:
                                    P.op('dve', lambda e: e.tensor_copy(out=END[:, :, 0:nch], in_=Bc[:, :, 0:n:csz]), reads=['hB'], writes=['hEND'])
                                for h in range(4):
                                    P.op('dve', lambda e: e.tensor_tensor(
                                        out=Bl[:, h, :n].rearrange("p (c j) -> p c j", j=csz), in0=Bc[:, h, :n].rearrange("p (c j) -> p c j", j=csz),
                                        in1=REF[:, h, 0:nch].unsqueeze(2).to_broadcast([64, nch, csz]), op=ALU.subtract),
                                        reads=['hB', 'hREF'], writes=['hBl'])
                                    P.op('dve', lambda e: e.tensor_tensor(
                                        out=Be[:, h, :n].rearrange("p (c j) -> p c j", j=csz), in0=Bc[:, h, :n].rearrange("p (c j) -> p c j", j=csz),
                                        in1=END[:, h, 0:nch].unsqueeze(2).to_broadcast([64, nch, csz]), op=ALU.subtract),
                                        reads=['hB', 'hEND'], writes=['hBe'])
                                P.op('dve', lambda e: e.tensor_tensor(out=DEC[:, :, 0:nch], in0=END[:, :, 0:nch], in1=REF[:, :, 0:nch], op=ALU.subtract),
                                     reads=['hEND', 'hREF'], writes=['hDEC'])
                                P.op('act', lambda e: e.activation(out=DEC[:, :, 0:nch], in_=DEC[:, :, 0:nch], func=AF.Exp), reads=['hDEC'], writes=['hDEC'])
                                P.op('act', lambda e: e.activation(out=Bc[:, :, :n], in_=Bl[:, :, :n], func=AF.Exp), reads=['hBl'], writes=['hB'])
                                P.op('act', lambda e: e.activation(out=Bl[:, :, :n], in_=Bl[:, :, :n], func=AF.Exp, scale=-1.0), reads=['hBl'], writes=['hBl'])
                                P.op('act', lambda e: e.activation(out=Be[:, :, :n], in_=Be[:, :, :n], func=AF.Exp, scale=-1.0), reads=['hBe'], writes=['hBe'])
                                P.op('dve', lambda e: e.tensor_tensor(out=Qt[:, :, :n], in0=qf[:, :, :n], in1=Bc[:, :, :n], op=ALU.mult), reads=['hq', 'hB'], writes=['hQt'])
                                P.op('dve', lambda e: e.tensor_tensor(out=Kt[:, :, :n], in0=kk[:, :, :n], in1=Bl[:, :, :n], op=ALU.mult), reads=['hk', 'hBl'], writes=['hKt'])
                                P.op('dve', lambda e: e.tensor_tensor(out=Kh[:, :, :n], in0=kk[:, :, :n], in1=Be[:, :, :n], op=ALU.mult), reads=['hk', 'hBe'], writes=['hKh'])
                                if hg_stop < 3:
                                    continue
                                pst, pkt = next_ps()
                                for h in range(4):
                                    P.op('pe', lambda e: e.transpose(out=pst[:n, h * 64:(h + 1) * 64], in_=Kh[:, h, :n], identity=ident[0:64, 0:64]),
                                         reads=['hKh', 'ident'], writes=[pkt])
                                for c in range(nch):
                                    P.op('act', lambda e: e.activation(out=Khtok[:n, c, :], in_=pst[:n, 0:256], func=AF.Copy, scale=cmask[:n, c:c + 1]),
                                         reads=[pkt, 'cmask'], writes=['hKhtok'])
                                if hg_stop < 4:
                                    continue
                                psa, pka = next_ps()
                                for h in range(4):
                                    P.op('pe', lambda e: e.matmul(psa[:n, h * 128:h * 128 + n], lhsT=Kt[:, h, :n], rhs=Qt[:, h, :n],
                                                                  start=True, stop=True), reads=['hKt', 'hQt'], writes=[pka])
                                P.op('dve', lambda e: e.tensor_tensor(out=attm[:n, :, :n], in0=psa[:n, :].rearrange("p (h i) -> p h i", h=4)[:, :, :n],
                                                                      in1=mask[:n, :n].unsqueeze(1).to_broadcast([n, 4, n]), op=ALU.mult),
                                     reads=[pka, 'maskf', 'maskb'], writes=['hattm'])
                                if hg_stop < 5:
                                    continue
                                pso, pko = next_ps()
                                psd, pkd = next_ps_big()
                                order = list(range(nch)) if not bwd else list(range(nch))[::-1]
                                for h in range(4):
                                    for c in range(nch):
                                        P.op('pe', lambda e: e.matmul(
                                            psd[0:64, (h * 4 + c) * 64:(h * 4 + c) * 64 + 64], lhsT=Khtok[:n, c, h * 64:(h + 1) * 64],
                                            rhs=Vtok[:n, h * 64:(h + 1) * 64], start=True, stop=True),
                                            reads=['hKhtok', 'hVtok'], writes=pkd)
                                for i, c in enumerate(order):
                                    for h in range(4):
                                        P.op('dve', lambda e: e.scalar_tensor_tensor(
                                            out=S32[:, h, :], in0=S32[:, h, :], scalar=DEC[:, h, c:c + 1], in1=psd[0:64, (h * 4 + c) * 64:(h * 4 + c) * 64 + 64],
                                            op0=ALU.mult, op1=ALU.add), reads=['S32', 'hDEC'] + pkd, writes=['S32'])
                                    P.op('act', lambda e: e.activation(out=Sbf[:, :, i + 1, :], in_=S32[:], func=AF.Copy), reads=['S32'], writes=[('Sbf', i + 1)])
                                if hg_stop < 6:
                                    continue
                                for h in range(4):
                                    for i, c in enumerate(order):
                                        P.op('pe', lambda e: e.matmul(pso[0:64, h * 128 + c * csz:h * 128 + (c + 1) * csz], lhsT=Vtok[:n, h * 64:(h + 1) * 64],
                                                                      rhs=attm[:n, h, c * csz:(c + 1) * csz], start=True, stop=False),
                                             reads=['hVtok', 'hattm'], writes=[pko])
                                        P.op('pe', lambda e: e.matmul(
                                            pso[0:64, h * 128 + c * csz:h * 128 + (c + 1) * csz], lhsT=Sbf[:, h, i, :],
                                            rhs=Qt[:, h, c * csz:(c + 1) * csz], start=False, stop=True),
                                            reads=[('Sbf', i), 'hQt'], writes=[pko])
                                P.op('act', lambda e: e.activation(out=Sbf[:, :, 0, :], in_=S32[:], func=AF.Copy), reads=['S32'] + [('Sbf', i) for i in range(5)],
                                     writes=[('Sbf', 0)])
                                if not bwd:
                                    P.op('dve', lambda e: e.tensor_copy(out=ob_[:, :, :n], in_=pso[0:64, :].rearrange("p (h i) -> p h i", h=4)[:, :, :n]),
                                         reads=[pko], writes=['hob'])
                                else:
                                    P.op('dve', lambda e: e.tensor_tensor(out=ob_[:, :, :n], in0=pso[0:64, :].rearrange("p (h i) -> p h i", h=4)[:, :, :n],
                                                                          in1=ob2[:, :, :n], op=ALU.add), reads=[pko, 'hob2'], writes=['hob'])
                                P.op('sp', lambda e: e.dma_start(out=ycV[:, :, pos:pos + n], in_=ob_[:, :, :n]), reads=['hob'], writes=[('ycT', s)], dma=True)
                    P.alias = {}
                    P.barrier()
                P.barrier()
            with ExitStack() as ph:
                def pb_(name, shape, dt=F32):
                    return ph.enter_context(nc.sbuf_tensor(f"{name}_{l}", list(shape), dt))
                WC = pb_("WC", [128, 8, 3328], BF16)
                WUA = pb_("WUA", [128, 2, D], BF16)
                WUB = pb_("WUB", [128, 4, D], BF16)
                WUC = pb_("WUC", [128, 2, D], BF16)
                WO = pb_("WO", [128, 8, D], BF16)
                ya = pb_("ya", [128, 2, TT], BF16)
                yb = pb_("yb", [128, 4, TT], BF16)
                yc = pb_("yc", [128, 2, TT])
                ycs = pb_("ycs", [128, 2, TT])
                ycn = pb_("ycn", [128, 2, TT], BF16)
                rc = pb_("rc", [128, TT])
                sg = pb_("sg", [128, 3, TT])
                m1 = pb_("m1", [128, TT])
                m2 = pb_("m2", [128, TT])
                mix = pb_("mix", [128, 8, TT], BF16)
                for k in range(8):
                    load_w(WC[:, k, :], 'WC', w_in[l, k * 128:(k + 1) * 128, 2816:6144], 3328, scale=g1s[:, l, k:k + 1])
                    load_w(WO[:, k, :], 'WO', w_o[l, k * 128:(k + 1) * 128, :], D)
                for k in range(2):
                    load_w(WUA[:, k, :], 'WUA', w_up_a[l, k * 128:(k + 1) * 128, :], D)
                    load_w(WUC[:, k, :], 'WUC', w_up_c[l, k * 128:(k + 1) * 128, :], D)
                for k in range(4):
                    load_w(WUB[:, k, :], 'WUB', w_up_b[l, k * 128:(k + 1) * 128, :], D)
                for s, n_tok in seqs:
                    yaV = scr[s]['yaT'].rearrange("(k p) l -> p k l", p=128)
                    ybV = scr[s]['ybT'].rearrange("(k p) l -> p k l", p=128)
                    ycV = scr[s]['ycT'].rearrange("(k p) l -> p k l", p=128)
                    for pos, n in tiles_of(n_tok):
                        load_h(s, pos, n)
                        if 'a' in mixers:
                            P.op('pool', lambda e, pos=pos, n=n, yaV=yaV: e.dma_start(out=ya[:, :, :n], in_=yaV[:, :, pos:pos + n]),
                                 reads=[('yaT', s)], writes=['ya'], dma=True)
                        else:
                            P.op('dve', lambda e: e.memset(ya[:], 0.0), writes=['ya'])
                        if 'b' in mixers:
                            P.op('pool', lambda e, pos=pos, n=n, ybV=ybV: e.dma_start(out=yb[:, :, :n], in_=ybV[:, :, pos:pos + n]),
                                 reads=[('ybT', s)], writes=['yb'], dma=True)
                        else:
                            P.op('dve', lambda e: e.memset(yb[:], 0.0), writes=['yb'])
                        if 'c' in mixers:
                            P.op('pool', lambda e, pos=pos, n=n, ycV=ycV: e.dma_start(out=yc[:, :, :n], in_=ycV[:, :, pos:pos + n]),
                                 reads=[('ycT', s)], writes=['yc'], dma=True)
                        else:
                            P.op('dve', lambda e: e.memset(yc[:], 1.0), writes=['yc'])
                        rmsnorm_to_xn(n)
                        P.op('act', lambda e, n=n: e.activation(out=ycs[:, :, :n], in_=yc[:, :, :n], func=AF.Square),
                             reads=['yc'], writes=['ycs'])
                        for t in range(2):
                            ps, pk = next_ps()
                            P.op('pe', lambda e, t=t, n=n, ps=ps: e.matmul(ps[:, :n], lhsT=blk[:], rhs=ycs[:, t, :n],
                                                                        start=True, stop=True),
                                 reads=['ycs', 'blk'], writes=[pk])
                            rsqrt_ps(rc, 'rc', ps, pk, n)
                            P.op('dve', lambda e, t=t, n=n: e.scalar_tensor_tensor(
                                out=yc[:, t, :n], in0=yc[:, t, :n], scalar=onorms[:, l:l + 1], in1=rc[:, :n],
                                op0=ALU.mult, op1=ALU.mult), reads=['yc', 'rc', 'onorms'], writes=['yc'])
                            ps2, pk2 = next_ps()
                            for k in range(8):
                                P.op('pe', lambda e, k=k, t=t, n=n, ps2=ps2: e.matmul(
                                    ps2[:, :n], lhsT=WC[:, k, t * 128:(t + 1) * 128], rhs=xn[:, k, :n],
                                    start=(k == 0), stop=(k == 7)), reads=['WC', 'xn'], writes=[pk2])
                            P.op('act', lambda e, n=n, ps2=ps2: e.activation(out=m1[:, :n], in_=ps2[:, :n], func=AF.Silu),
                                 reads=[pk2], writes=['m1'])
                            P.op('dve', lambda e, t=t, n=n: e.tensor_tensor(out=ycn[:, t, :n], in0=yc[:, t, :n],
                                                                           in1=m1[:, :n], op=ALU.mult),
                                 reads=['yc', 'm1'], writes=['ycn'])
                        for oc in range(8):
                            pss = []
                            for (W, src_, nk, key) in [(WUA, ya, 2, 'ya'), (WUB, yb, 4, 'yb'), (WUC, ycn, 2, 'ycn')]:
                                ps, pk = next_ps()
                                for k in range(nk):
                                    P.op('pe', lambda e, k=k, n=n, ps=ps, W=W, src_=src_, nk=nk: e.matmul(
                                        ps[:, :n], lhsT=W[:, k, oc * 128:(oc + 1) * 128], rhs=src_[:, k, :n],
                                        start=(k == 0), stop=(k == nk - 1)), reads=[key, 'WUA', 'WUB', 'WUC'], writes=[pk])
                                pss.append((ps, pk))
                            for gi in range(3):
                                ps, pk = next_ps()
                                c0 = 256 + gi * 1024 + oc * 128
                                for k in range(8):
                                    P.op('pe', lambda e, k=k, n=n, ps=ps, c0=c0: e.matmul(
                                        ps[:, :n], lhsT=WC[:, k, c0:c0 + 128], rhs=xn[:, k, :n],
                                        start=(k == 0), stop=(k == 7)), reads=['WC', 'xn'], writes=[pk])
                                P.op('act', lambda e, gi=gi, n=n, ps=ps: e.activation(out=sg[:, gi, :n], in_=ps[:, :n],
                                                                                  func=AF.Sigmoid),
                                     reads=[pk], writes=[('sg', gi)])
                            P.op('dve', lambda e, n=n, p0=pss[0][0]: e.tensor_tensor(out=m1[:, :n], in0=p0[:, :n], in1=sg[:, 0, :n], op=ALU.mult),
                                 reads=[pss[0][1], ('sg', 0)], writes=['m1'])
                            P.op('dve', lambda e, n=n, p1=pss[1][0]: e.tensor_tensor(out=m2[:, :n], in0=p1[:, :n], in1=sg[:, 1, :n], op=ALU.mult),
                                 reads=[pss[1][1], ('sg', 1)], writes=['m2'])
                            P.op('dve', lambda e, n=n: e.tensor_tensor(out=m1[:, :n], in0=m1[:, :n], in1=m2[:, :n], op=ALU.add),
                                 reads=['m1', 'm2'], writes=['m1'])
                            P.op('dve', lambda e, n=n, p2=pss[2][0]: e.tensor_tensor(out=m2[:, :n], in0=p2[:, :n], in1=sg[:, 2, :n], op=ALU.mult),
                                 reads=[pss[2][1], ('sg', 2)], writes=['m2'])
                            P.op('dve', lambda e, n=n, oc=oc: e.tensor_tensor(out=mix[:, oc, :n], in0=m1[:, :n], in1=m2[:, :n], op=ALU.add),
                                 reads=['m1', 'm2'], writes=['mix'])
                        for oc in range(8):
                            ps, pk = next_ps()
                            for k in range(8):
                                P.op('pe', lambda e, k=k, n=n, ps=ps, oc=oc: e.matmul(
                                    ps[:, :n], lhsT=WO[:, k, oc * 128:(oc + 1) * 128], rhs=mix[:, k, :n],
                                    start=(k == 0), stop=(k == 7)), reads=['WO', 'mix'], writes=[pk])
                            P.op('dve', lambda e, n=n, ps=ps, oc=oc: e.tensor_tensor(
                                out=hbuf[:, oc, :n], in0=hbuf[:, oc, :n], in1=ps[:, :n], op=ALU.add),
                                reads=[pk, 'hbuf'], writes=['hbuf'])
                        store_h(s, pos, n)
                P.barrier()
            P.barrier()
            with ExitStack() as ph:
                def pb_(name, shape, dt=F32):
                    return ph.enter_context(nc.sbuf_tensor(f"{name}_{l}", list(shape), dt))
                WG = pb_("WG", [128, 8, FF], BF16)
                WU = pb_("WU", [128, 8, FF], BF16)
                WD = pb_("WD", [128, 22, D], BF16)
                act = pb_("act", [128, 22, TT], BF16)
                sgt = pb_("sgt", [128, TT])
                for k in range(8):
                    load_w(WG[:, k, :], 'WG', w_fg[l, k * 128:(k + 1) * 128, :], FF, scale=g2s[:, l, k:k + 1])
                    load_w(WU[:, k, :], 'WU', w_fu[l, k * 128:(k + 1) * 128, :], FF, scale=g2s[:, l, k:k + 1])
                for k in range(22):
                    load_w(WD[:, k, :], 'WD', w_fd[l, k * 128:(k + 1) * 128, :], D)
                for s, n_tok in seqs:
                    for pos, n in tiles_of(n_tok):
                        load_h(s, pos, n)
                        rmsnorm_to_xn(n)
                        for fc in range(22):
                            psg, pkg = next_ps()
                            psu, pku = next_ps()
                            for (W, ps, pk, key) in [(WG, psg, pkg, 'WG'), (WU, psu, pku, 'WU')]:
                                for k in range(8):
                                    P.op('pe', lambda e, k=k, n=n, ps=ps, W=W, fc=fc: e.matmul(
                                        ps[:, :n], lhsT=W[:, k, fc * 128:(fc + 1) * 128], rhs=xn[:, k, :n],
                                        start=(k == 0), stop=(k == 7)), reads=[key, 'xn'], writes=[pk])
                            P.op('act', lambda e, n=n, psg=psg: e.activation(out=sgt[:, :n], in_=psg[:, :n], func=AF.Silu),
                                 reads=[pkg], writes=['sgt'])
                            P.op('dve', lambda e, n=n, psu=psu, fc=fc: e.tensor_tensor(
                                out=act[:, fc, :n], in0=psu[:, :n], in1=sgt[:, :n], op=ALU.mult),
                                reads=[pku, 'sgt'], writes=['act'])
                        for oc in range(8):
                            ps, pk = next_ps()
                            for k in range(22):
                                P.op('pe', lambda e, k=k, n=n, ps=ps, oc=oc: e.matmul(
                                    ps[:, :n], lhsT=WD[:, k, oc * 128:(oc + 1) * 128], rhs=act[:, k, :n],
                                    start=(k == 0), stop=(k == 21)), reads=['WD', 'act'], writes=[pk])
                            P.op('dve', lambda e, n=n, ps=ps, oc=oc: e.tensor_tensor(
                                out=hbuf[:, oc, :n], in0=hbuf[:, oc, :n], in1=ps[:, :n], op=ALU.add),
                                reads=[pk, 'hbuf'], writes=['hbuf'])
                        store_h(s, pos, n)
                P.barrier()
            P.barrier()

        with ExitStack() as ph:
            xo = ph.enter_context(nc.sbuf_tensor("xo", [128, 8, TT], F32))
            ob = ph.enter_context(nc.sbuf_tensor("ob", [128, D], F32))
            for s, n_tok in seqs:
                for pos, n in tiles_of(n_tok)[1:]:
                    load_h(s, pos, n)
                    P.op('act', lambda e, n=n: e.activation(out=sqb[:, :, :n], in_=hbuf[:, :, :n], func=AF.Square),
                         reads=['hbuf'], writes=['sqb'])
                    ps, pk = next_ps()
                    for k in range(8):
                        P.op('pe', lambda e, k=k, n=n, ps=ps: e.matmul(ps[:, :n], lhsT=ones[:], rhs=sqb[:, k, :n],
                                                                    start=(k == 0), stop=(k == 7)),
                             reads=['sqb', 'ones'], writes=[pk])
                    rsqrt_ps(rstd, 'rstd', ps, pk, n)
                    for k in range(8):
                        P.op('dve', lambda e, k=k, n=n: e.scalar_tensor_tensor(
                            out=xo[:, k, :n], in0=hbuf[:, k, :n], scalar=gfs[:, k:k + 1], in1=rstd[:, :n],
                            op0=ALU.mult, op1=ALU.mult), reads=['hbuf', 'rstd', 'gfs'], writes=['xo'])
                    for tb in range(n // 128):
                        pb, pks = next_ps_big()
                        for k in range(8):
                            P.op('pe', lambda e, k=k, tb=tb, pb=pb: e.transpose(
                                out=pb[:, k * 128:(k + 1) * 128], in_=xo[:, k, tb * 128:(tb + 1) * 128],
                                identity=ident[:]), reads=['xo', 'ident'], writes=pks)
                        P.op('act', lambda e, pb=pb: e.activation(out=ob[:], in_=pb[:], func=AF.Copy),
                             reads=pks, writes=['ob'])
                        r0 = pos - NM + tb * 128
                        P.op('sp', lambda e, r0=r0, s=s: e.dma_start(out=y_out[s][r0:r0 + 128, :], in_=ob[:]),
                             reads=['ob'], writes=[('y', s, r0)], dma=True)
        P.barrier()
        P.emit(sems)
    return nc


def host_inputs(inputs, depth, core):
    f = lambda a: np.ascontiguousarray(np.asarray(a, dtype=np.float32))
    d = {}
    d["x_p"] = f(inputs["x_prompt"][core])
    d["x_s"] = f(inputs["x_sample"][core // 4])
    d["meta"] = f(inputs["meta_tokens"])
    for k in ["w_in", "w_up_a", "w_up_b", "w_up_c", "w_o", "w_ffn_gate", "w_ffn_up", "w_ffn_down"]:
        d[k] = f(inputs[k][:depth])
    d["g1"] = f(np.asarray(inputs["norm1_g"])[:depth].reshape(depth, 8, 128).transpose(2, 0, 1))
    d["g2"] = f(np.asarray(inputs["norm2_g"])[:depth].reshape(depth, 8, 128).transpose(2, 0, 1))
    d["gf"] = f(np.asarray(inputs["final_norm_g"]).reshape(8, 128).T)
    d["onorm"] = f(np.tile(np.asarray(inputs["hg_onorm_g"])[:depth], (1, 2)).T)
    d["c_ident"] = np.eye(128, dtype=np.float32)
    d["c_ones"] = np.full((128, 128), 1.0 / 1024, np.float32)
    b = np.zeros((128, 128), np.float32)
    b[:64, :64] = 1.0 / 64
    b[64:, 64:] = 1.0 / 64
    d["c_blk"] = b
    jj, ii = np.meshgrid(np.arange(128), np.arange(128), indexing='ij')
    same = (jj // 32) == (ii // 32)
    d["c_maskf"] = (same & (jj <= ii)).astype(np.float32)
    d["c_maskb"] = (same & (jj >= ii)).astype(np.float32)
    d["c_cmask"] = (np.arange(128)[:, None] // 32 == np.arange(4)[None, :]).astype(np.float32)
    qc = np.arange(64); kc = np.arange(64)
    ws = np.clip(qc - 8, 0, 48)
    cm = (kc[:, None] >= ws[None, :]) & (kc[:, None] < ws[None, :] + 16)
    d["c_negm"] = np.tile(np.where(cm, 0.0, NEGV).astype(np.float32), (2, 2))
    rp = np.zeros((depth, 8, 16, 127), np.float32)
    rp[:, :, :15, 48:79] = np.asarray(inputs["na_rpb"])[:depth]
    rp[:, :, 15, :] = NEGV
    d["rpbpad"] = rp
    d["c_iota"] = np.tile(np.arange(1, TS + 1, dtype=np.float32)[None, :], (128, 1))
    ar = np.asarray(inputs["s5_a_re"])[:depth]; ai = np.asarray(inputs["s5_a_im"])[:depth]
    ld = np.repeat(np.asarray(inputs["s5_log_dt"])[:depth][..., None], 64, axis=-1)
    flat = np.stack([ar, ai, ld], axis=2).reshape(depth, 2, 3, 1024)
    d["s5sp"] = f(flat.reshape(depth, 2, 3, 8, 128).transpose(4, 0, 1, 2, 3))
    d["s5rep"] = f(np.broadcast_to(flat[None], (128, depth, 2, 3, 1024)))
    Bb = np.zeros((depth, 2, 2, 128, 8, 128), np.float32)
    Cb = np.zeros((depth, 2, 2, 128, 8, 128), np.float32)
    for ri, (bsrc, csrc) in enumerate([(inputs["s5_b_re"], inputs["s5_c_re"]), (inputs["s5_b_im"], inputs["s5_c_im"])]):
        bsrc = np.asarray(bsrc)[:depth]; csrc = np.asarray(csrc)[:depth]
        for g in range(16):
            j = g // 2
            r0 = 32 * (j % 4) + 16 * (g % 2)
            s0 = 64 * (g % 2)
            c0 = 16 * (g % 8)
            Bb[:, :, ri, r0:r0 + 16, j, s0:s0 + 64] = bsrc[:, :, g].transpose(0, 1, 3, 2)
            Cb[:, :, ri, s0:s0 + 64, j, c0:c0 + 16] = csrc[:, :, g].transpose(0, 1, 3, 2)
    d["s5B"] = Bb
    d["s5C"] = Cb
    d["s5d"] = f(np.asarray(inputs["s5_d"])[:depth].reshape(depth, 2, 128).transpose(2, 0, 1))
    d["w_glu"] = f(inputs["s5_w_glu"][:depth])
    d["lbl"] = f(np.asarray(inputs["hg_lb_logits"]).T.reshape(4, 64, 4).transpose(1, 0, 2))
    return d


def run(inputs, nP, nS, depth, n_cores=8, **kw):
    nc = build(nP, nS, depth, **kw)
    in_maps = [host_inputs(inputs, depth, c) for c in range(n_cores)]
    res = run_bass_kernel_spmd(nc, in_maps, core_ids=list(range(n_cores)))
    yp = np.stack([res.results[c]["y_p"] for c in range(n_cores)], 0)
    ys = np.stack([res.results[c]["y_s"] for c in range(0, n_cores, 4)], 0)
    return yp.astype(np.float32), ys.astype(np.float32)


def kernel(**inputs):
    return run(inputs, 4096, 16384, 4)
```
